# Optimizing a Trainium2 kernel written in Bass

```python
import math
import jax, jax.numpy as jnp
from jax import lax
import numpy as np

D_MODEL = 1024
BATCH = 8
SEQ = 4096
DEPTH = 2

GRID_W = 64
CONV_W = 256
RWKV_HEADS = 4
RWKV_HEAD_DIM = 64
RWKV_W = RWKV_HEADS * RWKV_HEAD_DIM
DECAY_RANK = 16
ICLR_RANK = 16
GATE_RANK = 32
DECAY_SCALE = math.exp(-0.5)
GN_EPS = 64e-5
NA_HEADS = 4
NA_HEAD_DIM = 64
NA_W = NA_HEADS * NA_HEAD_DIM
NA_KH = 8
NA_KW = 16
SGU_HEADS = 4
SGU_HEAD_DIM = 64
SGU_W = SGU_HEADS * SGU_HEAD_DIM
SGU_CHUNK = 128
N_GROUPS = 4
GROUP_W = 256
D_MIX = CONV_W + RWKV_W + NA_W + SGU_W
D_FF = 2816
NORM_EPS = 1e-6
IN_SPLITS = (CONV_W, CONV_W, CONV_W,
             RWKV_W, RWKV_W, RWKV_W,
             DECAY_RANK, DECAY_RANK, ICLR_RANK, ICLR_RANK,
             GATE_RANK,
             NA_W, NA_W, NA_W,
             SGU_W, SGU_W)
D_IN_PROJ = sum(IN_SPLITS)

kernel_name = "hybrid_parallel_heads_encoder"


def rms_norm(x, g):
    x32 = x.astype(jnp.float32)
    y = x32 * lax.rsqrt(jnp.mean(x32 * x32, axis=-1, keepdims=True) + NORM_EPS)
    return (y * g).astype(x.dtype)


def dwconv3(x, w):
    xp = jnp.pad(x, ((0, 0), (1, 1), (0, 0)))
    return xp[:, :-2] * w[0] + xp[:, 1:-1] * w[1] + xp[:, 2:] * w[2]


def wkv7_scan(r, w, k, v, kk, a, reverse):
    B, T, H, dh = r.shape

    def step(S, inp):
        r_t, w_t, k_t, v_t, kk_t, a_t = inp
        sa = jnp.einsum('bhvk,bhk->bhv', S, -kk_t)
        S = (S * w_t[:, :, None, :]
             + sa[..., None] * (kk_t * a_t)[:, :, None, :]
             + v_t[..., None] * k_t[:, :, None, :])
        return S, jnp.einsum('bhvk,bhk->bhv', S, r_t)

    xs = (jnp.swapaxes(r, 0, 1), jnp.swapaxes(w, 0, 1), jnp.swapaxes(k, 0, 1),
          jnp.swapaxes(v, 0, 1), jnp.swapaxes(kk, 0, 1), jnp.swapaxes(a, 0, 1))
    S0 = jnp.zeros((B, H, dh, dh), jnp.float32)
    _, ys = lax.scan(step, S0, xs, reverse=reverse)
    return jnp.swapaxes(ys, 0, 1)


def rwkv7_mix(r, k, v, dec_f, dec_b, iclr_f, iclr_b, g_dn,
              w0, w_up, a0, a_up, g_up, k_k, k_a, r_k, lnx_w, lnx_b):
    dt = r.dtype
    B, T, _ = r.shape
    f = lambda t: t.astype(jnp.float32)
    heads = lambda t: t.reshape(B, T, RWKV_HEADS, RWKV_HEAD_DIM)
    r, k, v = f(r), f(k), f(v)
    rh, kh, vh = heads(r), heads(k), heads(v)
    kk = heads(k * f(k_k))
    kk = kk * lax.rsqrt(jnp.sum(kk * kk, axis=-1, keepdims=True) + 1e-12)
    g = jax.nn.sigmoid(f(g_dn)) @ f(g_up)
    y = jnp.zeros_like(rh)
    for d, (dn_w, dn_a) in enumerate(((dec_f, iclr_f), (dec_b, iclr_b))):
        log_w = -DECAY_SCALE * jax.nn.sigmoid(f(w0[d]) + jnp.tanh(f(dn_w)) @ f(w_up[d]))
        a = jax.nn.sigmoid(f(a0[d]) + f(dn_a) @ f(a_up[d]))
        k_eff = k * (1.0 + (a - 1.0) * f(k_a))
        y = y + wkv7_scan(rh, heads(jnp.exp(log_w)), heads(k_eff), vh, kk, heads(a),
                          reverse=(d == 1))
    mu = jnp.mean(y, axis=-1, keepdims=True)
    var = jnp.mean(jnp.square(y - mu), axis=-1, keepdims=True)
    y = ((y - mu) * lax.rsqrt(var + GN_EPS)).reshape(B, T, RWKV_W) * f(lnx_w) + f(lnx_b)
    bonus = (jnp.sum(rh * kh * f(r_k), axis=-1, keepdims=True) * vh).reshape(B, T, RWKV_W)
    return ((y + bonus) * g).astype(dt)


def neighborhood_attention(q, k, v, rpb):
    B, T, _ = q.shape
    rows = T // GRID_W
    kh = min(NA_KH, rows)
    grid = lambda t: t.reshape(B, rows, GRID_W, NA_HEADS, NA_HEAD_DIM)
    qg = grid(q) * (NA_HEAD_DIM ** -0.5)
    kg, vg = grid(k), grid(v)
    r = jnp.arange(rows)
    row_start = jnp.clip(r - kh // 2, 0, rows - kh)
    row_idx = row_start[:, None] + jnp.arange(kh)[None, :]
    k_band = kg[:, row_idx]
    v_band = vg[:, row_idx]
    s = jnp.einsum('brqhd,brkwhd->bhrqkw', qg, k_band).astype(jnp.float32)
    c = jnp.arange(GRID_W)
    col_start = jnp.clip(c - NA_KW // 2, 0, GRID_W - NA_KW)
    col_mask = (c[None, :] >= col_start[:, None]) & (c[None, :] < col_start[:, None] + NA_KW)
    dy = row_idx - r[:, None] + (NA_KH - 1)
    dx = jnp.clip(c[None, :] - c[:, None], -(NA_KW - 1), NA_KW - 1) + (NA_KW - 1)
    bias = rpb[:, dy[:, None, :, None], dx[None, :, None, :]]
    s = s + bias[None].astype(jnp.float32)
    s = jnp.where(col_mask[None, None, None, :, None, :], s, -1e30)
    p = jax.nn.softmax(s, axis=(-2, -1))
    o = jnp.einsum('bhrqkw,brkwhd->brqhd', p.astype(v.dtype), v_band)
    return o.reshape(B, T, NA_W)


def layer_norm(x, g):
    x32 = x.astype(jnp.float32)
    mu = jnp.mean(x32, axis=-1, keepdims=True)
    var = jnp.mean(jnp.square(x32 - mu), axis=-1, keepdims=True)
    return ((x32 - mu) * lax.rsqrt(var + NORM_EPS) * g).astype(x.dtype)


def spatial_gating(u, v, norm_w, w_s, b_s):
    B, T, _ = u.shape
    u = jax.nn.gelu(u)
    v = layer_norm(jax.nn.gelu(v), norm_w)
    vc = v.reshape(B, T // SGU_CHUNK, SGU_CHUNK, SGU_HEADS, SGU_HEAD_DIM)
    mixed = jnp.einsum('hpq,bnqhd->bnphd', w_s, vc) + jnp.swapaxes(b_s, 0, 1)[:, :, None]
    return u * mixed.reshape(B, T, SGU_W)


def setup_inputs(seed: int = 0) -> dict:
    key = jax.random.key(seed)
    ks = jax.random.split(key, 26)
    L = DEPTH
    nrm = lambda k, shape, scale: jax.random.normal(k, shape, jnp.float32) * scale
    gain = lambda k, shape, base=1.0: base + 0.05 * jax.random.normal(k, shape, jnp.float32)
    return {
        "x": nrm(ks[0], (BATCH, SEQ, D_MODEL), 1.0),
        "norm_mix_pre": gain(ks[1], (L, D_MODEL)),
        "norm_mix_post": gain(ks[2], (L, D_MODEL)),
        "norm_ffn_pre": gain(ks[3], (L, D_MODEL)),
        "norm_ffn_post": gain(ks[4], (L, D_MODEL)),
        "w_in": nrm(ks[5], (L, D_MODEL, D_IN_PROJ), D_MODEL ** -0.5),
        "conv_a_w": nrm(ks[6], (L, 3, CONV_W), 3 ** -0.5),
        "rwkv_w0": nrm(ks[7], (L, 2, RWKV_W), 0.5),
        "rwkv_w_up": nrm(ks[8], (L, 2, DECAY_RANK, RWKV_W), DECAY_RANK ** -0.5),
        "rwkv_a0": nrm(ks[9], (L, 2, RWKV_W), 0.5),
        "rwkv_a_up": nrm(ks[10], (L, 2, ICLR_RANK, RWKV_W), ICLR_RANK ** -0.5),
        "rwkv_g_up": nrm(ks[11], (L, GATE_RANK, RWKV_W), GATE_RANK ** -0.5),
        "rwkv_k_k": gain(ks[12], (L, RWKV_W), 0.85),
        "rwkv_k_a": gain(ks[13], (L, RWKV_W)),
        "rwkv_r_k": nrm(ks[14], (L, RWKV_HEADS, RWKV_HEAD_DIM), 0.1),
        "rwkv_lnx_w": gain(ks[15], (L, RWKV_W)),
        "rwkv_lnx_b": nrm(ks[16], (L, RWKV_W), 0.02),
        "na_rpb": nrm(ks[17], (L, NA_HEADS, 2 * NA_KH - 1, 2 * NA_KW - 1), 0.1),
        "sgu_norm": gain(ks[18], (L, SGU_W)),
        "sgu_w": nrm(ks[19], (L, SGU_HEADS, SGU_CHUNK, SGU_CHUNK), SGU_CHUNK ** -0.5),
        "sgu_b": gain(ks[20], (L, SGU_HEADS, SGU_CHUNK)),
        "merge_gain": gain(ks[21], (L, D_MIX)),
        "w_out": nrm(ks[22], (L, D_MIX, D_MODEL), D_MIX ** -0.5),
        "ffn_w_up": nrm(ks[23], (L, D_MODEL, 2 * D_FF), D_MODEL ** -0.5),
        "ffn_conv": nrm(ks[24], (L, 3, D_FF), 3 ** -0.5),
        "ffn_w_down": nrm(ks[25], (L, D_FF, D_MODEL), D_FF ** -0.5),
    }


def reference(x, norm_mix_pre, norm_mix_post, norm_ffn_pre, norm_ffn_post, w_in, conv_a_w,
              rwkv_w0, rwkv_w_up, rwkv_a0, rwkv_a_up, rwkv_g_up, rwkv_k_k, rwkv_k_a, rwkv_r_k,
              rwkv_lnx_w, rwkv_lnx_b, na_rpb, sgu_norm, sgu_w, sgu_b, merge_gain, w_out,
              ffn_w_up, ffn_conv, ffn_w_down):
    B, T, _ = x.shape
    split_points = np.cumsum(IN_SPLITS)[:-1].tolist()
    for l in range(DEPTH):
        h = rms_norm(x, norm_mix_pre[l])
        proj = h @ w_in[l]
        (c_h, c_b, c_c, r, k, v, dec_f, dec_b, iclr_f, iclr_b, g_dn,
         q_na, k_na, v_na, u_sg, v_sg) = jnp.split(proj, split_points, axis=-1)
        y_conv = c_b * dwconv3(c_c * c_h, conv_a_w[l])
        y_rwkv = rwkv7_mix(r, k, v, dec_f, dec_b, iclr_f, iclr_b, g_dn,
                           rwkv_w0[l], rwkv_w_up[l], rwkv_a0[l], rwkv_a_up[l], rwkv_g_up[l],
                           rwkv_k_k[l], rwkv_k_a[l], rwkv_r_k[l], rwkv_lnx_w[l], rwkv_lnx_b[l])
        y_na = neighborhood_attention(q_na, k_na, v_na, na_rpb[l])
        y_sgu = spatial_gating(u_sg, v_sg, sgu_norm[l], sgu_w[l], sgu_b[l])
        groups = jnp.stack([y_conv, y_rwkv, y_na, y_sgu], axis=2)
        merged = rms_norm(groups, 1.0).reshape(B, T, D_MIX) * merge_gain[l]
        x = x + rms_norm(merged @ w_out[l], norm_mix_post[l])
        h = rms_norm(x, norm_ffn_pre[l])
        gate, lin = jnp.split(h @ ffn_w_up[l], 2, axis=-1)
        hid = jax.nn.gelu(dwconv3(gate, ffn_conv[l])) * lin
        x = x + rms_norm(hid @ ffn_w_down[l], norm_ffn_post[l])
    return x
```

```python
import contextlib
import numpy as np
import concourse.bass as bass
import concourse.mybir as mybir

F32 = mybir.dt.float32
BF16 = mybir.dt.bfloat16
AF = mybir.ActivationFunctionType
ALU = mybir.AluOpType
EPOCH = 30000
NDMA_SEM = 10
ENGS = ("pe", "act", "dve", "pool", "sp")


class Op:
    __slots__ = ("eng", "fn", "reads", "writes", "is_dma", "idx", "waits", "signal", "clock",
                 "sem", "count", "dsem", "dcount")

    def __init__(self, eng, fn, reads, writes, is_dma):
        self.eng = eng; self.fn = fn; self.reads = reads; self.writes = writes
        self.is_dma = is_dma; self.waits = []; self.signal = False; self.clock = None
        self.sem = None; self.count = None; self.dsem = None; self.dcount = None


class Prog:
    def __init__(self, nc, stack):
        self.nc = nc
        self.stack = stack
        self.sems = {}
        self.ops = []
        self.last_writer = {}
        self.readers = {}
        self.known = {e: {} for e in ENGS}
        self.n_on = {e: 0 for e in ENGS}
        self.sig_cnt = {e: 0 for e in ENGS}
        self.dma_rr = {e: 0 for e in ENGS}
        self.dma_last = {}
        self.dma_cnt = {}
        self.total_ops = 0
        for e in ENGS:
            for ep in range(3):
                self._sem((e, ep))
        for q in ("sp", "pool", "act"):
            for slot in range(NDMA_SEM):
                self._sem(("d", q, slot))
        for i in range(8):
            self._sem(("d", "pool", "bg%d" % i))
        self.persist = {}
        self.bg_cnt = {}
        self.clear_all()
        nc.all_engine_barrier()

    def add_bg(self, group, fn, reads=()):
        op = Op("pool", fn, tuple(reads), (), True)
        op.idx = self.n_on["pool"]; self.n_on["pool"] += 1
        op.dsem = group
        op.dcount = self.bg_cnt.get(group, 0)
        self.bg_cnt[group] = op.dcount + 1
        op.clock = {("d", "pool", group): op.dcount}
        self.persist[("bg", group)] = op
        self.ops.append(op)
        return op

    def clear_all(self):
        for h in self.sems.values():
            self.nc.gpsimd.sem_clear(h)

    def finish(self):
        self.flush()
        nc = self.nc
        for grp, cnt in self.bg_cnt.items():
            for eng in (nc.tensor, nc.scalar, nc.vector, nc.gpsimd, nc.sync):
                eng.wait_ge(self.sems[("d", "pool", grp)], 16 * cnt)
        self.nc.all_engine_barrier()
        self.clear_all()
        self.nc.all_engine_barrier()

    def _sem(self, key):
        if key not in self.sems:
            assert not getattr(self, "_frozen", False), key
            self.sems[key] = self.stack.enter_context(
                self.nc.semaphore("s_" + "_".join(str(x) for x in key)))
        return self.sems[key]

    def _need(self, op, dep, same_ok):
        if dep is None:
            return
        kn = self.known[op.eng]
        if dep.is_dma:
            key = ("d", dep.eng, dep.dsem)
            if kn.get(key, -1) >= dep.dcount:
                return
        else:
            if dep.eng == op.eng and not same_ok:
                return
            key = dep.eng
            if kn.get(key, -1) >= dep.idx:
                return
        dep.signal = True
        op.waits.append(dep)
        for k, v in dep.clock.items():
            if kn.get(k, -1) < v:
                kn[k] = v

    def add(self, eng, fn, reads=(), writes=(), dma=False):
        op = Op(eng, fn, tuple(reads), tuple(writes), dma)
        op.idx = self.n_on[eng]; self.n_on[eng] += 1
        kn = self.known[eng]
        if dma:
            slot = self.dma_rr[eng] % NDMA_SEM; self.dma_rr[eng] += 1
            op.dsem = slot
            prev = self.dma_last.get((eng, slot))
            op.dcount = self.dma_cnt.get((eng, slot), 0)
            self.dma_cnt[(eng, slot)] = op.dcount + 1
            if prev is not None:
                self._need(op, prev, True)
            self.dma_last[(eng, slot)] = op
        for r in op.reads:
            w = self.last_writer.get(r)
            if w is None:
                w = self.persist.get(r)
            if w is not None:
                self._need(op, w, True)
            if isinstance(r, str) and r.startswith("ps"):
                for rd in self.readers.get(r, ()):
                    if rd.eng != eng:
                        self._need(op, rd, True)
        strict = (eng != "pe")
        for wkey in op.writes:
            w = self.last_writer.get(wkey)
            if w is not None:
                self._need(op, w, w.is_dma or dma or strict)
            for rd in self.readers.get(wkey, ()):
                self._need(op, rd, rd.is_dma or dma or strict)
        for r in op.reads:
            self.readers.setdefault(r, []).append(op)
        for wkey in op.writes:
            self.last_writer[wkey] = op
            self.readers[wkey] = []
        ck = dict(kn)
        if dma:
            ck[("d", eng, op.dsem)] = op.dcount
        else:
            ck[eng] = op.idx
        op.clock = ck
        self.ops.append(op)
        return op

    def flush(self):
        nc = self.nc
        if not self.ops:
            return
        self.total_ops += len(self.ops)
        per = {e: [o for o in self.ops if o.eng == e] for e in ENGS}
        lastc = {}
        for e in ENGS:
            comp = [o for o in per[e] if not o.is_dma]
            if comp:
                comp[-1].signal = True
                lastc[e] = comp[-1]
        for e in ENGS:
            c = self.sig_cnt[e]
            for o in per[e]:
                if o.is_dma:
                    continue
                if o.signal:
                    c += 1
                    o.sem = (e, (c - 1) // EPOCH); o.count = (c - 1) % EPOCH + 1
            self.sig_cnt[e] = c
        dma_final = dict(self.dma_last)
        import os
        if os.environ.get("FW_DEBUG"):
            print("FLUSH sig_cnt", self.sig_cnt, "max dma cnt", max([16 * (v.dcount + 1) for v in dma_final.values()] or [0]), "nops", len(self.ops))
        active = list(ENGS)

        def run(e, eng):
            for o in per[e]:
                for d in o.waits:
                    if d.is_dma:
                        eng.wait_ge(self._sem(("d", d.eng, d.dsem)), 16 * (d.dcount + 1))
                    else:
                        eng.wait_ge(self._sem(d.sem), d.count)
                inst = o.fn(eng)
                if o.is_dma:
                    inst.then_inc(self._sem(("d", o.eng, o.dsem)), 16)
                elif o.signal:
                    inst.then_inc(self._sem(o.sem), 1)
            for (q, slot), last in dma_final.items():
                eng.wait_ge(self._sem(("d", q, slot)), 16 * (last.dcount + 1))
            for e2, lo in lastc.items():
                if e2 != e:
                    eng.wait_ge(self._sem(lo.sem), lo.count)

        with nc.Block() as block:
            dec = {"pe": block.tensor, "act": block.scalar, "dve": block.vector,
                   "pool": block.gpsimd, "sp": block.sync}
            for e in active:
                def body(eng, e=e):
                    run(e, eng)
                dec[e](body)
        self.ops = []
        self.last_writer = {}
        self.readers = {}
        for e in ENGS:
            kn = self.known[e]
            for e2 in ENGS:
                kn[e2] = self.n_on[e2] - 1
            for (q, slot), last in dma_final.items():
                kn[("d", q, slot)] = last.dcount


import contextlib
import numpy as np

D = 1024
DIN = 2912
DFF = 2816
EPS = 1e-6

PARAMS = [("norm_mix_pre", (2, 1024)), ("norm_mix_post", (2, 1024)), ("norm_ffn_pre", (2, 1024)),
          ("norm_ffn_post", (2, 1024)), ("w_in", (2, 1024, 2912)), ("conv_a_w", (2, 3, 256)),
          ("rwkv_w0", (2, 2, 256)), ("rwkv_w_up", (2, 2, 16, 256)), ("rwkv_a0", (2, 2, 256)),
          ("rwkv_a_up", (2, 2, 16, 256)), ("rwkv_g_up", (2, 32, 256)), ("rwkv_k_k", (2, 256)),
          ("rwkv_k_a", (2, 256)), ("rwkv_r_k", (2, 4, 64)), ("rwkv_lnx_w", (2, 256)),
          ("rwkv_lnx_b", (2, 256)), ("na_rpb", (2, 4, 15, 31)), ("sgu_norm", (2, 256)),
          ("sgu_w", (2, 4, 128, 128)), ("sgu_b", (2, 4, 128)), ("merge_gain", (2, 1024)),
          ("w_out", (2, 1024, 1024)), ("ffn_w_up", (2, 1024, 5632)), ("ffn_conv", (2, 3, 2816)),
          ("ffn_w_down", (2, 2816, 1024))]


class Ctx:
    pass


def mk_sb(g, st):
    def sb(name, shape, dt):
        g.uid = getattr(g, "uid", 0) + 1
        return st.enter_context(g.nc.sbuf_tensor("%s_%d" % (name, g.uid), shape, dt))
    return sb


def build(T, nlayers=2, dbg=(), phases=('conv','sgu','na','rwkv','p3')):
    nc = bass.Bass("TRN2", target_bir_lowering=False)
    g = Ctx()
    g.nc = nc; g.T = T; g.dbgnames = dbg; g.phases = phases
    g.x = nc.dram_tensor("x", [T, D], F32, kind="ExternalInput").ap()
    g.w = {n: nc.dram_tensor(n, list(s), F32, kind="ExternalInput").ap() for n, s in PARAMS}
    g.y = nc.dram_tensor("y", [T, D], F32, kind="ExternalOutput").ap()
    def scratch(name, shape, dt):
        kind = "ExternalOutput" if name in dbg else "Internal"
        return nc.dram_tensor(name, shape, dt, kind=kind).ap()
    g.sc = dict(
        pT=scratch("pT", [256, T], F32), cbT=scratch("cbT", [256, T], F32),
        rT=scratch("rT", [256, T], F32), kT=scratch("kT", [256, T], F32), vT=scratch("vT", [256, T], F32),
        loraT=scratch("loraT", [96, T], F32), vtok=scratch("vtok", [T, 256], BF16),
        qnT=scratch("qnT", [256, T], BF16), knT=scratch("knT", [256, T], BF16),
        vntok=scratch("vntok", [T, 256], BF16),
        usT=scratch("usT", [256, T], F32), vstok=scratch("vstok", [T, 256], F32),
        ymT=scratch("ymT", [1024, T], BF16),
        xa=scratch("xa", [T, D], F32), xb=scratch("xb", [T, D], F32),
        wup_bf=scratch("wup_bf", [2, D, 2 * DFF], BF16), wdn_bf=scratch("wdn_bf", [2, DFF, D], BF16),
        wout_bf=scratch("wout_bf", [2, D, D], BF16), win_bf=scratch("win_bf", [2, D, DIN], BF16),
    )
    with contextlib.ExitStack() as st:
        P = Prog(nc, st)
        g.P = P
        g.ident_bf = st.enter_context(nc.sbuf_tensor("ident_bf", [128, 128], BF16))
        g.ones_f = st.enter_context(nc.sbuf_tensor("ones_f", [128, 128], F32))
        g.tmp_f = st.enter_context(nc.sbuf_tensor("tmp_f", [128, 128], F32))
        g.ps = [st.enter_context(nc.psum_tensor("ps%d" % i, [128, 512], F32)) for i in range(7)]
        g.pst = st.enter_context(nc.psum_tensor("pst", [128, 1024], BF16))
        P.add("pool", lambda e: e.memset(g.ones_f[:], 1.0), writes=["ones_f"])
        P.add("pool", lambda e: e.affine_select(out=g.tmp_f[:], in_=g.ones_f[:], pattern=[[1, 128]],
                                                compare_op=ALU.is_equal, fill=0.0, base=0,
                                                channel_multiplier=-1), reads=["ones_f"], writes=["tmp_f"])
        P.add("dve", lambda e: e.tensor_copy(out=g.ident_bf[:], in_=g.tmp_f[:]), reads=["tmp_f"], writes=["ident_bf"])
        P.flush()
        xin = g.x
        g.mg = st.enter_context(nc.sbuf_tensor("mg", [128, 8], F32))
        if "rwkv" in g.phases: rwkv_consts(g, st)
        for l in range(nlayers):
            load_layer_consts(g, l)
            phase1(g, l, xin)
            if l == 0:
                def bgconv(group, dst, src, rows, step):
                    for r0 in range(0, rows, step):
                        P.add_bg(group, lambda e, r0=r0, dst=dst, src=src, step=step: e.dma_start(out=dst[r0:r0 + step, :], in_=src[r0:r0 + step, :]))
                for ll in range(nlayers):
                    bgconv("bg%d" % (0 + 4 * ll), g.sc["wout_bf"][ll], g.w["w_out"][ll], D, 256)
                    bgconv("bg%d" % (1 + 4 * ll), g.sc["wup_bf"][ll], g.w["ffn_w_up"][ll], D, 128)
                    bgconv("bg%d" % (2 + 4 * ll), g.sc["wdn_bf"][ll], g.w["ffn_w_down"][ll], DFF, 256)
                    if ll > 0:
                        bgconv("bg%d" % (3 + 4 * ll), g.sc["win_bf"][ll], g.w["w_in"][ll], D, 128)
            if "conv" in g.phases: phase_conv(g, l)
            if "sgu" in g.phases: phase_sgu(g, l)
            if "na" in g.phases: phase_na(g, l)
            if "rwkv" in g.phases: phase_rwkv(g, l)
            if "p3" in g.phases:
                phase3a(g, l, xin)
                phase3b(g, l, g.y if l == nlayers - 1 else g.sc["xa"])
            P.flush()
            xin = g.sc["xa"]
        P.finish()
    return nc


FM_CHUNKS = (
    ("ch0", 0, 128, "save_ch", None, 0), ("cc0", 512, 128, "mul_ch", "pT", 0),
    ("ch1", 128, 128, "save_ch", None, 0), ("cc1", 640, 128, "mul_ch", "pT", 128),
    ("cb0", 256, 128, "copy", "cbT", 0), ("cb1", 384, 128, "copy", "cbT", 128),
    ("r0", 768, 128, "copy", "rT", 0), ("r1", 896, 128, "copy", "rT", 128),
    ("k0", 1024, 128, "copy", "kT", 0), ("k1", 1152, 128, "copy", "kT", 128),
    ("v0", 1280, 128, "copy", "vT", 0), ("v1", 1408, 128, "copy", "vT", 128),
    ("lora", 1536, 96, "lora", "loraT", 0),
    ("qn0", 1632, 128, "q", "qnT", 0), ("qn1", 1760, 128, "q", "qnT", 128),
    ("kn0", 1888, 128, "copybf", "knT", 0), ("kn1", 2016, 128, "copybf", "knT", 128),
    ("us0", 2400, 128, "gelu", "usT", 0), ("us1", 2528, 128, "gelu", "usT", 128),
)
TM_GROUPS = (("vtok", 1280, "copybf"), ("vntok", 2144, "copybf"), ("vstok", 2656, "gelu"))


def phase1(g, l, xin):
    nc, P, T = g.nc, g.P, g.T
    NW = T // 512
    with contextlib.ExitStack() as st:
        sb = mk_sb(g, st)
        Win = sb("Win", [128, 8, DIN], BF16)
        gpre = sb("gpre", [128, 8], F32)
        xt = [sb("xt%d" % i, [128, D], F32) for i in range(2)]
        junk = sb("junk", [128, D], BF16)
        xs = sb("xs", [128, D], BF16)
        ss = sb("ss", [128, 4], F32)
        hT = [sb("hT%d" % i, [128, 8, 512], BF16) for i in range(2)]
        stg = [sb("stg%d" % i, [128, 512], F32) for i in range(4)]
        stgb = [sb("stgb%d" % i, [128, 512], BF16) for i in range(2)]
        cht = sb("cht", [128, 512], F32)
        win_src = g.w["w_in"]
        for kc in range(8):
            if l == 0:
                def f(e, kc=kc):
                    return e.dma_start(out=Win[:, kc, :], in_=win_src[l, kc * 128:(kc + 1) * 128, :])
                P.add("pool", f, writes=[("Win", kc)], dma=True)
            else:
                DMA(P, "sp", Win[:, kc, :], g.sc["win_bf"][l, kc * 128:(kc + 1) * 128, :], reads=[("bg", "bg%d" % (3 + 4 * l))],
                    writes=[("Win", kc)])
        P.add("sp", lambda e: e.dma_start(out=gpre[:], in_=g.w["norm_mix_pre"][l].rearrange("(c p) -> p c", p=128),
                                          allow_slow_non_contiguous=True), writes=["gpre"], dma=True)
        cnt = dict(stg=0, stgb=0, ps=0, x=0)
        for w in range(NW):
            h = hT[w % 2]; hk = "hT%d" % (w % 2)
            for s in range(4):
                t0 = w * 512 + s * 128
                xi = cnt["x"] % 2; cnt["x"] += 1
                xtile = xt[xi]; xk = "xt%d" % xi
                P.add("sp", lambda e, xtile=xtile, t0=t0: e.dma_start(out=xtile[:], in_=xin[t0:t0 + 128, :]),
                      writes=[xk], dma=True)
                P.add("act", lambda e, xtile=xtile, s=s: e.activation(out=junk[:], in_=xtile[:], func=AF.Square,
                                                                     accum_out=ss[:, 0:1]),
                      reads=[xk], writes=["junk", "ss"])
                P.add("act", lambda e: e.activation(out=ss[:, 1:2], in_=ss[:, 0:1], func=AF.Sqrt,
                                                    scale=1.0 / D, bias=g_eps(g)),
                      reads=["ss"], writes=["ss1"])
                P.add("dve", lambda e: e.reciprocal(out=ss[:, 2:3], in_=ss[:, 1:2]), reads=["ss1"], writes=["ss2"])
                P.add("dve", lambda e, xtile=xtile: e.tensor_scalar(out=xs[:], in0=xtile[:], scalar1=ss[:, 2:3],
                                                                   scalar2=None, op0=ALU.mult),
                      reads=[xk, "ss2"], writes=["xs"])
                for c in range(8):
                    P.add("pe", lambda e, c=c: e.transpose(out=g.pst[:, c * 128:(c + 1) * 128],
                                                           in_=xs[:, c * 128:(c + 1) * 128], identity=g.ident_bf[:]),
                          reads=["xs", "ident_bf"], writes=["pst"])
                P.add("dve", lambda e, h=h, s=s: e.tensor_tensor(
                    out=h[:, :, s * 128:(s + 1) * 128], in0=g.pst[:].rearrange("p (c t) -> p c t", c=8),
                    in1=gpre[:].unsqueeze(2).to_broadcast([128, 8, 128]), op=ALU.mult),
                      reads=["pst", "gpre"], writes=[hk])
            if w == 0 and 'dbg_ss' in g.dbgnames:
                d1 = nc.dram_tensor("dbg_ss", [128, 4], F32, kind="ExternalOutput").ap()
                d2 = nc.dram_tensor("dbg_hT", [128, 8 * 512], BF16, kind="ExternalOutput").ap()
                d3 = nc.dram_tensor("dbg_id", [128, 128], BF16, kind="ExternalOutput").ap()
                d4 = nc.dram_tensor("dbg_xs", [128, 1024], BF16, kind="ExternalOutput").ap()
                P.add("sp", lambda e: e.dma_start(out=d1, in_=ss[:]), reads=["ss", "ss1", "ss2"], dma=True)
                P.add("sp", lambda e, h=h: e.dma_start(out=d2, in_=h[:].rearrange("p c t -> p (c t)")), reads=[hk], dma=True)
                P.add("sp", lambda e: e.dma_start(out=d3, in_=g.ident_bf[:]), reads=["ident_bf"], dma=True)
                P.add("sp", lambda e: e.dma_start(out=d4, in_=xs[:]), reads=["xs"], dma=True)
            tsl = slice(w * 512, (w + 1) * 512)
            for (name, c0, ncol, kind, dst, r0) in FM_CHUNKS:
                pi = cnt["ps"] % 7; cnt["ps"] += 1
                ps = g.ps[pi]; pk = "ps%d" % pi
                for kc in range(8):
                    P.add("pe", lambda e, ps=ps, kc=kc, c0=c0, ncol=ncol, h=h: e.matmul(
                        ps[0:ncol, :], Win[:, kc, c0:c0 + ncol], h[:, kc, :], start=(kc == 0), stop=(kc == 7)),
                          reads=[("Win", kc), hk], writes=[pk])
                if kind == "save_ch":
                    P.add("act", lambda e, ps=ps: e.copy(out=cht[:], in_=ps[:]), reads=[pk], writes=["cht"])
                    continue
                if kind in ("q", "copybf"):
                    si = cnt["stgb"] % 2; cnt["stgb"] += 1
                    so = stgb[si]; sk = "stgb%d" % si
                else:
                    si = cnt["stg"] % 4; cnt["stg"] += 1
                    so = stg[si]; sk = "stg%d" % si
                if kind == "mul_ch":
                    P.add("dve", lambda e, ps=ps, so=so: e.tensor_tensor(out=so[:], in0=ps[:], in1=cht[:], op=ALU.mult),
                          reads=[pk, "cht"], writes=[sk])
                elif kind == "copy":
                    P.add("dve", lambda e, ps=ps, so=so: e.tensor_copy(out=so[:], in_=ps[:]), reads=[pk], writes=[sk])
                elif kind == "copybf":
                    P.add("act", lambda e, ps=ps, so=so: e.copy(out=so[:], in_=ps[:]), reads=[pk], writes=[sk])
                elif kind == "q":
                    P.add("act", lambda e, ps=ps, so=so: e.mul(out=so[:], in_=ps[:], mul=0.125), reads=[pk], writes=[sk])
                elif kind == "gelu":
                    P.add("act", lambda e, ps=ps, so=so: e.activation(out=so[:], in_=ps[:], func=AF.Gelu),
                          reads=[pk], writes=[sk])
                elif kind == "lora":
                    P.add("act", lambda e, ps=ps, so=so: e.activation(out=so[0:32, :], in_=ps[0:32, :], func=AF.Tanh),
                          reads=[pk], writes=[sk])
                    P.add("act", lambda e, ps=ps, so=so: e.copy(out=so[32:64, :], in_=ps[32:64, :]),
                          reads=[pk], writes=[sk])
                    P.add("act", lambda e, ps=ps, so=so: e.activation(out=so[64:96, :], in_=ps[64:96, :], func=AF.Sigmoid),
                          reads=[pk], writes=[sk])
                P.add("pool", lambda e, so=so, dst=dst, r0=r0, ncol=ncol, tsl=tsl: e.dma_start(
                    out=g.sc[dst][r0:r0 + ncol, tsl], in_=so[0:ncol, :]), reads=[sk], dma=True)
            for s in range(4):
                t0 = w * 512 + s * 128
                for (dst, c0, kind) in TM_GROUPS:
                    pi = cnt["ps"] % 7; cnt["ps"] += 1
                    ps = g.ps[pi]; pk = "ps%d" % pi
                    for kc in range(8):
                        P.add("pe", lambda e, ps=ps, kc=kc, c0=c0, s=s, h=h: e.matmul(
                            ps[:, 0:256], h[:, kc, s * 128:(s + 1) * 128], Win[:, kc, c0:c0 + 256],
                            start=(kc == 0), stop=(kc == 7)), reads=[("Win", kc), hk], writes=[pk])
                    if kind == "gelu":
                        si = cnt["stg"] % 4; cnt["stg"] += 1
                        so = stg[si]; sk = "stg%d" % si
                        P.add("act", lambda e, ps=ps, so=so: e.activation(out=so[:, 0:256], in_=ps[:, 0:256], func=AF.Gelu),
                              reads=[pk], writes=[sk])
                    else:
                        si = cnt["stgb"] % 2; cnt["stgb"] += 1
                        so = stgb[si]; sk = "stgb%d" % si
                        P.add("dve", lambda e, ps=ps, so=so: e.tensor_copy(out=so[:, 0:256], in_=ps[:, 0:256]),
                              reads=[pk], writes=[sk])
                    P.add("pool", lambda e, so=so, dst=dst, t0=t0: e.dma_start(
                        out=g.sc[dst][t0:t0 + 128, :], in_=so[:, 0:256]), reads=[sk], dma=True)
        P.flush()


def g_eps(g):
    return EPS


def A(P, eng, fn, reads=(), writes=()):
    return P.add(eng, fn, reads, writes)


def DMA(P, q, out, in_, reads=(), writes=(), slow=False):
    if q == "sp" and not writes:
        q = "pool"
    if slow:
        return P.add(q, lambda e: e.dma_start(out=out, in_=in_, allow_slow_non_contiguous=True), reads, writes, dma=True)
    return P.add(q, lambda e: e.dma_start(out=out, in_=in_), reads, writes, dma=True)


def gnorm(g, l, grp, y0, y1, ykeys, t0, N, W):
    nc, P = g.nc, g.P
    ps = g.ps[6]; pk = "ps6"
    sq = W["gn_sq"]; rin = W["gn_rin"]
    for j, yj in enumerate((y0, y1)):
        P.add("act", lambda e, yj=yj, j=j: e.activation(out=sq[j][:, 0:N], in_=yj, func=AF.Square),
              reads=[ykeys[j]], writes=["gn_sq%d" % j])
    for j in range(2):
        P.add("pe", lambda e, j=j: e.matmul(ps[:, 0:N], g.ones_f[:], sq[j][:, 0:N], start=(j == 0), stop=(j == 1)),
              reads=["gn_sq%d" % j, "ones_f"], writes=[pk])
    P.add("act", lambda e: e.activation(out=rin[:, 0:N], in_=ps[:, 0:N], func=AF.Sqrt, scale=1.0 / 256, bias=EPS),
          reads=[pk], writes=["gn_rin"])
    P.add("dve", lambda e: e.reciprocal(out=rin[:, 0:N], in_=rin[:, 0:N]), reads=["gn_rin"], writes=["gn_rin"])
    for j, yj in enumerate((y0, y1)):
        ob = W["gn_ob"][j]; ok = "gn_ob%d" % j
        c = grp * 2 + j
        P.add("dve", lambda e, yj=yj, ob=ob, c=c: e.scalar_tensor_tensor(
            out=ob[:, 0:N], in0=yj, scalar=g.mg[:, c:c + 1], in1=rin[:, 0:N], op0=ALU.mult, op1=ALU.mult),
              reads=[ykeys[j], "gn_rin", "mg"], writes=[ok])
        DMA(P, "sp", g.sc["ymT"][c * 128:(c + 1) * 128, t0:t0 + N], ob[:, 0:N], reads=[ok])


def gn_alloc(g, st):
    nc = g.nc
    sb = mk_sb(g, st)
    return dict(gn_sq=[sb("gn_sq%d" % j, [128, 512], F32) for j in range(2)], gn_rin=sb("gn_rin", [128, 512], F32),
                gn_ob=[sb("gn_ob%d" % j, [128, 512], BF16) for j in range(2)])


def load_layer_consts(g, l):
    nc, P = g.nc, g.P
    DMA(P, "sp", g.mg[:], g.w["merge_gain"][l].rearrange("(c p) -> p c", p=128), writes=["mg"], slow=True)


def phase_conv(g, l):
    nc, P, T = g.nc, g.P, g.T
    with contextlib.ExitStack() as st:
        sb = mk_sb(g, st)
        W = gn_alloc(g, st)
        cw = sb("cw", [128, 2, 3], F32)
        pp = sb("pp", [128, T + 2], F32)
        cb = sb("cb", [128, T], F32)
        yy = [sb("yc%d" % j, [128, T], F32) for j in range(2)]
        for j in range(2):
            DMA(P, "sp", cw[:, j, :], g.w["conv_a_w"][l][:, j * 128:(j + 1) * 128].rearrange("k p -> p k"),
                writes=["cw"], slow=True)
        for j in range(2):
            A(P, "pool", lambda e: e.memset(pp[:, 0:1], 0.0), writes=["pp"])
            A(P, "pool", lambda e: e.memset(pp[:, T + 1:T + 2], 0.0), writes=["pp"])
            DMA(P, "sp", pp[:, 1:T + 1], g.sc["pT"][j * 128:(j + 1) * 128, :], writes=["pp"])
            DMA(P, "sp", cb[:], g.sc["cbT"][j * 128:(j + 1) * 128, :], writes=["cb"])
            y = yy[j]; yk = "yc%d" % j
            A(P, "dve", lambda e, y=y, j=j: e.tensor_scalar(out=y[:], in0=pp[:, 0:T], scalar1=cw[:, j, 0:1], scalar2=None,
                                                           op0=ALU.mult), reads=["pp", "cw"], writes=[yk])
            for k in (1, 2):
                A(P, "dve", lambda e, y=y, j=j, k=k: e.scalar_tensor_tensor(
                    out=y[:], in0=pp[:, k:T + k], scalar=cw[:, j, k:k + 1], in1=y[:], op0=ALU.mult, op1=ALU.add),
                  reads=["pp", "cw", yk], writes=[yk])
            A(P, "dve", lambda e, y=y: e.tensor_tensor(out=y[:], in0=y[:], in1=cb[:], op=ALU.mult),
              reads=[yk, "cb"], writes=[yk])
        for t0 in range(0, T, 512):
            gnorm(g, l, 0, yy[0][:, t0:t0 + 512], yy[1][:, t0:t0 + 512], ["yc0", "yc1"], t0, 512, W)
        P.flush()


def phase_sgu(g, l):
    nc, P, T = g.nc, g.P, g.T
    with contextlib.ExitStack() as st:
        sb = mk_sb(g, st)
        W = gn_alloc(g, st)
        wraw = sb("wraw", [128, 4, 128], F32)
        wbf = sb("wbf", [128, 4, 128], BF16)
        wsT = sb("wsT", [128, 4, 128], BF16)
        bs = sb("bs", [1, 4, 128], F32)
        sgn = sb("sgn", [128, 256], F32)
        uT = sb("uT", [128, 2, T], F32)
        yy = [sb("ys%d" % j, [128, T], F32) for j in range(2)]
        vt = [sb("vt%d" % i, [128, 256], F32) for i in range(2)]
        st6 = sb("st6", [128, 6], F32)
        mv = sb("mv", [128, 4], F32)
        vc = sb("vc", [128, 256], F32)
        vn = [sb("vn%d" % i, [128, 256], BF16) for i in range(2)]
        DMA(P, "sp", wraw[:], g.w["sgu_w"][l].rearrange("h p q -> p h q"), writes=["wraw"])
        DMA(P, "sp", bs[:], g.w["sgu_b"][l:l + 1], writes=["bs"])
        DMA(P, "sp", sgn[:], g.w["sgu_norm"][l].partition_broadcast(128), writes=["sgn"], slow=True)
        for j in range(2):
            DMA(P, "sp", uT[:, j, :], g.sc["usT"][j * 128:(j + 1) * 128, :], writes=["uT"])
        A(P, "dve", lambda e: e.tensor_copy(out=wbf[:], in_=wraw[:]), reads=["wraw"], writes=["wbf"])
        for h in range(4):
            A(P, "pe", lambda e, h=h: e.transpose(out=g.pst[:, h * 128:(h + 1) * 128], in_=wbf[:, h, :],
                                                  identity=g.ident_bf[:]), reads=["wbf", "ident_bf"], writes=["pst"])
        A(P, "dve", lambda e: e.tensor_copy(out=wsT[:].rearrange("p h q -> p (h q)"), in_=g.pst[:, 0:512]),
          reads=["pst"], writes=["wsT"])
        for n in range(T // 128):
            i = n % 2
            v = vt[i]; vk = "vt%d" % i
            DMA(P, "sp", v[:], g.sc["vstok"][n * 128:(n + 1) * 128, :], writes=[vk])
            A(P, "dve", lambda e, v=v: e.bn_stats(out=st6[:], in_=v[:]), reads=[vk], writes=["st6"])
            A(P, "dve", lambda e: e.bn_aggr(out=mv[:, 0:2], in_=st6[:]), reads=["st6"], writes=["mv"])
            A(P, "act", lambda e: e.activation(out=mv[:, 2:3], in_=mv[:, 1:2], func=AF.Sqrt, scale=1.0, bias=EPS),
              reads=["mv"], writes=["mv2"])
            A(P, "dve", lambda e: e.reciprocal(out=mv[:, 3:4], in_=mv[:, 2:3]), reads=["mv2"], writes=["mv3"])
            A(P, "dve", lambda e, v=v: e.tensor_scalar(out=vc[:], in0=v[:], scalar1=mv[:, 0:1], scalar2=mv[:, 3:4],
                                                      op0=ALU.subtract, op1=ALU.mult),
              reads=[vk, "mv", "mv3"], writes=["vc"])
            vnn = vn[i]; vnk = "vn%d" % i
            A(P, "dve", lambda e, vnn=vnn: e.tensor_tensor(out=vnn[:], in0=vc[:], in1=sgn[:], op=ALU.mult),
              reads=["vc", "sgn"], writes=[vnk])
            for hp in range(2):
                pi = (n * 2 + hp) % 6
                ps = g.ps[pi]; pk = "ps%d" % pi
                for e2 in range(2):
                    h = hp * 2 + e2
                    A(P, "pe", lambda e, ps=ps, vnn=vnn, hp=hp, h=h, e2=e2: e.matmul(
                        ps[:, e2 * 128:(e2 + 1) * 128], vnn[:, hp * 128:(hp + 1) * 128], wsT[:, h, :],
                        start=True, stop=False), reads=[vnk, "wsT"], writes=[pk])
                    A(P, "pe", lambda e, ps=ps, h=h, e2=e2: e.matmul(
                        ps[:, e2 * 128:(e2 + 1) * 128], g.ones_f[0:1, :], bs[0:1, h, :],
                        start=False, stop=True), reads=["ones_f", "bs"], writes=[pk])
                y = yy[hp]; yk = "ys%d" % hp
                for e2 in range(2):
                    A(P, "dve", lambda e, ps=ps, y=y, hp=hp, e2=e2, n=n: e.tensor_tensor(
                        out=y[e2 * 64:(e2 + 1) * 64, n * 128:(n + 1) * 128],
                        in0=ps[e2 * 64:(e2 + 1) * 64, e2 * 128:(e2 + 1) * 128],
                        in1=uT[e2 * 64:(e2 + 1) * 64, hp, n * 128:(n + 1) * 128], op=ALU.mult),
                      reads=[pk, "uT"], writes=[yk])
        for t0 in range(0, T, 512):
            gnorm(g, l, 3, yy[0][:, t0:t0 + 512], yy[1][:, t0:t0 + 512], ["ys0", "ys1"], t0, 512, W)
        P.flush()


def na_consts(g, st, l):
    nc, P = g.nc, g.P
    sbp = mk_sb(g, st)
    g.BT = {l: sbp("BT%d" % l, [128, 4, 14, 64], BF16)}
    g.ones_bf = sbp("ones_bf", [128, 128], BF16)
    A(P, "dve", lambda e: e.tensor_copy(out=g.ones_bf[:], in_=g.ones_f[:]), reads=["ones_f"], writes=["ones_bf"])
    with contextlib.ExitStack() as st2:
        sb = mk_sb(g, st2)
        OH = sb("OH", [31, 2, 64, 64], F32)
        madd = sb("madd", [128, 64], F32)
        rpbT = sb("rpbT", [31, 60], F32)
        A(P, "pool", lambda e: e.memset(OH[:], 1.0), writes=["OH"])
        A(P, "pool", lambda e: e.affine_select(out=OH[:], in_=OH[:], pattern=[[0, 2], [1, 64], [-1, 64]],
                                               compare_op=ALU.is_equal, fill=0.0, base=15, channel_multiplier=-1),
          reads=["OH"], writes=["OH"])
        A(P, "pool", lambda e: e.memset(madd[:], 0.0), writes=["madd"])
        for e2 in range(2):
            m = madd[e2 * 64:(e2 + 1) * 64, :]
            sel = lambda ap, pat, base, cm: A(P, "pool", lambda e: e.affine_select(
                out=ap, in_=ap, pattern=pat, compare_op=ALU.is_ge, fill=-100.0, base=base, channel_multiplier=cm),
                                              reads=["madd"], writes=["madd"])
            sel(m[:, 0:8], [[0, 8]], 15, -1)
            sel(m[:, 8:57], [[-1, 49]], 0, 1)
            sel(m[:, 8:57], [[1, 49]], 15, -1)
            sel(m[:, 57:64], [[0, 7]], -48, 1)
        for l in (l,):
            DMA(P, "sp", rpbT[:], g.w["na_rpb"][l].rearrange("h d i -> i (h d)"), writes=["rpbT"], slow=True)
            for cg in range(8):
                pi = cg % 6
                ps = g.ps[pi]; pk = "ps%d" % pi
                for ci in range(8):
                    c = cg * 8 + ci
                    A(P, "pe", lambda e, ps=ps, ci=ci, c=c: e.matmul(
                        ps[:, ci * 60:(ci + 1) * 60], OH[:, :, :, c].rearrange("i e k -> i (e k)"), rpbT[:],
                        start=True, stop=True), reads=["OH", "rpbT"], writes=[pk])
                for e2 in range(2):
                    A(P, "dve", lambda e, ps=ps, e2=e2, cg=cg, l=l: e.tensor_tensor(
                        out=g.BT[l][e2 * 64:(e2 + 1) * 64, :, :, cg * 8:(cg + 1) * 8],
                        in0=ps[e2 * 64:(e2 + 1) * 64, 0:480].rearrange("p (c h d) -> p h d c", c=8, h=4)[:, :, e2:e2 + 14, :],
                        in1=madd[e2 * 64:(e2 + 1) * 64, cg * 8:(cg + 1) * 8].unsqueeze(1).unsqueeze(1).to_broadcast([64, 4, 14, 8]),
                        op=ALU.add), reads=[pk, "madd"], writes=[("BT", l)])
        if "dbg_BT" in g.dbgnames:
            d1 = nc.dram_tensor("dbg_BT", [128, 4 * 14 * 64], BF16, kind="ExternalOutput").ap()
            DMA(P, "sp", d1, g.BT[l][:].rearrange("p h j c -> p (h j c)"), reads=[("BT", l)])
        P.flush()


def phase_na(g, l):
    nc, P, T = g.nc, g.P, g.T
    rows = T // 64
    with contextlib.ExitStack() as st:
        na_consts(g, st, l)
        sb = mk_sb(g, st)
        W = gn_alloc(g, st)
        kn = sb("kn", [128, 2, T], BF16)
        qn = sb("qn", [128, 2, T], BF16)
        Va = sb("Va", [128, T // 128, 256], BF16)
        Vb = sb("Vb", [128, T // 128 - 1, 256], BF16)
        yn = sb("yn", [128, 2, T], F32)
        pT = [sb("pT%d" % i, [128, 256], BF16) for i in range(2)]
        rec = [sb("rec%d" % i, [128, 64], F32) for i in range(2)]
        ssb = [sb("ssb%d" % i, [128, 4, 64], F32) for i in range(2)]
        for j in range(2):
            DMA(P, "sp", kn[:, j, :], g.sc["knT"][j * 128:(j + 1) * 128, :], writes=["kn"])
            DMA(P, "sp", qn[:, j, :], g.sc["qnT"][j * 128:(j + 1) * 128, :], writes=["qn"])
        DMA(P, "sp", Va[:], g.sc["vntok"].rearrange("(n p) f -> p n f", p=128), writes=["Va"])
        DMA(P, "sp", Vb[:], g.sc["vntok"][64:T - 64, :].rearrange("(n p) f -> p n f", p=128), writes=["Vb"])
        it = 0
        for h in range(4):
            hp, hb = h // 2, (h % 2) * 64
            for r in range(rows):
                rs = min(max(r - 4, 0), rows - 8)
                pa = g.ps[(it % 3) * 2]; pak = "ps%d" % ((it % 3) * 2)
                pb = g.ps[(it % 3) * 2 + 1]; pbk = "ps%d" % ((it % 3) * 2 + 1)
                pt = pT[it % 2]; ptk = "pT%d" % (it % 2)
                rc = rec[it % 2]; rck = "rec%d" % (it % 2)
                it += 1
                j0 = rs - r + 7
                for i in range(4):
                    kr0 = rs + 2 * i
                    A(P, "pe", lambda e, pa=pa, i=i, kr0=kr0, r=r, hp=hp, hb=hb: e.matmul(
                        pa[:, i * 64:(i + 1) * 64], kn[hb:hb + 64, hp, kr0 * 64:kr0 * 64 + 128],
                        qn[hb:hb + 64, hp, r * 64:(r + 1) * 64], start=True, stop=True),
                      reads=["kn", "qn"], writes=[pak])
                sb_ = ssb[it % 2]; sbk = "ssb%d" % (it % 2)
                A(P, "dve", lambda e, pa=pa, sb_=sb_, h=h, j0=j0: e.tensor_tensor(
                    out=sb_[:], in0=pa[:, 0:256].rearrange("p (i q) -> p i q", i=4), in1=g.BT[l][:, h, j0:j0 + 7:2, :],
                    op=ALU.add), reads=[pak, ("BT", l)], writes=[sbk])
                A(P, "act", lambda e, sb_=sb_, pt=pt: e.activation(out=pt[:], in_=sb_[:].rearrange("p i q -> p (i q)"), func=AF.Exp),
                  reads=[sbk], writes=[ptk])
                for i in range(4):
                    kr0 = rs + 2 * i
                    Vt = Va[:, kr0 // 2, hp * 128:(hp + 1) * 128] if kr0 % 2 == 0 else Vb[:, (kr0 - 1) // 2, hp * 128:(hp + 1) * 128]
                    A(P, "pe", lambda e, pb=pb, pt=pt, i=i, Vt=Vt: e.matmul(
                        pb[:, 0:64], Vt, pt[:, i * 64:(i + 1) * 64], start=(i == 0), stop=(i == 3)),
                      reads=[ptk, "Va", "Vb"], writes=[pbk])
                A(P, "pe", lambda e, pb=pb, pt=pt: e.matmul(pb[:, 64:320], g.ones_bf[:], pt[:, 0:256], start=True, stop=True),
                  reads=[ptk, "ones_bf"], writes=[pbk])
                A(P, "dve", lambda e, pb=pb, rc=rc, hb=hb: e.tensor_reduce(
                    out=rc[hb:hb + 64, :], in_=pb[hb:hb + 64, 64:320].rearrange("p (i q) -> p q i", i=4),
                    axis=mybir.AxisListType.X, op=ALU.add), reads=[pbk], writes=[rck])
                A(P, "dve", lambda e, rc=rc, hb=hb: e.reciprocal(out=rc[hb:hb + 64, :], in_=rc[hb:hb + 64, :]),
                  reads=[rck], writes=[rck])
                A(P, "dve", lambda e, pb=pb, rc=rc, hb=hb, hp=hp, r=r: e.tensor_tensor(
                    out=yn[hb:hb + 64, hp, r * 64:(r + 1) * 64], in0=pb[hb:hb + 64, 0:64], in1=rc[hb:hb + 64, :],
                    op=ALU.mult), reads=[pbk, rck], writes=["yn"])
        if "dbg_yn" in g.dbgnames:
            d1 = nc.dram_tensor("dbg_yn", [128, 2 * T], F32, kind="ExternalOutput").ap()
            DMA(P, "sp", d1, yn[:].rearrange("p a t -> p (a t)"), reads=["yn"])
            d2 = nc.dram_tensor("dbg_Va", [128, (T // 128) * 256], BF16, kind="ExternalOutput").ap()
            DMA(P, "sp", d2, Va[:].rearrange("p a t -> p (a t)"), reads=["Va"])
            d3 = nc.dram_tensor("dbg_Vb", [128, (T // 128 - 1) * 256], BF16, kind="ExternalOutput").ap()
            DMA(P, "sp", d3, Vb[:].rearrange("p a t -> p (a t)"), reads=["Vb"])
        for t0 in range(0, T, 512):
            gnorm(g, l, 2, yn[:, 0, t0:t0 + 512], yn[:, 1, t0:t0 + 512], ["yn", "yn"], t0, 512, W)
        P.flush()


DSC = float(np.exp(-0.5))
NB = 256


def mm(P, out, lhsT, rhs, start, stop, reads, writes):
    return P.add("pe", lambda e: e.matmul(out, lhsT, rhs, start=start, stop=stop), reads, writes)


def rwkv_consts(g, st):
    nc, P = g.nc, g.P
    sb = mk_sb(g, st)
    g.blk1 = sb("blk1", [128, 128], F32)
    g.idst = sb("idst", [128, 64], F32)
    g.msk3 = [sb("msk3_%d" % d, [64, 3, 64], F32) for d in range(2)]
    g.msk2 = [sb("msk2_%d" % d, [64, 2, 64], F32) for d in range(2)]
    A(P, "pool", lambda e: e.memset(g.blk1[:], 0.0), writes=["blk1"])
    A(P, "pool", lambda e: e.memset(g.blk1[0:64, 0:64], 1.0), writes=["blk1"])
    A(P, "pool", lambda e: e.memset(g.blk1[64:128, 64:128], 1.0), writes=["blk1"])
    A(P, "dve", lambda e: e.tensor_copy(out=g.idst[0:64, :], in_=g.tmp_f[0:64, 0:64]), reads=["tmp_f"], writes=["idst"])
    A(P, "dve", lambda e: e.tensor_copy(out=g.idst[64:128, :], in_=g.tmp_f[64:128, 64:128]), reads=["tmp_f"], writes=["idst"])

    g.hm = sb("hm", [128, 2], F32)
    g.hmn = sb("hmn", [128, 2], F32)
    g.idst_bf = sb("idst_bf", [128, 64], BF16)
    g.mask4 = [sb("mask4_%d" % d, [128, 4, 128], F32) for d in range(2)]
    g.maskT = [sb("maskT_%d" % d, [128, 128], F32) for d in range(2)]
    A(P, "pool", lambda e: e.memset(g.hm[:], 0.0), writes=["hm"])
    A(P, "pool", lambda e: e.memset(g.hm[0:64, 0:1], 1.0), writes=["hm"])
    A(P, "pool", lambda e: e.memset(g.hm[64:128, 1:2], 1.0), writes=["hm"])
    A(P, "pool", lambda e: e.tensor_scalar(out=g.hmn[:], in0=g.hm[:], scalar1=-1.0, scalar2=None, op0=ALU.mult),
      reads=["hm"], writes=["hmn"])
    A(P, "dve", lambda e: e.tensor_copy(out=g.idst_bf[:], in_=g.idst[:]), reads=["idst"], writes=["idst_bf"])

    def selbd(tile_ap, cm, step, op, key):
        A(P, "pool", lambda e: e.memset(tile_ap, 0.0), writes=[key])
        for h2 in range(2):
            ap = tile_ap[h2 * 64:(h2 + 1) * 64, h2 * 64:(h2 + 1) * 64]
            A(P, "pool", lambda e, ap=ap: e.memset(ap, 1.0), writes=[key])
            A(P, "pool", lambda e, ap=ap: e.affine_select(out=ap, in_=ap, pattern=[[step, 64]], compare_op=op, fill=0.0,
                                                         base=0, channel_multiplier=cm), reads=[key], writes=[key])
    for d in range(2):
        sg = 1 if d == 0 else -1
        selbd(g.mask4[d][:, 0, :], -sg, sg, ALU.is_gt, "mskbd")
        selbd(g.mask4[d][:, 1, :], -sg, sg, ALU.is_ge, "mskbd")
        selbd(g.mask4[d][:, 2, :], -sg, sg, ALU.is_gt, "mskbd")
        selbd(g.mask4[d][:, 3, :], -sg, sg, ALU.is_ge, "mskbd")
        selbd(g.maskT[d][:, :], sg, -sg, ALU.is_gt, "mskbd")

    def sel(ap, cm, step, op, key):
        A(P, "pool", lambda e: e.memset(ap, 1.0), writes=[key])
        A(P, "pool", lambda e: e.affine_select(out=ap, in_=ap, pattern=[[step, 64]], compare_op=op, fill=0.0,
                                               base=0, channel_multiplier=cm), reads=[key], writes=[key])
    for d in range(2):
        sg = 1 if d == 0 else -1
        sel(g.msk3[d][:, 0, :], -sg, sg, ALU.is_gt, "msk")
        sel(g.msk3[d][:, 1, :], sg, -sg, ALU.is_gt, "msk")
        sel(g.msk3[d][:, 2, :], -sg, sg, ALU.is_gt, "msk")
        sel(g.msk2[d][:, 0, :], -sg, sg, ALU.is_ge, "msk")
        sel(g.msk2[d][:, 1, :], -sg, sg, ALU.is_ge, "msk")


def phase_rwkv(g, l):
    nc, P, T = g.nc, g.P, g.T
    with contextlib.ExitStack() as st:
        sb = mk_sb(g, st)
        W = gn_alloc(g, st)
        Wl = sb("Wl", [96, 5, 256], F32)
        w0c = sb("w0c", [128, 2, 2], F32); a0c = sb("a0c", [128, 2, 2], F32)
        kkc = sb("kkc", [128, 2], F32); kac = sb("kac", [128, 2], F32); omk = sb("omk", [128, 2], F32)
        rkc = sb("rkc", [128, 2], F32); lwc = sb("lwc", [128, 2], F32); lbc = sb("lbc", [128, 2], F32)
        yacc = sb("yacc", [128, 2, T], F32)
        A(P, "pool", lambda e: e.memset(Wl[:], 0.0), writes=["Wl"])
        A(P, "pool", lambda e: e.memset(yacc[:], 0.0), writes=[("yacc", 0), ("yacc", 1)])
        for d in range(2):
            DMA(P, "sp", Wl[d * 16:(d + 1) * 16, d, :], g.w["rwkv_w_up"][l, d], writes=["Wl"])
            DMA(P, "sp", Wl[32 + d * 16:32 + (d + 1) * 16, 2 + d, :], g.w["rwkv_a_up"][l, d], writes=["Wl"])
            for hp in range(2):
                DMA(P, "sp", w0c[:, d, hp:hp + 1], g.w["rwkv_w0"][l, d, hp * 128:(hp + 1) * 128].unsqueeze(1), writes=["cst"], slow=True)
                DMA(P, "sp", a0c[:, d, hp:hp + 1], g.w["rwkv_a0"][l, d, hp * 128:(hp + 1) * 128].unsqueeze(1), writes=["cst"], slow=True)
        DMA(P, "sp", Wl[64:96, 4, :], g.w["rwkv_g_up"][l], writes=["Wl"])
        for nm, tl in (("rwkv_k_k", kkc), ("rwkv_k_a", kac), ("rwkv_lnx_w", lwc), ("rwkv_lnx_b", lbc)):
            DMA(P, "sp", tl[:], g.w[nm][l].rearrange("(c p) -> p c", p=128), writes=["cst"], slow=True)
        DMA(P, "sp", rkc[:], g.w["rwkv_r_k"][l].rearrange("(c a) k -> (a k) c", a=2), writes=["cst"], slow=True)
        A(P, "dve", lambda e: e.tensor_scalar(out=omk[:], in0=kac[:], scalar1=-1.0, scalar2=1.0, op0=ALU.mult, op1=ALU.add),
          reads=["cst"], writes=["omk"])
        C_ = dict(Wl=Wl, w0c=w0c, a0c=a0c, kkc=kkc, kac=kac, omk=omk, yacc=yacc)
        gens = []
        sid = 0
        import os
        nsw = int(os.environ.get("NSW", "4"))
        for d in range(2):
            for hp in range(2):
                if sid < nsw:
                    gens.append((rwkv_sweep_bd if os.environ.get("RWBD", "1") == "1" else rwkv_sweep)(g, l, d, hp, sid, st, C_))
                sid += 1
        alive = list(gens)
        import os
        lim = int(os.environ.get("RW_LIMIT", "1000000"))
        rounds = 0
        while alive and rounds < lim:
            nxt = []
            for gen in alive:
                try:
                    next(gen)
                    nxt.append(gen)
                except StopIteration:
                    pass
            alive = nxt
            rounds += 1
        P.flush()
        if "dbg_yacc" in g.dbgnames:
            d1 = nc.dram_tensor("dbg_yacc", [128, 2 * T], F32, kind="ExternalOutput").ap()
            DMA(P, "sp", d1, yacc[:].rearrange("p a t -> p (a t)"), reads=[("yacc", 0), ("yacc", 1)])
            P.flush()
        if lim < 1000000:
            return
        rB = sb("prB", [128, 512], F32); kB = sb("pkB", [128, 512], F32); vB = sb("pvB", [128, 512], F32)
        loB = sb("ploB", [96, 512], F32)
        u = [sb("pu%d" % i, [128, 512], F32) for i in range(4)]
        for t0 in range(0, T, 512):
            ts = slice(t0, t0 + 512)
            DMA(P, "sp", loB[:], g.sc["loraT"][:, ts], writes=["ploB"])
            for hp in range(2):
                hs = slice(hp * 128, (hp + 1) * 128)
                y = yacc[:, hp, ts]
                DMA(P, "sp", rB[:], g.sc["rT"][hs, ts], writes=["prB"])
                DMA(P, "sp", kB[:], g.sc["kT"][hs, ts], writes=["pkB"])
                DMA(P, "sp", vB[:], g.sc["vT"][hs, ts], writes=["pvB"])
                pa, pb = g.ps[4], g.ps[5]
                mm(P, pa[:], g.blk1[:], y, True, True, ["yacc", "blk1"], ["ps4"])
                A(P, "act", lambda e, y=y: e.activation(out=u[0][:], in_=y, func=AF.Square), reads=["yacc"], writes=["pu0"])
                mm(P, pb[:], g.blk1[:], u[0][:], True, True, ["pu0", "blk1"], ["ps5"])
                A(P, "dve", lambda e: e.tensor_scalar(out=u[1][:], in0=pa[:], scalar1=1.0 / 64, scalar2=None, op0=ALU.mult),
                  reads=["ps4"], writes=["pu1"])
                A(P, "dve", lambda e: e.tensor_tensor(out=u[2][:], in0=u[1][:], in1=u[1][:], op=ALU.mult),
                  reads=["pu1"], writes=["pu2"])
                A(P, "dve", lambda e: e.scalar_tensor_tensor(out=u[2][:], in0=pb[:], scalar=1.0 / 64, in1=u[2][:],
                                                             op0=ALU.mult, op1=ALU.subtract),
                  reads=["ps5", "pu2"], writes=["pu2"])
                A(P, "act", lambda e: e.activation(out=u[2][:], in_=u[2][:], func=AF.Sqrt, scale=1.0, bias=64e-5),
                  reads=["pu2"], writes=["pu2"])
                A(P, "dve", lambda e: e.reciprocal(out=u[2][:], in_=u[2][:]), reads=["pu2"], writes=["pu2"])
                A(P, "dve", lambda e, y=y: e.tensor_tensor(out=u[1][:], in0=y, in1=u[1][:], op=ALU.subtract),
                  reads=["yacc", "pu1"], writes=["pu1"])
                A(P, "dve", lambda e: e.tensor_tensor(out=u[1][:], in0=u[1][:], in1=u[2][:], op=ALU.mult),
                  reads=["pu1", "pu2"], writes=["pu1"])
                A(P, "dve", lambda e, hp=hp: e.tensor_scalar(out=u[1][:], in0=u[1][:], scalar1=lwc[:, hp:hp + 1],
                                                            scalar2=lbc[:, hp:hp + 1], op0=ALU.mult, op1=ALU.add),
                  reads=["pu1", "cst"], writes=["pu1"])
                A(P, "dve", lambda e, hp=hp: e.scalar_tensor_tensor(out=u[3][:], in0=rB[:], scalar=rkc[:, hp:hp + 1],
                                                                   in1=kB[:], op0=ALU.mult, op1=ALU.mult),
                  reads=["prB", "pkB", "cst"], writes=["pu3"])
                mm(P, pa[:], g.blk1[:], u[3][:], True, True, ["pu3", "blk1"], ["ps4"])
                A(P, "dve", lambda e: e.tensor_tensor(out=u[3][:], in0=pa[:], in1=vB[:], op=ALU.mult),
                  reads=["ps4", "pvB"], writes=["pu3"])
                A(P, "dve", lambda e: e.tensor_tensor(out=u[1][:], in0=u[1][:], in1=u[3][:], op=ALU.add),
                  reads=["pu1", "pu3"], writes=["pu1"])
                mm(P, pb[:], Wl[:, 4, hs], loB[:], True, True, ["Wl", "ploB"], ["ps5"])
                A(P, "dve", lambda e, y=y: e.tensor_tensor(out=y, in0=u[1][:], in1=pb[:], op=ALU.mult),
                  reads=["pu1", "ps5"], writes=["yacc"])
            gnorm(g, l, 1, yacc[:, 0, ts], yacc[:, 1, ts], ["yacc", "yacc"], t0, 512, W)
        P.flush()


def rwkv_sweep(g, l, d, hp, sid, st, C_):
    nc, P, T = g.nc, g.P, g.T
    sbq = mk_sb(g, st)
    K = lambda n: "%s_s%d" % (n, sid)
    f32t = {}
    for n in ("rB", "kB", "s", "a", "kkr", "kk", "t1", "ke", "bb", "cs", "cw", "u1", "E"):
        f32t[n] = sbq(K(n), [128, NB], F32)
    loB = sbq(K("loB"), [96, NB], F32)
    bft = {n: sbq(K(n), [128, NB], BF16) for n in ("rt", "at", "bt", "kt", "Bh", "Kh")}
    bft1 = {n: sbq(K(n + "1"), [64, NB], BF16) for n in ("rt", "at", "bt", "kt", "Bh", "Kh")}
    VB = sbq(K("VB"), [64, NB // 64, 256], BF16)
    Wt = sbq(K("Wt"), [128, NB // 64], F32)
    Dg = sbq(K("Dg"), [128, NB // 64, 64], BF16)
    Dg1 = sbq(K("Dg1"), [64, NB // 64, 64], BF16)

    def fm(n, h2, cs_):
        return (bft[n] if h2 == 0 else bft1[n])[0:64, cs_]

    def fk(n, h2):
        return K(n) if h2 == 0 else K(n + "1")
    L3 = sbq(K("L3"), [64, 2, 3, 64], BF16)
    M2 = sbq(K("M2"), [64, 2, 2, 64], BF16)
    QT = sbq(K("QT"), [64, 2, 64], BF16)
    PP = sbq(K("PP"), [64, 2, 2, 64], BF16)
    X = [sbq(K("X%d" % i), [64, 2, 128], BF16) for i in range(2)]
    U0d = sbq(K("U0d"), [64, 2, 2, 64], BF16)
    BK = sbq(K("BK"), [64, 2, 2, 64], BF16)
    GT = sbq(K("GT"), [64, 2, 64], BF16)
    Dsb = sbq(K("Dsb"), [64, 2, 64], BF16)
    RhT = sbq(K("RhT"), [64, 2, 64], BF16)
    H2 = [sbq(K("H2%d" % i), [64, 2, 2, 64], BF16) for i in range(2)]
    B = g.ps[sid]; BKEY = "ps%d" % sid
    PB = g.ps[4 + (sid % 2)]; PBK = "ps%d" % (4 + (sid % 2))
    Wl, w0c, a0c, kkc, kac, omk, yacc = (C_[n] for n in ("Wl", "w0c", "a0c", "kkc", "kac", "omk", "yacc"))
    hs = slice(hp * 128, (hp + 1) * 128)
    idb = g.ident_bf
    t_ = f32t
    nck = NB // 64
    A(P, "pool", lambda e: e.memset(H2[0][:], 0.0), writes=[K("H20")])
    hcur = 0
    nblk = T // NB
    blocks = list(range(nblk)) if d == 0 else list(range(nblk - 1, -1, -1))
    first_dir = False

    def v3(ap):
        return ap.rearrange("p (c t) -> p c t", t=64)

    for bi in blocks:
        bs = slice(bi * NB, (bi + 1) * NB)
        DMA(P, "sp", t_["rB"][:], g.sc["rT"][hs, bs], writes=[K("rB")])
        DMA(P, "sp", t_["kB"][:], g.sc["kT"][hs, bs], writes=[K("kB")])
        DMA(P, "sp", loB[:], g.sc["loraT"][:, bs], writes=[K("loB")])
        DMA(P, "sp", VB[:], g.sc["vtok"][bs, :].rearrange("(n s) f -> s n f", s=64), writes=[K("VB")])
        mm(P, PB[:, 0:NB], Wl[:, d, hs], loB[:], True, True, ["Wl", K("loB")], [PBK])
        mm(P, PB[:, NB:2 * NB], Wl[:, 2 + d, hs], loB[:], True, True, ["Wl", K("loB")], [PBK])
        A(P, "act", lambda e: e.activation(out=t_["s"][:], in_=PB[:, 0:NB], func=AF.Sigmoid, bias=w0c[:, d, hp:hp + 1]),
          reads=[PBK, "cst"], writes=[K("s")])
        A(P, "act", lambda e: e.activation(out=t_["a"][:], in_=PB[:, NB:2 * NB], func=AF.Sigmoid, bias=a0c[:, d, hp:hp + 1]),
          reads=[PBK, "cst"], writes=[K("a")])
        A(P, "dve", lambda e: e.tensor_scalar(out=t_["kkr"][:], in0=t_["kB"][:], scalar1=kkc[:, hp:hp + 1], scalar2=None,
                                              op0=ALU.mult), reads=[K("kB"), "cst"], writes=[K("kkr")])
        A(P, "act", lambda e: e.activation(out=t_["u1"][:], in_=t_["kkr"][:], func=AF.Square), reads=[K("kkr")], writes=[K("u1")])
        mm(P, PB[:, 0:NB], g.blk1[:], t_["u1"][:], True, True, ["blk1", K("u1")], [PBK])
        A(P, "act", lambda e: e.activation(out=t_["u1"][:], in_=PB[:, 0:NB], func=AF.Sqrt, scale=1.0, bias=1e-12),
          reads=[PBK], writes=[K("u1")])
        A(P, "dve", lambda e: e.reciprocal(out=t_["u1"][:], in_=t_["u1"][:]), reads=[K("u1")], writes=[K("u1")])
        A(P, "dve", lambda e: e.tensor_tensor(out=t_["kk"][:], in0=t_["kkr"][:], in1=t_["u1"][:], op=ALU.mult),
          reads=[K("kkr"), K("u1")], writes=[K("kk")])
        A(P, "dve", lambda e: e.tensor_scalar(out=t_["t1"][:], in0=t_["a"][:], scalar1=kac[:, hp:hp + 1],
                                              scalar2=omk[:, hp:hp + 1], op0=ALU.mult, op1=ALU.add),
          reads=[K("a"), "cst", "omk"], writes=[K("t1")])
        A(P, "dve", lambda e: e.tensor_tensor(out=t_["ke"][:], in0=t_["kB"][:], in1=t_["t1"][:], op=ALU.mult),
          reads=[K("kB"), K("t1")], writes=[K("ke")])
        A(P, "dve", lambda e: e.tensor_tensor(out=t_["bb"][:], in0=t_["kk"][:], in1=t_["a"][:], op=ALU.mult),
          reads=[K("kk"), K("a")], writes=[K("bb")])
        for c in range(nck):
            A(P, "dve", lambda e, c=c: e.tensor_tensor_scan(out=t_["cs"][:, c * 64:(c + 1) * 64], data0=g.ones_f[:, 0:64],
                                                           data1=t_["s"][:, c * 64:(c + 1) * 64], initial=0.0,
                                                           op0=ALU.mult, op1=ALU.add),
              reads=[K("s"), "ones_f"], writes=[K("cs")])
        totb = v3(t_["cs"][:])[:, :, 63:64].to_broadcast([128, nck, 64])
        if d == 0:
            cw = t_["cs"]; cwk = K("cs")
        else:
            cw = t_["cw"]; cwk = K("cw")
            A(P, "dve", lambda e: e.tensor_tensor(out=t_["u1"][:], in0=t_["s"][:], in1=t_["cs"][:], op=ALU.subtract),
              reads=[K("s"), K("cs")], writes=[K("u1")])
            A(P, "dve", lambda e: e.tensor_tensor(out=v3(cw[:]), in0=v3(t_["u1"][:]), in1=totb, op=ALU.add),
              reads=[K("u1"), K("cs")], writes=[cwk])
        A(P, "act", lambda e: e.activation(out=t_["E"][:], in_=cw[:], func=AF.Exp, scale=-DSC), reads=[cwk], writes=[K("E")])
        A(P, "dve", lambda e: e.tensor_tensor(out=bft["rt"][:], in0=t_["rB"][:], in1=t_["E"][:], op=ALU.mult),
          reads=[K("rB"), K("E")], writes=[K("rt")])
        A(P, "dve", lambda e: e.tensor_tensor(out=t_["u1"][:], in0=cw[:], in1=t_["s"][:], op=ALU.subtract),
          reads=[cwk, K("s")], writes=[K("u1")])
        A(P, "act", lambda e: e.activation(out=t_["E"][:], in_=t_["u1"][:], func=AF.Exp, scale=-DSC), reads=[K("u1")], writes=[K("E")])
        A(P, "dve", lambda e: e.scalar_tensor_tensor(out=bft["at"][:], in0=t_["kk"][:], scalar=-1.0, in1=t_["E"][:],
                                                     op0=ALU.mult, op1=ALU.mult), reads=[K("kk"), K("E")], writes=[K("at")])
        A(P, "act", lambda e: e.activation(out=t_["E"][:], in_=cw[:], func=AF.Exp, scale=DSC), reads=[cwk], writes=[K("E")])
        A(P, "dve", lambda e: e.tensor_tensor(out=bft["bt"][:], in0=t_["bb"][:], in1=t_["E"][:], op=ALU.mult),
          reads=[K("bb"), K("E")], writes=[K("bt")])
        A(P, "dve", lambda e: e.tensor_tensor(out=bft["kt"][:], in0=t_["ke"][:], in1=t_["E"][:], op=ALU.mult),
          reads=[K("ke"), K("E")], writes=[K("kt")])
        if d == 0:
            totc = v3(t_["cs"][:])[:, :, 63:64]
        else:
            totc = v3(cw[:])[:, :, 0:1]
        A(P, "dve", lambda e: e.tensor_tensor(out=v3(t_["u1"][:]), in0=totc.to_broadcast([128, nck, 64]), in1=v3(cw[:]),
                                              op=ALU.subtract), reads=[cwk, K("cs")], writes=[K("u1")])
        A(P, "act", lambda e: e.activation(out=t_["E"][:], in_=t_["u1"][:], func=AF.Exp, scale=-DSC), reads=[K("u1")], writes=[K("E")])
        A(P, "dve", lambda e: e.tensor_tensor(out=bft["Bh"][:], in0=t_["bb"][:], in1=t_["E"][:], op=ALU.mult),
          reads=[K("bb"), K("E")], writes=[K("Bh")])
        A(P, "dve", lambda e: e.tensor_tensor(out=bft["Kh"][:], in0=t_["ke"][:], in1=t_["E"][:], op=ALU.mult),
          reads=[K("ke"), K("E")], writes=[K("Kh")])
        A(P, "act", lambda e: e.activation(out=Wt[:].unsqueeze(2), in_=totc, func=AF.Exp, scale=-DSC),
          reads=[cwk, K("cs")], writes=[K("Wt")])
        A(P, "dve", lambda e: e.tensor_tensor(out=Dg[:], in0=g.idst[:].unsqueeze(1).to_broadcast([128, nck, 64]),
                                              in1=Wt[:].unsqueeze(2).to_broadcast([128, nck, 64]), op=ALU.mult),
          reads=["idst", K("Wt")], writes=[K("Dg")])
        for n in ("rt", "at", "bt", "kt", "Bh", "Kh"):
            DMA(P, "sp", bft1[n][:], bft[n][64:128, :], reads=[K(n)], writes=[K(n + "1")])
        DMA(P, "sp", Dg1[:], Dg[64:128, :, :], reads=[K("Dg")], writes=[K("Dg1")])
        yield
        chunks = list(range(nck)) if d == 0 else list(range(nck - 1, -1, -1))
        for ci in chunks:
            cs_ = slice(ci * 64, (ci + 1) * 64)
            tok = slice(bi * NB + ci * 64, bi * NB + (ci + 1) * 64)
            rt, at, bt, kt, Bh, Kh = (bft[n] for n in ("rt", "at", "bt", "kt", "Bh", "Kh"))
            for h2 in range(2):
                hb = h2 * 64; hh = slice(hb, hb + 64)
                mm(P, B[0:64, (h2 * 3 + 0) * 64:(h2 * 3 + 1) * 64], fm("bt", h2, cs_), fm("at", h2, cs_), True, True, [fk("bt", h2), fk("at", h2)], [BKEY])
                mm(P, B[0:64, (h2 * 3 + 1) * 64:(h2 * 3 + 2) * 64], fm("at", h2, cs_), fm("bt", h2, cs_), True, True, [fk("bt", h2), fk("at", h2)], [BKEY])
                mm(P, B[0:64, (h2 * 3 + 2) * 64:(h2 * 3 + 3) * 64], fm("kt", h2, cs_), fm("at", h2, cs_), True, True, [fk("kt", h2), fk("at", h2)], [BKEY])
            A(P, "dve", lambda e: e.tensor_tensor(
                out=L3[:], in0=B[0:64, 0:384].rearrange("p (h m t) -> p h m t", h=2, m=3),
                in1=g.msk3[d][:].unsqueeze(1).to_broadcast([64, 2, 3, 64]), op=ALU.mult),
              reads=[BKEY, "msk"], writes=[K("L3")])
            A(P, "dve", lambda e: e.tensor_tensor(
                out=QT[:], in0=L3[:, :, 0, :], in1=g.tmp_f[0:64, 0:64].unsqueeze(1).to_broadcast([64, 2, 64]), op=ALU.add),
              reads=[K("L3"), "tmp_f"], writes=[K("QT")])
            yield
            for h2 in range(2):
                hb = h2 * 64; hh = slice(hb, hb + 64)
                mm(P, B[0:64, (h2 * 2 + 0) * 64:(h2 * 2 + 1) * 64], fm("bt", h2, cs_), fm("rt", h2, cs_), True, True, [fk("bt", h2), fk("rt", h2)], [BKEY])
                mm(P, B[0:64, (h2 * 2 + 1) * 64:(h2 * 2 + 2) * 64], fm("kt", h2, cs_), fm("rt", h2, cs_), True, True, [fk("kt", h2), fk("rt", h2)], [BKEY])
            A(P, "dve", lambda e: e.tensor_tensor(
                out=M2[:], in0=B[0:64, 0:256].rearrange("p (h m t) -> p h m t", h=2, m=2),
                in1=g.msk2[d][:].unsqueeze(1).to_broadcast([64, 2, 2, 64]), op=ALU.mult),
              reads=[BKEY, "msk"], writes=[K("M2")])
            yield
            xi = 0
            for h2 in range(2):
                hb = h2 * 64; hh = slice(hb, hb + 64)
                mm(P, B[0:64, h2 * 128:h2 * 128 + 64], fm("at", h2, cs_), idb[0:64, 0:64], True, True, [fk("at", h2), "ident_bf"], [BKEY])
                mm(P, B[0:64, h2 * 128 + 64:h2 * 128 + 128], L3[:, h2, 2, :], VB[:, ci, hp * 128 + hb:hp * 128 + hb + 64],
                   True, True, [K("L3"), K("VB")], [BKEY])
                mm(P, B[0:64, 256 + (h2 * 2) * 64:256 + (h2 * 2 + 1) * 64], fm("Bh", h2, cs_), idb[0:64, 0:64], True, True,
                   [fk("Bh", h2), "ident_bf"], [BKEY])
                mm(P, B[0:64, 256 + (h2 * 2 + 1) * 64:256 + (h2 * 2 + 2) * 64], fm("Kh", h2, cs_), idb[0:64, 0:64], True, True,
                   [fk("Kh", h2), "ident_bf"], [BKEY])
            A(P, "act", lambda e: e.copy(out=X[0][:], in_=B[0:64, 0:256].rearrange("p (h x) -> p h x", h=2)),
              reads=[BKEY], writes=[K("X0")])
            A(P, "act", lambda e: e.copy(out=BK[:], in_=B[0:64, 256:512].rearrange("p (h m t) -> p h m t", h=2, m=2)),
              reads=[BKEY], writes=[K("BK")])
            yield
            for lev in range(6):
                Xc = X[xi]; Xn = X[1 - xi]; xck = K("X%d" % xi); xnk = K("X%d" % (1 - xi))
                for h2 in range(2):
                    mm(P, B[0:64, h2 * 128:(h2 + 1) * 128], QT[:, h2, :], Xc[:, h2, :], True, True, [K("QT"), xck], [BKEY])
                    if lev < 5:
                        if lev == 0:
                            Pm = L3[:, h2, 1, :]; PTm = L3[:, h2, 0, :]; pk = [K("L3")]
                        else:
                            Pm = PP[:, h2, 0, :]; PTm = PP[:, h2, 1, :]; pk = [K("PP")]
                        mm(P, B[0:64, 256 + (h2 * 2) * 64:256 + (h2 * 2 + 1) * 64], PTm, Pm, True, True, pk, [BKEY])
                        mm(P, B[0:64, 256 + (h2 * 2 + 1) * 64:256 + (h2 * 2 + 2) * 64], Pm, PTm, True, True, pk, [BKEY])
                A(P, "act", lambda e, Xn=Xn: e.copy(out=Xn[:], in_=B[0:64, 0:256].rearrange("p (h x) -> p h x", h=2)),
                  reads=[BKEY], writes=[xnk])
                if lev < 5:
                    A(P, "act", lambda e: e.copy(out=PP[:], in_=B[0:64, 256:512].rearrange("p (h m t) -> p h m t", h=2, m=2)),
                      reads=[BKEY], writes=[K("PP")])
                    A(P, "dve", lambda e: e.tensor_tensor(
                        out=QT[:], in0=B[0:64, 256:512].rearrange("p (h m t) -> p h m t", h=2, m=2)[:, :, 1, :],
                        in1=g.tmp_f[0:64, 0:64].unsqueeze(1).to_broadcast([64, 2, 64]), op=ALU.add),
                      reads=[BKEY, "tmp_f"], writes=[K("QT")])
                else:
                    for dup in range(2):
                        A(P, "act", lambda e, dup=dup: e.copy(
                            out=U0d[:, :, dup, :], in_=B[0:64, 0:256].rearrange("p (h x) -> p h x", h=2)[:, :, 64:128]),
                          reads=[BKEY], writes=[K("U0d")])
                xi = 1 - xi
                yield
            Xf = X[xi]; xfk = K("X%d" % xi)
            for h2 in range(2):
                hb = h2 * 64; hh = slice(hb, hb + 64)
                mm(P, B[0:64, h2 * 64:(h2 + 1) * 64], Xf[:, h2, 0:64], BK[:, h2, 0, :], True, False, [xfk, K("BK")], [BKEY])
                mm(P, B[0:64, h2 * 64:(h2 + 1) * 64], idb[0:64, 0:64], (Dg if h2 == 0 else Dg1)[0:64, ci, :], False, True, ["ident_bf", K("Dg"), K("Dg1")], [BKEY])
                mm(P, B[0:64, 128 + h2 * 64:128 + (h2 + 1) * 64], BK[:, h2, 0, :], Xf[:, h2, 64:128], True, False, [xfk, K("BK")], [BKEY])
                mm(P, B[0:64, 128 + h2 * 64:128 + (h2 + 1) * 64], BK[:, h2, 1, :], VB[:, ci, hp * 128 + hb:hp * 128 + hb + 64],
                   False, True, [K("BK"), K("VB")], [BKEY])
                mm(P, B[0:64, 256 + h2 * 64:256 + (h2 + 1) * 64], idb[0:64, 0:64], fm("rt", h2, cs_), True, False, ["ident_bf", fk("rt", h2)], [BKEY])
                mm(P, B[0:64, 256 + h2 * 64:256 + (h2 + 1) * 64], Xf[:, h2, 0:64], M2[:, h2, 0, :], False, True, [xfk, K("M2")], [BKEY])
            import os
            S6V = os.environ.get("S6V", "")
            if S6V == "mm":
                yield
                return
            A(P, "act", lambda e: e.copy(out=GT[:], in_=B[0:64, 0:128].rearrange("p (h t) -> p h t", h=2)), reads=[BKEY], writes=[K("GT")])
            if S6V == "gt":
                yield
                return
            A(P, "act", lambda e: e.copy(out=Dsb[:].rearrange("p h t -> p (h t)"), in_=B[0:64, 128:256]), reads=[BKEY], writes=[K("Dsb")])
            if S6V == "dsb":
                yield
                return
            A(P, "act", lambda e: e.copy(out=RhT[:], in_=B[0:64, 256:384].rearrange("p (h t) -> p h t", h=2)), reads=[BKEY], writes=[K("RhT")])
            yield
            Hc = H2[hcur]; Hn = H2[1 - hcur]; hck = K("H2%d" % hcur); hnk = K("H2%d" % (1 - hcur))
            for h2 in range(2):
                o = B[:, 384 + h2 * 64:384 + (h2 + 1) * 64]
                mm(P, o, Hc[:, h2, :, :].rearrange("p a t -> p (a t)"), RhT[:, h2, :], True, False, [hck, K("RhT")], [BKEY])
                mm(P, o, U0d[:, h2, :, :].rearrange("p a t -> p (a t)"), M2[:, h2, 0, :], False, False, [K("U0d"), K("M2")], [BKEY])
                mm(P, o, VB[:, ci, hp * 128:(hp + 1) * 128], M2[:, h2, 1, :], False, True, [K("VB"), K("M2")], [BKEY])
            S7V = os.environ.get("S7V", "")
            if S7V == "mmY":
                yield
                return
            for h2 in range(2):
                mm(P, B[0:64, h2 * 64:(h2 + 1) * 64], GT[:, h2, :], Hc[:, h2, 0, :], True, False, [K("GT"), hck], [BKEY])
                mm(P, B[0:64, h2 * 64:(h2 + 1) * 64], idb[0:64, 0:64], Dsb[:, h2, :], False, True, ["ident_bf", K("Dsb")], [BKEY])
            if S7V == "mmH":
                yield
                return
            for h2 in range(2):
                hh = slice(h2 * 64, (h2 + 1) * 64)
                if first_dir:
                    A(P, "act", lambda e, h2=h2, hh=hh, tok=tok: e.copy(out=yacc[hh, hp, tok], in_=B[hh, 384 + h2 * 64:384 + (h2 + 1) * 64]),
                      reads=[BKEY], writes=[("yacc", hp)])
                else:
                    A(P, "dve", lambda e, h2=h2, hh=hh, tok=tok: e.tensor_tensor(out=yacc[hh, hp, tok], in0=B[hh, 384 + h2 * 64:384 + (h2 + 1) * 64],
                                                                       in1=yacc[hh, hp, tok], op=ALU.add),
                      reads=[BKEY, ("yacc", hp)], writes=[("yacc", hp)])
            if S7V == "ev1":
                yield
                return
            if S7V == "B":
                for dup in range(2):
                    A(P, "act", lambda e, Hn=Hn, dup=dup: e.copy(
                        out=Hn[:, :, dup, :], in_=B[0:64, 128:256].rearrange("p (h t) -> p h t", h=2)),
                      reads=[BKEY], writes=[hnk])
            elif S7V == "D":
                A(P, "act", lambda e: e.copy(out=RhT[:], in_=B[0:64, 256:384].rearrange("p (h t) -> p h t", h=2)), reads=[BKEY], writes=[K("RhT")])
            elif S7V == "C":
                A(P, "act", lambda e, Hn=Hn: e.copy(
                    out=Hn[:, 0, :, :].rearrange("p a t -> p (a t)"), in_=B[0:64, 0:128]),
                  reads=[BKEY], writes=[hnk])
            else:
              for dup in range(2):
                A(P, "act", lambda e, Hn=Hn, dup=dup: e.copy(
                    out=Hn[:, :, dup, :], in_=B[0:64, 0:128].rearrange("p (h t) -> p h t", h=2)),
                  reads=[BKEY], writes=[hnk])
            hcur = 1 - hcur
            yield


def rowsum_rstd(g, P, srcs, skeys, ssq, eps, key):
    junk = g.junk
    for i, (src, sk) in enumerate(zip(srcs, skeys)):
        P.add("act", lambda e, src=src, i=i: e.activation(out=junk[:, 0:src.shape[1]], in_=src, func=AF.Square,
                                                         accum_out=ssq[:, i:i + 1]),
              reads=[sk], writes=["junk", key + "a%d" % i])
    if len(srcs) == 2:
        P.add("dve", lambda e: e.tensor_tensor(out=ssq[:, 2:3], in0=ssq[:, 0:1], in1=ssq[:, 1:2], op=ALU.add),
              reads=[key + "a0", key + "a1"], writes=[key + "s"])
        tot = ssq[:, 2:3]
    else:
        tot = ssq[:, 0:1]
    P.add("act", lambda e: e.activation(out=ssq[:, 2:3], in_=tot, func=AF.Sqrt, scale=1.0 / D, bias=eps),
          reads=[key + "s", key + "a0"], writes=[key + "q"])
    P.add("dve", lambda e: e.reciprocal(out=ssq[:, 3:4], in_=ssq[:, 2:3]), reads=[key + "q"], writes=[key + "r"])
    return ssq[:, 3:4], key + "r"


def phase3a(g, l, xin):
    nc, P, T = g.nc, g.P, g.T
    with contextlib.ExitStack() as st:
        sb = mk_sb(g, st)
        Wout = sb("Wout", [128, 8, D], BF16)
        gpost = sb("gpost", [128, D], F32)
        mT = [sb("mT%d" % i, [128, 8, 512], BF16) for i in range(2)]
        xt = [sb("x3t%d" % i, [128, D], F32) for i in range(2)]
        tmp = [sb("tmp3_%d" % i, [128, D], F32) for i in range(2)]
        g.junk = sb("junk3", [128, D], BF16)
        ssq = [sb("ssq%d" % i, [128, 4], F32) for i in range(2)]
        for kc in range(8):
            DMA(P, "sp", Wout[:, kc, :], g.sc["wout_bf"][l, kc * 128:(kc + 1) * 128, :], reads=[("bg", "bg%d" % (0 + 4 * l))],
                writes=[("Wout", kc)])
        DMA(P, "sp", gpost[:], g.w["norm_mix_post"][l].partition_broadcast(128), writes=["gpost"], slow=True)
        it = 0
        for w in range(T // 512):
            m = mT[w % 2]; mk = "mT%d" % (w % 2)
            DMA(P, "sp", m[:], g.sc["ymT"][:, w * 512:(w + 1) * 512].rearrange("(c p) t -> p c t", p=128), writes=[mk])
            for s in range(4):
                t0 = w * 512 + s * 128
                i2 = it % 2
                xx = xt[i2]; xk = "x3t%d" % i2
                tt = tmp[i2]; tk = "tmp3_%d" % i2
                sq = ssq[i2]
                pa = g.ps[(it % 3) * 2]; pak = "ps%d" % ((it % 3) * 2)
                pb = g.ps[(it % 3) * 2 + 1]; pbk = "ps%d" % ((it % 3) * 2 + 1)
                it += 1
                DMA(P, "sp", xx[:], xin[t0:t0 + 128, :], writes=[xk])
                for half, (pp, ppk) in enumerate(((pa, pak), (pb, pbk))):
                    for kc in range(8):
                        mm(P, pp[:, :], m[:, kc, s * 128:(s + 1) * 128], Wout[:, kc, half * 512:(half + 1) * 512],
                           kc == 0, kc == 7, [mk, ("Wout", kc)], [ppk])
                rstd, rk = rowsum_rstd(g, P, [pa[:, :], pb[:, :]], [pak, pbk], sq, EPS, "ssq%d" % i2)
                for half, (pp, ppk) in enumerate(((pa, pak), (pb, pbk))):
                    hs = slice(half * 512, (half + 1) * 512)
                    A(P, "dve", lambda e, pp=pp, tt=tt, hs=hs, rstd=rstd: e.scalar_tensor_tensor(
                        out=tt[:, hs], in0=pp[:, :], scalar=rstd, in1=gpost[:, hs], op0=ALU.mult, op1=ALU.mult),
                      reads=[ppk, rk, "gpost"], writes=[tk])
                A(P, "pool", lambda e, tt=tt, xx=xx: e.tensor_tensor(out=tt[:], in0=tt[:], in1=xx[:], op=ALU.add),
                  reads=[tk, xk], writes=[tk])
                DMA(P, "sp", g.sc["xb"][t0:t0 + 128, :], tt[:], reads=[tk])
        P.flush()


def ffn_windows(T):
    wins = []
    pos = 0
    while pos < T:
        s = 0 if pos == 0 else pos - 1
        if s + 512 <= T:
            N = 512
            hi = T if s + N == T else s + N - 1
        else:
            need = T - s
            N = ((need + 127) // 128) * 128
            s = T - N
            hi = T
        wins.append((s, N, pos, hi))
        pos = hi
    return wins


def phase3b(g, l, xout):
    nc, P, T = g.nc, g.P, g.T
    NM = DFF // 128
    with contextlib.ExitStack() as st:
        sb = mk_sb(g, st)
        Wup = sb("Wup", [128, 8, 2 * DFF], BF16)
        Wdn = sb("Wdn", [128, NM, D], BF16)
        gpost = sb("gpost2", [128, D], F32)
        gpre = sb("gpre2", [128, 8], F32)
        fcw = sb("fcw", [128, NM, 3], F32)
        xt = [sb("x4t%d" % i, [128, D], F32) for i in range(2)]
        tmp = sb("tmp4", [128, D], F32)
        g.junk = sb("junk4", [128, D], BF16)
        xs = sb("xs4", [128, D], BF16)
        ssq = [sb("ssq4_%d" % i, [128, 4], F32) for i in range(2)]
        hT = sb("h2T", [128, 8, 512], BF16)
        hid = sb("hidT", [128, NM, 512], BF16)
        gbuf = [sb("gbuf%d" % i, [128, 514], F32) for i in range(2)]
        cv = [sb("cv%d" % i, [128, 512], F32) for i in range(2)]
        ge = [sb("ge%d" % i, [128, 512], BF16) for i in range(2)]
        linsb = [sb("linsb%d" % i, [128, 512], BF16) for i in range(2)]
        for kc in range(8):
            DMA(P, "sp", Wup[:, kc, :], g.sc["wup_bf"][l, kc * 128:(kc + 1) * 128, :], reads=[("bg", "bg%d" % (1 + 4 * l))],
                writes=[("Wup", kc)])
        for m in range(NM):
            DMA(P, "sp", Wdn[:, m, :], g.sc["wdn_bf"][l, m * 128:(m + 1) * 128, :], reads=[("bg", "bg%d" % (2 + 4 * l))],
                writes=[("Wdn", m)])
        DMA(P, "sp", gpost[:], g.w["norm_ffn_post"][l].partition_broadcast(128), writes=["gpost2"], slow=True)
        DMA(P, "sp", gpre[:], g.w["norm_ffn_pre"][l].rearrange("(c p) -> p c", p=128), writes=["gpre2"], slow=True)
        for k in range(3):
            DMA(P, "sp", fcw[:, :, k:k + 1], g.w["ffn_conv"][l, k].rearrange("(m p) -> p m", p=128).unsqueeze(2),
                writes=["fcw"], slow=True)
        for i in range(2):
            A(P, "pool", lambda e, i=i: e.memset(gbuf[i][:], 0.0), writes=["gbuf%d" % i])
        xcnt = 0
        for (s, N, lo, hi) in ffn_windows(T):
            nsub = N // 128
            for i in range(nsub):
                t0 = s + i * 128
                xi = xcnt % 2; xcnt += 1
                xx = xt[xi]; xk = "x4t%d" % xi
                sq = ssq[xi]
                DMA(P, "sp", xx[:], g.sc["xb"][t0:t0 + 128, :], writes=[xk])
                rstd, rk = rowsum_rstd(g, P, [xx[:]], [xk], sq, EPS, "ssq4_%d" % xi)
                A(P, "dve", lambda e, xx=xx, rstd=rstd: e.tensor_scalar(out=xs[:], in0=xx[:], scalar1=rstd, scalar2=None,
                                                                       op0=ALU.mult), reads=[xk, rk], writes=["xs4"])
                for c in range(8):
                    A(P, "pe", lambda e, c=c: e.transpose(out=g.pst[:, c * 128:(c + 1) * 128], in_=xs[:, c * 128:(c + 1) * 128],
                                                          identity=g.ident_bf[:]), reads=["xs4", "ident_bf"], writes=["pst"])
                A(P, "dve", lambda e, i=i: e.tensor_tensor(
                    out=hT[:, :, i * 128:(i + 1) * 128], in0=g.pst[:].rearrange("p (c t) -> p c t", c=8),
                    in1=gpre[:].unsqueeze(2).to_broadcast([128, 8, 128]), op=ALU.mult),
                  reads=["pst", "gpre2"], writes=["h2T"])
            def tail(m):
                b2 = m % 2
                cc = cv[b2]; ck = "cv%d" % b2
                gg = ge[b2]; gek = "ge%d" % b2
                ll = linsb[b2]; lk = "linsb%d" % b2
                A(P, "act", lambda e, cc=cc, gg=gg, N=N: e.activation(out=gg[:, 0:N], in_=cc[:, 0:N], func=AF.Gelu),
                  reads=[ck], writes=[gek])
                A(P, "dve", lambda e, gg=gg, ll=ll, m=m, N=N: e.tensor_tensor(out=hid[:, m, 0:N], in0=ll[:, 0:N], in1=gg[:, 0:N],
                                                                             op=ALU.mult),
                  reads=[lk, gek], writes=[("hid", m)])
            for m in range(NM + 1):
                if m < NM:
                    b2 = m % 2
                    pg = g.ps[b2 * 2]; pgk = "ps%d" % (b2 * 2)
                    pl = g.ps[b2 * 2 + 1]; plk = "ps%d" % (b2 * 2 + 1)
                    for kc in range(8):
                        mm(P, pg[:, 0:N], Wup[:, kc, m * 128:(m + 1) * 128], hT[:, kc, 0:N], kc == 0, kc == 7,
                           [("Wup", kc), "h2T"], [pgk])
                    for kc in range(8):
                        mm(P, pl[:, 0:N], Wup[:, kc, DFF + m * 128:DFF + (m + 1) * 128], hT[:, kc, 0:N], kc == 0, kc == 7,
                           [("Wup", kc), "h2T"], [plk])
                    gb = gbuf[b2]; gk = "gbuf%d" % b2
                    cc = cv[b2]; ck = "cv%d" % b2
                    ll = linsb[b2]; lk = "linsb%d" % b2
                    A(P, "act", lambda e, gb=gb, pg=pg, N=N: e.copy(out=gb[:, 1:N + 1], in_=pg[:, 0:N]), reads=[pgk], writes=[gk])
                    A(P, "act", lambda e, ll=ll, pl=pl, N=N: e.copy(out=ll[:, 0:N], in_=pl[:, 0:N]), reads=[plk], writes=[lk])
                    if N < 512:
                        A(P, "pool", lambda e, gb=gb, N=N: e.memset(gb[:, N + 1:N + 2], 0.0), writes=[gk])
                    A(P, "dve", lambda e, gb=gb, cc=cc, m=m, N=N: e.tensor_scalar(
                        out=cc[:, 0:N], in0=gb[:, 0:N], scalar1=fcw[:, m, 0:1], scalar2=None, op0=ALU.mult),
                      reads=[gk, "fcw"], writes=[ck])
                    for k2 in (1, 2):
                        A(P, "dve", lambda e, gb=gb, cc=cc, m=m, k2=k2, N=N: e.scalar_tensor_tensor(
                            out=cc[:, 0:N], in0=gb[:, k2:N + k2], scalar=fcw[:, m, k2:k2 + 1], in1=cc[:, 0:N],
                            op0=ALU.mult, op1=ALU.add), reads=[gk, "fcw", ck], writes=[ck])
                if m >= 1:
                    tail(m - 1)
            for i in range(nsub):
                t0 = s + i * 128
                a = max(lo, t0) - t0; b = min(hi, t0 + 128) - t0
                if b <= a:
                    continue
                xi = xcnt % 2; xcnt += 1
                xx = xt[xi]; xk = "x4t%d" % xi
                sq = ssq[xi]
                DMA(P, "sp", xx[:], g.sc["xb"][t0:t0 + 128, :], writes=[xk])
                pa = g.ps[4]; pak = "ps4"; pb = g.ps[5]; pbk = "ps5"
                for half, (pp, ppk) in enumerate(((pa, pak), (pb, pbk))):
                    for m in range(NM):
                        mm(P, pp[:, :], hid[:, m, i * 128:(i + 1) * 128], Wdn[:, m, half * 512:(half + 1) * 512],
                           m == 0, m == NM - 1, [("hid", m), ("Wdn", m)], [ppk])
                rstd, rk = rowsum_rstd(g, P, [pa[:, :], pb[:, :]], [pak, pbk], sq, EPS, "ssq4_%d" % xi)
                for half, (pp, ppk) in enumerate(((pa, pak), (pb, pbk))):
                    hs = slice(half * 512, (half + 1) * 512)
                    A(P, "dve", lambda e, pp=pp, hs=hs, rstd=rstd: e.scalar_tensor_tensor(
                        out=tmp[:, hs], in0=pp[:, :], scalar=rstd, in1=gpost[:, hs], op0=ALU.mult, op1=ALU.mult),
                      reads=[ppk, rk, "gpost2"], writes=["tmp4"])
                A(P, "pool", lambda e, xx=xx: e.tensor_tensor(out=tmp[:], in0=tmp[:], in1=xx[:], op=ALU.add),
                  reads=["tmp4", xk], writes=["tmp4"])
                DMA(P, "sp", xout[t0 + a:t0 + b, :], tmp[a:b, :], reads=["tmp4"])
        P.flush()


def rwkv_sweep_bd(g, l, d, hp, sid, st, C_):
    nc, P, T = g.nc, g.P, g.T
    sbq = mk_sb(g, st)
    K = lambda n: "%s_s%d" % (n, sid)
    t_ = {}
    for n in ("rB", "kB", "s", "a", "kkr", "kk", "t1", "ke", "bb", "cs", "cw", "u1", "E"):
        t_[n] = sbq(K(n), [128, NB], F32)
    nck = NB // 64
    loB = sbq(K("loB"), [96, NB], F32)
    AR = sbq(K("AR"), [128, nck, 2, 2, 64], BF16)
    BDf = {n: sbq(K("BD" + n), [128, nck, 2, 64], BF16) for n in ("bt", "kt", "Bh", "Kh")}
    rtp = sbq(K("rtp"), [128, NB], BF16)
    Vst = sbq(K("Vst"), [128, nck, 64], BF16)
    BDV = sbq(K("BDV"), [128, nck, 2, 64], BF16)
    Wt = sbq(K("Wt"), [128, nck], F32)
    LM = sbq(K("LM"), [128, 4, 128], BF16)
    P0 = sbq(K("P0"), [128, 128], BF16)
    QT = sbq(K("QT"), [128, 128], BF16)
    Mst = sbq(K("Mst"), [128, 2, 64], BF16)
    X = [sbq(K("X%d" % i), [128, 128], BF16) for i in range(2)]
    PP = sbq(K("PP"), [128, 2, 128], BF16)
    BKt = sbq(K("BKt"), [128, 2, 128], BF16)
    AU = sbq(K("AU"), [128, 2, 2, 64], BF16)
    BDGT = sbq(K("BDGT"), [128, 128], BF16)
    BDD = sbq(K("BDD"), [128, 2, 64], BF16)
    RhT = sbq(K("RhT"), [128, 64], BF16)
    BDH = [sbq(K("BDH%d" % i), [128, 128], BF16) for i in range(2)]
    B = g.ps[sid]; BKEY = "ps%d" % sid
    PB = g.ps[4 + (sid % 2)]; PBK = "ps%d" % (4 + (sid % 2))
    Wl, w0c, a0c, kkc, kac, omk, yacc = (C_[n] for n in ("Wl", "w0c", "a0c", "kkc", "kac", "omk", "yacc"))
    hs = slice(hp * 128, (hp + 1) * 128)
    idb = g.ident_bf
    I128f = g.tmp_f
    A(P, "pool", lambda e: e.memset(BDH[0][:], 0.0), writes=[K("BDH0")])
    A(P, "pool", lambda e: e.memset(BDV[:], 0.0), writes=[K("BDV")])
    hcur = 0
    nblk = T // NB
    blocks = list(range(nblk)) if d == 0 else list(range(nblk - 1, -1, -1))

    def v3(ap):
        return ap.rearrange("p (c t) -> p c t", t=64)

    def bd_embed(out4, src, E, neg, rkeys, wkey):
        for h2 in range(2):
            sc = (g.hmn if neg else g.hm)[:, h2:h2 + 1]
            A(P, "dve", lambda e, h2=h2, sc=sc: e.scalar_tensor_tensor(
                out=out4[:, :, h2, :], in0=v3(src[:]), scalar=sc, in1=v3(E[:]), op0=ALU.mult, op1=ALU.mult),
              reads=rkeys + ["hm", "hmn"], writes=[wkey])

    for bi in blocks:
        bs = slice(bi * NB, (bi + 1) * NB)
        DMA(P, "sp", t_["rB"][:], g.sc["rT"][hs, bs], writes=[K("rB")])
        DMA(P, "sp", t_["kB"][:], g.sc["kT"][hs, bs], writes=[K("kB")])
        DMA(P, "sp", loB[:], g.sc["loraT"][:, bs], writes=[K("loB")])
        for h2 in range(2):
            src = g.sc["vtok"][bs, hp * 128 + h2 * 64:hp * 128 + (h2 + 1) * 64].rearrange("(n s) f -> s n f", s=64)
            DMA(P, "sp", Vst[h2 * 64:(h2 + 1) * 64, :, :], src, writes=[K("Vst")])
            DMA(P, "sp", BDV[h2 * 64:(h2 + 1) * 64, :, h2, :], src, writes=[K("BDV")])
        mm(P, PB[:, 0:NB], Wl[:, d, hs], loB[:], True, True, ["Wl", K("loB")], [PBK])
        mm(P, PB[:, NB:2 * NB], Wl[:, 2 + d, hs], loB[:], True, True, ["Wl", K("loB")], [PBK])
        A(P, "act", lambda e: e.activation(out=t_["s"][:], in_=PB[:, 0:NB], func=AF.Sigmoid, bias=w0c[:, d, hp:hp + 1]),
          reads=[PBK, "cst"], writes=[K("s")])
        A(P, "act", lambda e: e.activation(out=t_["a"][:], in_=PB[:, NB:2 * NB], func=AF.Sigmoid, bias=a0c[:, d, hp:hp + 1]),
          reads=[PBK, "cst"], writes=[K("a")])
        A(P, "dve", lambda e: e.tensor_scalar(out=t_["kkr"][:], in0=t_["kB"][:], scalar1=kkc[:, hp:hp + 1], scalar2=None,
                                              op0=ALU.mult), reads=[K("kB"), "cst"], writes=[K("kkr")])
        A(P, "act", lambda e: e.activation(out=t_["u1"][:], in_=t_["kkr"][:], func=AF.Square), reads=[K("kkr")], writes=[K("u1")])
        mm(P, PB[:, 0:NB], g.blk1[:], t_["u1"][:], True, True, ["blk1", K("u1")], [PBK])
        A(P, "act", lambda e: e.activation(out=t_["u1"][:], in_=PB[:, 0:NB], func=AF.Sqrt, scale=1.0, bias=1e-12),
          reads=[PBK], writes=[K("u1")])
        A(P, "dve", lambda e: e.reciprocal(out=t_["u1"][:], in_=t_["u1"][:]), reads=[K("u1")], writes=[K("u1")])
        A(P, "dve", lambda e: e.tensor_tensor(out=t_["kk"][:], in0=t_["kkr"][:], in1=t_["u1"][:], op=ALU.mult),
          reads=[K("kkr"), K("u1")], writes=[K("kk")])
        A(P, "dve", lambda e: e.tensor_scalar(out=t_["t1"][:], in0=t_["a"][:], scalar1=kac[:, hp:hp + 1],
                                              scalar2=omk[:, hp:hp + 1], op0=ALU.mult, op1=ALU.add),
          reads=[K("a"), "cst", "omk"], writes=[K("t1")])
        A(P, "pool", lambda e: e.tensor_tensor(out=t_["ke"][:], in0=t_["kB"][:], in1=t_["t1"][:], op=ALU.mult),
          reads=[K("kB"), K("t1")], writes=[K("ke")])
        A(P, "pool", lambda e: e.tensor_tensor(out=t_["bb"][:], in0=t_["kk"][:], in1=t_["a"][:], op=ALU.mult),
          reads=[K("kk"), K("a")], writes=[K("bb")])
        for c in range(nck):
            A(P, "dve", lambda e, c=c: e.tensor_tensor_scan(out=t_["cs"][:, c * 64:(c + 1) * 64], data0=g.ones_f[:, 0:64],
                                                           data1=t_["s"][:, c * 64:(c + 1) * 64], initial=0.0,
                                                           op0=ALU.mult, op1=ALU.add),
              reads=[K("s"), "ones_f"], writes=[K("cs")])
        totb = v3(t_["cs"][:])[:, :, 63:64].to_broadcast([128, nck, 64])
        if d == 0:
            cw = t_["cs"]; cwk = K("cs")
        else:
            cw = t_["cw"]; cwk = K("cw")
            A(P, "pool", lambda e: e.tensor_tensor(out=t_["u1"][:], in0=t_["s"][:], in1=t_["cs"][:], op=ALU.subtract),
              reads=[K("s"), K("cs")], writes=[K("u1")])
            A(P, "dve", lambda e: e.tensor_tensor(out=v3(cw[:]), in0=v3(t_["u1"][:]), in1=totb, op=ALU.add),
              reads=[K("u1"), K("cs")], writes=[cwk])
        A(P, "act", lambda e: e.activation(out=t_["E"][:], in_=cw[:], func=AF.Exp, scale=-DSC), reads=[cwk], writes=[K("E")])
        A(P, "pool", lambda e: e.tensor_tensor(out=rtp[:], in0=t_["rB"][:], in1=t_["E"][:], op=ALU.mult),
          reads=[K("rB"), K("E")], writes=[K("rtp")])
        bd_embed(AR[:, :, 1, :, :], t_["rB"], t_["E"], False, [K("rB"), K("E")], K("AR"))
        A(P, "pool", lambda e: e.tensor_tensor(out=t_["u1"][:], in0=cw[:], in1=t_["s"][:], op=ALU.subtract),
          reads=[cwk, K("s")], writes=[K("u1")])
        A(P, "act", lambda e: e.activation(out=t_["E"][:], in_=t_["u1"][:], func=AF.Exp, scale=-DSC), reads=[K("u1")], writes=[K("E")])
        bd_embed(AR[:, :, 0, :, :], t_["kk"], t_["E"], True, [K("kk"), K("E")], K("AR"))
        A(P, "act", lambda e: e.activation(out=t_["E"][:], in_=cw[:], func=AF.Exp, scale=DSC), reads=[cwk], writes=[K("E")])
        bd_embed(BDf["bt"][:], t_["bb"], t_["E"], False, [K("bb"), K("E")], K("BDbt"))
        bd_embed(BDf["kt"][:], t_["ke"], t_["E"], False, [K("ke"), K("E")], K("BDkt"))
        if d == 0:
            totc = v3(t_["cs"][:])[:, :, 63:64]
        else:
            totc = v3(cw[:])[:, :, 0:1]
        A(P, "dve", lambda e: e.tensor_tensor(out=v3(t_["u1"][:]), in0=totc.to_broadcast([128, nck, 64]), in1=v3(cw[:]),
                                              op=ALU.subtract), reads=[cwk, K("cs")], writes=[K("u1")])
        A(P, "act", lambda e: e.activation(out=t_["E"][:], in_=t_["u1"][:], func=AF.Exp, scale=-DSC), reads=[K("u1")], writes=[K("E")])
        bd_embed(BDf["Bh"][:], t_["bb"], t_["E"], False, [K("bb"), K("E")], K("BDBh"))
        bd_embed(BDf["Kh"][:], t_["ke"], t_["E"], False, [K("ke"), K("E")], K("BDKh"))
        A(P, "act", lambda e: e.activation(out=Wt[:].unsqueeze(2), in_=totc, func=AF.Exp, scale=-DSC),
          reads=[cwk, K("cs")], writes=[K("Wt")])
        yield
        chunks = list(range(nck)) if d == 0 else list(range(nck - 1, -1, -1))
        for ci in chunks:
            cs_ = slice(ci * 64, (ci + 1) * 64)
            tok = slice(bi * NB + ci * 64, bi * NB + (ci + 1) * 64)
            f2 = lambda ap: ap.rearrange("p a t -> p (a t)")
            bdbt = f2(BDf["bt"][:, ci]); bdkt = f2(BDf["kt"][:, ci]); bdBh = f2(BDf["Bh"][:, ci]); bdKh = f2(BDf["Kh"][:, ci])
            bdat = f2(AR[:, ci, 0]); arr = AR[:, ci].rearrange("p k a t -> p (k a t)")
            mm(P, B[:, 0:256], bdbt, arr, True, True, [K("BDbt"), K("AR")], [BKEY])
            mm(P, B[:, 256:512], bdkt, arr, True, True, [K("BDkt"), K("AR")], [BKEY])
            A(P, "dve", lambda e: e.tensor_tensor(out=LM[:], in0=B[:, 0:512].rearrange("p (m t) -> p m t", m=4),
                                                  in1=g.mask4[d][:], op=ALU.mult), reads=[BKEY, "mskbd"], writes=[K("LM")])
            A(P, "pool", lambda e: e.tensor_tensor(out=QT[:], in0=LM[:, 0, :], in1=I128f[:], op=ALU.add),
              reads=[K("LM"), "tmp_f"], writes=[K("QT")])
            A(P, "pool", lambda e: e.tensor_tensor(out=Mst[:], in0=LM[:, 1::2, 0:64], in1=LM[:, 1::2, 64:128], op=ALU.add),
              reads=[K("LM")], writes=[K("Mst")])
            yield
            mm(P, B[:, 0:128], bdat, bdbt, True, True, [K("BDbt"), K("AR")], [BKEY])
            mm(P, B[:, 128:192], bdat, g.idst_bf[:], True, True, [K("AR"), "idst_bf"], [BKEY])
            mm(P, B[:, 192:256], LM[:, 2, :], Vst[:, ci, :], True, True, [K("LM"), K("Vst")], [BKEY])
            mm(P, B[:, 256:384], bdBh, idb[:], True, True, [K("BDBh"), "ident_bf"], [BKEY])
            mm(P, B[:, 384:512], bdKh, idb[:], True, True, [K("BDKh"), "ident_bf"], [BKEY])
            A(P, "dve", lambda e: e.tensor_tensor(out=P0[:], in0=B[:, 0:128], in1=g.maskT[d][:], op=ALU.mult),
              reads=[BKEY, "mskbd"], writes=[K("P0")])
            A(P, "act", lambda e: e.copy(out=X[0][:], in_=B[:, 128:256]), reads=[BKEY], writes=[K("X0")])
            A(P, "act", lambda e: e.copy(out=BKt[:], in_=B[:, 256:512].rearrange("p (m t) -> p m t", m=2)),
              reads=[BKEY], writes=[K("BKt")])
            yield
            xi = 0
            for lev in range(6):
                Xc = X[xi]; Xn = X[1 - xi]; xck = K("X%d" % xi); xnk = K("X%d" % (1 - xi))
                mm(P, B[:, 0:128], QT[:], Xc[:], True, True, [K("QT"), xck], [BKEY])
                if lev < 5:
                    if lev == 0:
                        Pm = P0[:]; PTm = LM[:, 0, :]; pk = [K("P0"), K("LM")]
                    else:
                        Pm = PP[:, 0, :]; PTm = PP[:, 1, :]; pk = [K("PP")]
                    mm(P, B[:, 128:256], PTm, Pm, True, True, pk, [BKEY])
                    mm(P, B[:, 256:384], Pm, PTm, True, True, pk, [BKEY])
                A(P, "act", lambda e, Xn=Xn: e.copy(out=Xn[:], in_=B[:, 0:128]), reads=[BKEY], writes=[xnk])
                if lev < 5:
                    A(P, "act", lambda e: e.copy(out=PP[:], in_=B[:, 128:384].rearrange("p (m t) -> p m t", m=2)),
                      reads=[BKEY], writes=[K("PP")])
                    A(P, "dve", lambda e: e.tensor_tensor(out=QT[:], in0=B[:, 256:384], in1=I128f[:], op=ALU.add),
                      reads=[BKEY, "tmp_f"], writes=[K("QT")])
                else:
                    for kind in range(2):
                        for h2 in range(2):
                            A(P, "dve", lambda e, kind=kind, h2=h2: e.tensor_scalar(
                                out=AU[:, kind, h2, :], in0=B[:, kind * 64:(kind + 1) * 64], scalar1=g.hm[:, h2:h2 + 1],
                                scalar2=None, op0=ALU.mult), reads=[BKEY, "hm"], writes=[K("AU")])
                xi = 1 - xi
                yield
            Xf = X[xi]; xfk = K("X%d" % xi)
            bdA = f2(AU[:, 0]); bdU = f2(AU[:, 1])
            mm(P, B[:, 0:128], bdA, BKt[:, 0, :], True, True, [K("AU"), K("BKt")], [BKEY])
            mm(P, B[:, 128:192], BKt[:, 0, :], Xf[:, 64:128], True, False, [K("BKt"), xfk], [BKEY])
            mm(P, B[:, 128:192], BKt[:, 1, :], Vst[:, ci, :], False, True, [K("BKt"), K("Vst")], [BKEY])
            mm(P, B[:, 192:256], idb[:], rtp[:, cs_], True, False, ["ident_bf", K("rtp")], [BKEY])
            mm(P, B[:, 192:256], bdA, Mst[:, 0, :], False, True, [K("AU"), K("Mst")], [BKEY])
            A(P, "dve", lambda e, ci=ci: e.scalar_tensor_tensor(out=BDGT[:], in0=I128f[:], scalar=Wt[:, ci:ci + 1], in1=B[:, 0:128],
                                                               op0=ALU.mult, op1=ALU.add),
              reads=[BKEY, "tmp_f", K("Wt")], writes=[K("BDGT")])
            for h2 in range(2):
                A(P, "dve", lambda e, h2=h2: e.tensor_scalar(out=BDD[:, h2, :], in0=B[:, 128:192], scalar1=g.hm[:, h2:h2 + 1],
                                                            scalar2=None, op0=ALU.mult), reads=[BKEY, "hm"], writes=[K("BDD")])
            A(P, "dve", lambda e: e.tensor_copy(out=RhT[:], in_=B[:, 192:256]), reads=[BKEY], writes=[K("RhT")])
            yield
            Hc = BDH[hcur]; Hn = BDH[1 - hcur]; hck = K("BDH%d" % hcur); hnk = K("BDH%d" % (1 - hcur))
            mm(P, B[:, 384:448], Hc[:], RhT[:], True, False, [hck, K("RhT")], [BKEY])
            mm(P, B[:, 384:448], bdU, Mst[:, 0, :], False, False, [K("AU"), K("Mst")], [BKEY])
            mm(P, B[:, 384:448], f2(BDV[:, ci]), Mst[:, 1, :], False, True, [K("BDV"), K("Mst")], [BKEY])
            mm(P, B[:, 256:384], BDGT[:], Hc[:], True, False, [K("BDGT"), hck], [BKEY])
            mm(P, B[:, 256:384], idb[:], f2(BDD[:]), False, True, ["ident_bf", K("BDD")], [BKEY])
            A(P, "dve", lambda e, tok=tok: e.tensor_tensor(out=yacc[:, hp, tok], in0=B[:, 384:448], in1=yacc[:, hp, tok], op=ALU.add),
              reads=[BKEY, ("yacc", hp)], writes=[("yacc", hp)])
            A(P, "dve", lambda e, Hn=Hn: e.tensor_copy(out=Hn[:], in_=B[:, 256:384]), reads=[BKEY], writes=[hnk])
            hcur = 1 - hcur
            yield


_NC_CACHE = {}


def kernel(**inputs):
    from concourse.bass_utils import run_bass_kernel_spmd
    x = np.ascontiguousarray(np.asarray(inputs["x"], dtype=np.float32))
    Bn, T, _ = x.shape
    if T not in _NC_CACHE:
        _NC_CACHE[T] = build(T, nlayers=2)
    nc = _NC_CACHE[T]
    in_maps = []
    for b in range(Bn):
        m = {"x": np.ascontiguousarray(x[b])}
        for n, s in PARAMS:
            m[n] = np.ascontiguousarray(np.asarray(inputs[n], dtype=np.float32))
        in_maps.append(m)
    res = run_bass_kernel_spmd(nc, in_maps, core_ids=list(range(Bn)))
    return np.stack([np.asarray(r["y"], dtype=np.float32) for r in res.results], axis=0)
```

```python
import contextlib
import numpy as np
import concourse.bass as bass
import concourse.mybir as mybir

F32 = mybir.dt.float32
BF16 = mybir.dt.bfloat16
AF = mybir.ActivationFunctionType
ALU = mybir.AluOpType
EPOCH = 30000
NDMA_SEM = 10
ENGS = ("pe", "act", "dve", "pool", "sp")


class Op:
    __slots__ = ("eng", "fn", "reads", "writes", "is_dma", "idx", "waits", "signal", "clock",
                 "sem", "count", "dsem", "dcount")

    def __init__(self, eng, fn, reads, writes, is_dma):
        self.eng = eng; self.fn = fn; self.reads = reads; self.writes = writes
        self.is_dma = is_dma; self.waits = []; self.signal = False; self.clock = None
        self.sem = None; self.count = None; self.dsem = None; self.dcount = None


class Prog:
    def __init__(self, nc, stack):
        self.nc = nc
        self.stack = stack
        self.sems = {}
        self.ops = []
        self.last_writer = {}
        self.readers = {}
        self.known = {e: {} for e in ENGS}
        self.n_on = {e: 0 for e in ENGS}
        self.sig_cnt = {e: 0 for e in ENGS}
        self.dma_rr = {e: 0 for e in ENGS}
        self.dma_last = {}
        self.dma_cnt = {}
        self.total_ops = 0
        for e in ENGS:
            for ep in range(3):
                self._sem((e, ep))
        for q in ("sp", "pool", "act"):
            for slot in range(NDMA_SEM):
                self._sem(("d", q, slot))
        for i in range(8):
            self._sem(("d", "pool", "bg%d" % i))
        self.persist = {}
        self.bg_cnt = {}
        self.clear_all()
        nc.all_engine_barrier()

    def add_bg(self, group, fn, reads=()):
        op = Op("pool", fn, tuple(reads), (), True)
        op.idx = self.n_on["pool"]; self.n_on["pool"] += 1
        op.dsem = group
        op.dcount = self.bg_cnt.get(group, 0)
        self.bg_cnt[group] = op.dcount + 1
        op.clock = {("d", "pool", group): op.dcount}
        self.persist[("bg", group)] = op
        self.ops.append(op)
        return op

    def clear_all(self):
        for h in self.sems.values():
            self.nc.gpsimd.sem_clear(h)

    def finish(self):
        self.flush()
        nc = self.nc
        for grp, cnt in self.bg_cnt.items():
            for eng in (nc.tensor, nc.scalar, nc.vector, nc.gpsimd, nc.sync):
                eng.wait_ge(self.sems[("d", "pool", grp)], 16 * cnt)
        self.nc.all_engine_barrier()
        self.clear_all()
        self.nc.all_engine_barrier()

    def _sem(self, key):
        if key not in self.sems:
            assert not getattr(self, "_frozen", False), key
            self.sems[key] = self.stack.enter_context(
                self.nc.semaphore("s_" + "_".join(str(x) for x in key)))
        return self.sems[key]

    def _need(self, op, dep, same_ok):
        if dep is None:
            return
        kn = self.known[op.eng]
        if dep.is_dma:
            key = ("d", dep.eng, dep.dsem)
            if kn.get(key, -1) >= dep.dcount:
                return
        else:
            if dep.eng == op.eng and not same_ok:
                return
            key = dep.eng
            if kn.get(key, -1) >= dep.idx:
                return
        dep.signal = True
        op.waits.append(dep)
        for k, v in dep.clock.items():
            if kn.get(k, -1) < v:
                kn[k] = v

    def add(self, eng, fn, reads=(), writes=(), dma=False):
        op = Op(eng, fn, tuple(reads), tuple(writes), dma)
        op.idx = self.n_on[eng]; self.n_on[eng] += 1
        kn = self.known[eng]
        if dma:
            slot = self.dma_rr[eng] % NDMA_SEM; self.dma_rr[eng] += 1
            op.dsem = slot
            prev = self.dma_last.get((eng, slot))
            op.dcount = self.dma_cnt.get((eng, slot), 0)
            self.dma_cnt[(eng, slot)] = op.dcount + 1
            if prev is not None:
                self._need(op, prev, True)
            self.dma_last[(eng, slot)] = op
        for r in op.reads:
            w = self.last_writer.get(r)
            if w is None:
                w = self.persist.get(r)
            if w is not None:
                self._need(op, w, True)
            if isinstance(r, str) and r.startswith("ps"):
                for rd in self.readers.get(r, ()):
                    if rd.eng != eng:
                        self._need(op, rd, True)
        strict = (eng != "pe")
        for wkey in op.writes:
            w = self.last_writer.get(wkey)
            if w is not None:
                self._need(op, w, w.is_dma or dma or strict)
            for rd in self.readers.get(wkey, ()):
                self._need(op, rd, rd.is_dma or dma or strict)
        for r in op.reads:
            self.readers.setdefault(r, []).append(op)
        for wkey in op.writes:
            self.last_writer[wkey] = op
            self.readers[wkey] = []
        ck = dict(kn)
        if dma:
            ck[("d", eng, op.dsem)] = op.dcount
        else:
            ck[eng] = op.idx
        op.clock = ck
        self.ops.append(op)
        return op

    def flush(self):
        nc = self.nc
        if not self.ops:
            return
        self.total_ops += len(self.ops)
        per = {e: [o for o in self.ops if o.eng == e] for e in ENGS}
        lastc = {}
        for e in ENGS:
            comp = [o for o in per[e] if not o.is_dma]
            if comp:
                comp[-1].signal = True
                lastc[e] = comp[-1]
        for e in ENGS:
            c = self.sig_cnt[e]
            for o in per[e]:
                if o.is_dma:
                    continue
                if o.signal:
                    c += 1
                    o.sem = (e, (c - 1) // EPOCH); o.count = (c - 1) % EPOCH + 1
            self.sig_cnt[e] = c
        dma_final = dict(self.dma_last)
        import os
        if os.environ.get("FW_DEBUG"):
            print("FLUSH sig_cnt", self.sig_cnt, "max dma cnt", max([16 * (v.dcount + 1) for v in dma_final.values()] or [0]), "nops", len(self.ops))
        active = list(ENGS)

        def run(e, eng):
            for o in per[e]:
                for d in o.waits:
                    if d.is_dma:
                        eng.wait_ge(self._sem(("d", d.eng, d.dsem)), 16 * (d.dcount + 1))
                    else:
                        eng.wait_ge(self._sem(d.sem), d.count)
                inst = o.fn(eng)
                if o.is_dma:
                    inst.then_inc(self._sem(("d", o.eng, o.dsem)), 16)
                elif o.signal:
                    inst.then_inc(self._sem(o.sem), 1)
            for (q, slot), last in dma_final.items():
                eng.wait_ge(self._sem(("d", q, slot)), 16 * (last.dcount + 1))
            for e2, lo in lastc.items():
                if e2 != e:
                    eng.wait_ge(self._sem(lo.sem), lo.count)

        with nc.Block() as block:
            dec = {"pe": block.tensor, "act": block.scalar, "dve": block.vector,
                   "pool": block.gpsimd, "sp": block.sync}
            for e in active:
                def body(eng, e=e):
                    run(e, eng)
                dec[e](body)
        self.ops = []
        self.last_writer = {}
        self.readers = {}
        for e in ENGS:
            kn = self.known[e]
            for e2 in ENGS:
                kn[e2] = self.n_on[e2] - 1
            for (q, slot), last in dma_final.items():
                kn[("d", q, slot)] = last.dcount


import contextlib
import numpy as np

D = 1024
DIN = 2912
DFF = 2816
EPS = 1e-6

PARAMS = [("norm_mix_pre", (2, 1024)), ("norm_mix_post", (2, 1024)), ("norm_ffn_pre", (2, 1024)),
          ("norm_ffn_post", (2, 1024)), ("w_in", (2, 1024, 2912)), ("conv_a_w", (2, 3, 256)),
          ("rwkv_w0", (2, 2, 256)), ("rwkv_w_up", (2, 2, 16, 256)), ("rwkv_a0", (2, 2, 256)),
          ("rwkv_a_up", (2, 2, 16, 256)), ("rwkv_g_up", (2, 32, 256)), ("rwkv_k_k", (2, 256)),
          ("rwkv_k_a", (2, 256)), ("rwkv_r_k", (2, 4, 64)), ("rwkv_lnx_w", (2, 256)),
          ("rwkv_lnx_b", (2, 256)), ("na_rpb", (2, 4, 15, 31)), ("sgu_norm", (2, 256)),
          ("sgu_w", (2, 4, 128, 128)), ("sgu_b", (2, 4, 128)), ("merge_gain", (2, 1024)),
          ("w_out", (2, 1024, 1024)), ("ffn_w_up", (2, 1024, 5632)), ("ffn_conv", (2, 3, 2816)),
          ("ffn_w_down", (2, 2816, 1024))]


class Ctx:
    pass


def mk_sb(g, st):
    def sb(name, shape, dt):
        g.uid = getattr(g, "uid", 0) + 1
        return st.enter_context(g.nc.sbuf_tensor("%s_%d" % (name, g.uid), shape, dt))
    return sb


def build(T, nlayers=2, dbg=(), phases=('conv','sgu','na','rwkv','p3')):
    nc = bass.Bass("TRN2", target_bir_lowering=False)
    g = Ctx()
    g.nc = nc; g.T = T; g.dbgnames = dbg; g.phases = phases
    g.x = nc.dram_tensor("x", [T, D], F32, kind="ExternalInput").ap()
    g.w = {n: nc.dram_tensor(n, list(s), F32, kind="ExternalInput").ap() for n, s in PARAMS}
    g.y = nc.dram_tensor("y", [T, D], F32, kind="ExternalOutput").ap()
    def scratch(name, shape, dt):
        kind = "ExternalOutput" if name in dbg else "Internal"
        return nc.dram_tensor(name, shape, dt, kind=kind).ap()
    g.sc = dict(
        pT=scratch("pT", [256, T], F32), cbT=scratch("cbT", [256, T], F32),
        rT=scratch("rT", [256, T], F32), kT=scratch("kT", [256, T], F32), vT=scratch("vT", [256, T], F32),
        loraT=scratch("loraT", [96, T], F32), vtok=scratch("vtok", [T, 256], BF16),
        qnT=scratch("qnT", [256, T], BF16), knT=scratch("knT", [256, T], BF16),
        vntok=scratch("vntok", [T, 256], BF16),
        usT=scratch("usT", [256, T], F32), vstok=scratch("vstok", [T, 256], F32),
        ymT=scratch("ymT", [1024, T], BF16),
        xa=scratch("xa", [T, D], F32), xb=scratch("xb", [T, D], F32),
        wup_bf=scratch("wup_bf", [2, D, 2 * DFF], BF16), wdn_bf=scratch("wdn_bf", [2, DFF, D], BF16),
        wout_bf=scratch("wout_bf", [2, D, D], BF16), win_bf=scratch("win_bf", [2, D, DIN], BF16),
    )
    with contextlib.ExitStack() as st:
        P = Prog(nc, st)
        g.P = P
        g.ident_bf = st.enter_context(nc.sbuf_tensor("ident_bf", [128, 128], BF16))
        g.ones_f = st.enter_context(nc.sbuf_tensor("ones_f", [128, 128], F32))
        g.tmp_f = st.enter_context(nc.sbuf_tensor("tmp_f", [128, 128], F32))
        g.ps = [st.enter_context(nc.psum_tensor("ps%d" % i, [128, 512], F32)) for i in range(7)]
        g.pst = st.enter_context(nc.psum_tensor("pst", [128, 1024], BF16))
        P.add("pool", lambda e: e.memset(g.ones_f[:], 1.0), writes=["ones_f"])
        P.add("pool", lambda e: e.affine_select(out=g.tmp_f[:], in_=g.ones_f[:], pattern=[[1, 128]],
                                                compare_op=ALU.is_equal, fill=0.0, base=0,
                                                channel_multiplier=-1), reads=["ones_f"], writes=["tmp_f"])
        P.add("dve", lambda e: e.tensor_copy(out=g.ident_bf[:], in_=g.tmp_f[:]), reads=["tmp_f"], writes=["ident_bf"])
        P.flush()
        xin = g.x
        g.mg = st.enter_context(nc.sbuf_tensor("mg", [128, 8], F32))
        if "rwkv" in g.phases: rwkv_consts(g, st)
        for l in range(nlayers):
            load_layer_consts(g, l)
            phase1(g, l, xin)
            if l == 0:
                def bgconv(group, dst, src, rows, step):
                    for r0 in range(0, rows, step):
                        P.add_bg(group, lambda e, r0=r0, dst=dst, src=src, step=step: e.dma_start(out=dst[r0:r0 + step, :], in_=src[r0:r0 + step, :]))
                for ll in range(nlayers):
                    bgconv("bg%d" % (0 + 4 * ll), g.sc["wout_bf"][ll], g.w["w_out"][ll], D, 256)
                    bgconv("bg%d" % (1 + 4 * ll), g.sc["wup_bf"][ll], g.w["ffn_w_up"][ll], D, 128)
                    bgconv("bg%d" % (2 + 4 * ll), g.sc["wdn_bf"][ll], g.w["ffn_w_down"][ll], DFF, 256)
                    if ll > 0:
                        bgconv("bg%d" % (3 + 4 * ll), g.sc["win_bf"][ll], g.w["w_in"][ll], D, 128)
            if "conv" in g.phases: phase_conv(g, l)
            if "sgu" in g.phases: phase_sgu(g, l)
            if "na" in g.phases: phase_na(g, l)
            if "rwkv" in g.phases: phase_rwkv(g, l)
            if "p3" in g.phases:
                phase3a(g, l, xin)
                phase3b(g, l, g.y if l == nlayers - 1 else g.sc["xa"])
            P.flush()
            xin = g.sc["xa"]
        P.finish()
    return nc


FM_CHUNKS = (
    ("ch0", 0, 128, "save_ch", None, 0), ("cc0", 512, 128, "mul_ch", "pT", 0),
    ("ch1", 128, 128, "save_ch", None, 0), ("cc1", 640, 128, "mul_ch", "pT", 128),
    ("cb0", 256, 128, "copy", "cbT", 0), ("cb1", 384, 128, "copy", "cbT", 128),
    ("r0", 768, 128, "copy", "rT", 0), ("r1", 896, 128, "copy", "rT", 128),
    ("k0", 1024, 128, "copy", "kT", 0), ("k1", 1152, 128, "copy", "kT", 128),
    ("v0", 1280, 128, "copy", "vT", 0), ("v1", 1408, 128, "copy", "vT", 128),
    ("lora", 1536, 96, "lora", "loraT", 0),
    ("qn0", 1632, 128, "q", "qnT", 0), ("qn1", 1760, 128, "q", "qnT", 128),
    ("kn0", 1888, 128, "copybf", "knT", 0), ("kn1", 2016, 128, "copybf", "knT", 128),
    ("us0", 2400, 128, "gelu", "usT", 0), ("us1", 2528, 128, "gelu", "usT", 128),
)
TM_GROUPS = (("vtok", 1280, "copybf"), ("vntok", 2144, "copybf"), ("vstok", 2656, "gelu"))


def phase1(g, l, xin):
    nc, P, T = g.nc, g.P, g.T
    NW = T // 512
    with contextlib.ExitStack() as st:
        sb = mk_sb(g, st)
        Win = sb("Win", [128, 8, DIN], BF16)
        gpre = sb("gpre", [128, 8], F32)
        xt = [sb("xt%d" % i, [128, D], F32) for i in range(2)]
        junk = sb("junk", [128, D], BF16)
        xs = sb("xs", [128, D], BF16)
        ss = sb("ss", [128, 4], F32)
        hT = [sb("hT%d" % i, [128, 8, 512], BF16) for i in range(2)]
        stg = [sb("stg%d" % i, [128, 512], F32) for i in range(4)]
        stgb = [sb("stgb%d" % i, [128, 512], BF16) for i in range(2)]
        cht = sb("cht", [128, 512], F32)
        win_src = g.w["w_in"]
        for kc in range(8):
            if l == 0:
                def f(e, kc=kc):
                    return e.dma_start(out=Win[:, kc, :], in_=win_src[l, kc * 128:(kc + 1) * 128, :])
                P.add("pool", f, writes=[("Win", kc)], dma=True)
            else:
                DMA(P, "sp", Win[:, kc, :], g.sc["win_bf"][l, kc * 128:(kc + 1) * 128, :], reads=[("bg", "bg%d" % (3 + 4 * l))],
                    writes=[("Win", kc)])
        P.add("sp", lambda e: e.dma_start(out=gpre[:], in_=g.w["norm_mix_pre"][l].rearrange("(c p) -> p c", p=128),
                                          allow_slow_non_contiguous=True), writes=["gpre"], dma=True)
        cnt = dict(stg=0, stgb=0, ps=0, x=0)
        for w in range(NW):
            h = hT[w % 2]; hk = "hT%d" % (w % 2)
            for s in range(4):
                t0 = w * 512 + s * 128
                xi = cnt["x"] % 2; cnt["x"] += 1
                xtile = xt[xi]; xk = "xt%d" % xi
                P.add("sp", lambda e, xtile=xtile, t0=t0: e.dma_start(out=xtile[:], in_=xin[t0:t0 + 128, :]),
                      writes=[xk], dma=True)
                P.add("act", lambda e, xtile=xtile, s=s: e.activation(out=junk[:], in_=xtile[:], func=AF.Square,
                                                                     accum_out=ss[:, 0:1]),
                      reads=[xk], writes=["junk", "ss"])
                P.add("act", lambda e: e.activation(out=ss[:, 1:2], in_=ss[:, 0:1], func=AF.Sqrt,
                                                    scale=1.0 / D, bias=g_eps(g)),
                      reads=["ss"], writes=["ss1"])
                P.add("dve", lambda e: e.reciprocal(out=ss[:, 2:3], in_=ss[:, 1:2]), reads=["ss1"], writes=["ss2"])
                P.add("dve", lambda e, xtile=xtile: e.tensor_scalar(out=xs[:], in0=xtile[:], scalar1=ss[:, 2:3],
                                                                   scalar2=None, op0=ALU.mult),
                      reads=[xk, "ss2"], writes=["xs"])
                for c in range(8):
                    P.add("pe", lambda e, c=c: e.transpose(out=g.pst[:, c * 128:(c + 1) * 128],
                                                           in_=xs[:, c * 128:(c + 1) * 128], identity=g.ident_bf[:]),
                          reads=["xs", "ident_bf"], writes=["pst"])
                P.add("dve", lambda e, h=h, s=s: e.tensor_tensor(
                    out=h[:, :, s * 128:(s + 1) * 128], in0=g.pst[:].rearrange("p (c t) -> p c t", c=8),
                    in1=gpre[:].unsqueeze(2).to_broadcast([128, 8, 128]), op=ALU.mult),
                      reads=["pst", "gpre"], writes=[hk])
            if w == 0 and 'dbg_ss' in g.dbgnames:
                d1 = nc.dram_tensor("dbg_ss", [128, 4], F32, kind="ExternalOutput").ap()
                d2 = nc.dram_tensor("dbg_hT", [128, 8 * 512], BF16, kind="ExternalOutput").ap()
                d3 = nc.dram_tensor("dbg_id", [128, 128], BF16, kind="ExternalOutput").ap()
                d4 = nc.dram_tensor("dbg_xs", [128, 1024], BF16, kind="ExternalOutput").ap()
                P.add("sp", lambda e: e.dma_start(out=d1, in_=ss[:]), reads=["ss", "ss1", "ss2"], dma=True)
                P.add("sp", lambda e, h=h: e.dma_start(out=d2, in_=h[:].rearrange("p c t -> p (c t)")), reads=[hk], dma=True)
                P.add("sp", lambda e: e.dma_start(out=d3, in_=g.ident_bf[:]), reads=["ident_bf"], dma=True)
                P.add("sp", lambda e: e.dma_start(out=d4, in_=xs[:]), reads=["xs"], dma=True)
            tsl = slice(w * 512, (w + 1) * 512)
            for (name, c0, ncol, kind, dst, r0) in FM_CHUNKS:
                pi = cnt["ps"] % 7; cnt["ps"] += 1
                ps = g.ps[pi]; pk = "ps%d" % pi
                for kc in range(8):
                    P.add("pe", lambda e, ps=ps, kc=kc, c0=c0, ncol=ncol, h=h: e.matmul(
                        ps[0:ncol, :], Win[:, kc, c0:c0 + ncol], h[:, kc, :], start=(kc == 0), stop=(kc == 7)),
                          reads=[("Win", kc), hk], writes=[pk])
                if kind == "save_ch":
                    P.add("act", lambda e, ps=ps: e.copy(out=cht[:], in_=ps[:]), reads=[pk], writes=["cht"])
                    continue
                if kind in ("q", "copybf"):
                    si = cnt["stgb"] % 2; cnt["stgb"] += 1
                    so = stgb[si]; sk = "stgb%d" % si
                else:
                    si = cnt["stg"] % 4; cnt["stg"] += 1
                    so = stg[si]; sk = "stg%d" % si
                if kind == "mul_ch":
                    P.add("dve", lambda e, ps=ps, so=so: e.tensor_tensor(out=so[:], in0=ps[:], in1=cht[:], op=ALU.mult),
                          reads=[pk, "cht"], writes=[sk])
                elif kind == "copy":
                    P.add("dve", lambda e, ps=ps, so=so: e.tensor_copy(out=so[:], in_=ps[:]), reads=[pk], writes=[sk])
                elif kind == "copybf":
                    P.add("act", lambda e, ps=ps, so=so: e.copy(out=so[:], in_=ps[:]), reads=[pk], writes=[sk])
                elif kind == "q":
                    P.add("act", lambda e, ps=ps, so=so: e.mul(out=so[:], in_=ps[:], mul=0.125), reads=[pk], writes=[sk])
                elif kind == "gelu":
                    P.add("act", lambda e, ps=ps, so=so: e.activation(out=so[:], in_=ps[:], func=AF.Gelu),
                          reads=[pk], writes=[sk])
                elif kind == "lora":
                    P.add("act", lambda e, ps=ps, so=so: e.activation(out=so[0:32, :], in_=ps[0:32, :], func=AF.Tanh),
                          reads=[pk], writes=[sk])
                    P.add("act", lambda e, ps=ps, so=so: e.copy(out=so[32:64, :], in_=ps[32:64, :]),
                          reads=[pk], writes=[sk])
                    P.add("act", lambda e, ps=ps, so=so: e.activation(out=so[64:96, :], in_=ps[64:96, :], func=AF.Sigmoid),
                          reads=[pk], writes=[sk])
                P.add("pool", lambda e, so=so, dst=dst, r0=r0, ncol=ncol, tsl=tsl: e.dma_start(
                    out=g.sc[dst][r0:r0 + ncol, tsl], in_=so[0:ncol, :]), reads=[sk], dma=True)
            for s in range(4):
                t0 = w * 512 + s * 128
                for (dst, c0, kind) in TM_GROUPS:
                    pi = cnt["ps"] % 7; cnt["ps"] += 1
                    ps = g.ps[pi]; pk = "ps%d" % pi
                    for kc in range(8):
                        P.add("pe", lambda e, ps=ps, kc=kc, c0=c0, s=s, h=h: e.matmul(
                            ps[:, 0:256], h[:, kc, s * 128:(s + 1) * 128], Win[:, kc, c0:c0 + 256],
                            start=(kc == 0), stop=(kc == 7)), reads=[("Win", kc), hk], writes=[pk])
                    if kind == "gelu":
                        si = cnt["stg"] % 4; cnt["stg"] += 1
                        so = stg[si]; sk = "stg%d" % si
                        P.add("act", lambda e, ps=ps, so=so: e.activation(out=so[:, 0:256], in_=ps[:, 0:256], func=AF.Gelu),
                              reads=[pk], writes=[sk])
                    else:
                        si = cnt["stgb"] % 2; cnt["stgb"] += 1
                        so = stgb[si]; sk = "stgb%d" % si
                        P.add("dve", lambda e, ps=ps, so=so: e.tensor_copy(out=so[:, 0:256], in_=ps[:, 0:256]),
                              reads=[pk], writes=[sk])
                    P.add("pool", lambda e, so=so, dst=dst, t0=t0: e.dma_start(
                        out=g.sc[dst][t0:t0 + 128, :], in_=so[:, 0:256]), reads=[sk], dma=True)
        P.flush()


def g_eps(g):
    return EPS


def A(P, eng, fn, reads=(), writes=()):
    return P.add(eng, fn, reads, writes)


def DMA(P, q, out, in_, reads=(), writes=(), slow=False):
    if q == "sp" and not writes:
        q = "pool"
    if slow:
        return P.add(q, lambda e: e.dma_start(out=out, in_=in_, allow_slow_non_contiguous=True), reads, writes, dma=True)
    return P.add(q, lambda e: e.dma_start(out=out, in_=in_), reads, writes, dma=True)


def gnorm(g, l, grp, y0, y1, ykeys, t0, N, W):
    nc, P = g.nc, g.P
    ps = g.ps[6]; pk = "ps6"
    sq = W["gn_sq"]; rin = W["gn_rin"]
    for j, yj in enumerate((y0, y1)):
        P.add("act", lambda e, yj=yj, j=j: e.activation(out=sq[j][:, 0:N], in_=yj, func=AF.Square),
              reads=[ykeys[j]], writes=["gn_sq%d" % j])
    for j in range(2):
        P.add("pe", lambda e, j=j: e.matmul(ps[:, 0:N], g.ones_f[:], sq[j][:, 0:N], start=(j == 0), stop=(j == 1)),
              reads=["gn_sq%d" % j, "ones_f"], writes=[pk])
    P.add("act", lambda e: e.activation(out=rin[:, 0:N], in_=ps[:, 0:N], func=AF.Sqrt, scale=1.0 / 256, bias=EPS),
          reads=[pk], writes=["gn_rin"])
    P.add("dve", lambda e: e.reciprocal(out=rin[:, 0:N], in_=rin[:, 0:N]), reads=["gn_rin"], writes=["gn_rin"])
    for j, yj in enumerate((y0, y1)):
        ob = W["gn_ob"][j]; ok = "gn_ob%d" % j
        c = grp * 2 + j
        P.add("dve", lambda e, yj=yj, ob=ob, c=c: e.scalar_tensor_tensor(
            out=ob[:, 0:N], in0=yj, scalar=g.mg[:, c:c + 1], in1=rin[:, 0:N], op0=ALU.mult, op1=ALU.mult),
              reads=[ykeys[j], "gn_rin", "mg"], writes=[ok])
        DMA(P, "sp", g.sc["ymT"][c * 128:(c + 1) * 128, t0:t0 + N], ob[:, 0:N], reads=[ok])


def gn_alloc(g, st):
    nc = g.nc
    sb = mk_sb(g, st)
    return dict(gn_sq=[sb("gn_sq%d" % j, [128, 512], F32) for j in range(2)], gn_rin=sb("gn_rin", [128, 512], F32),
                gn_ob=[sb("gn_ob%d" % j, [128, 512], BF16) for j in range(2)])


def load_layer_consts(g, l):
    nc, P = g.nc, g.P
    DMA(P, "sp", g.mg[:], g.w["merge_gain"][l].rearrange("(c p) -> p c", p=128), writes=["mg"], slow=True)


def phase_conv(g, l):
    nc, P, T = g.nc, g.P, g.T
    with contextlib.ExitStack() as st:
        sb = mk_sb(g, st)
        W = gn_alloc(g, st)
        cw = sb("cw", [128, 2, 3], F32)
        pp = sb("pp", [128, T + 2], F32)
        cb = sb("cb", [128, T], F32)
        yy = [sb("yc%d" % j, [128, T], F32) for j in range(2)]
        for j in range(2):
            DMA(P, "sp", cw[:, j, :], g.w["conv_a_w"][l][:, j * 128:(j + 1) * 128].rearrange("k p -> p k"),
                writes=["cw"], slow=True)
        for j in range(2):
            A(P, "pool", lambda e: e.memset(pp[:, 0:1], 0.0), writes=["pp"])
            A(P, "pool", lambda e: e.memset(pp[:, T + 1:T + 2], 0.0), writes=["pp"])
            DMA(P, "sp", pp[:, 1:T + 1], g.sc["pT"][j * 128:(j + 1) * 128, :], writes=["pp"])
            DMA(P, "sp", cb[:], g.sc["cbT"][j * 128:(j + 1) * 128, :], writes=["cb"])
            y = yy[j]; yk = "yc%d" % j
            A(P, "dve", lambda e, y=y, j=j: e.tensor_scalar(out=y[:], in0=pp[:, 0:T], scalar1=cw[:, j, 0:1], scalar2=None,
                                                           op0=ALU.mult), reads=["pp", "cw"], writes=[yk])
            for k in (1, 2):
                A(P, "dve", lambda e, y=y, j=j, k=k: e.scalar_tensor_tensor(
                    out=y[:], in0=pp[:, k:T + k], scalar=cw[:, j, k:k + 1], in1=y[:], op0=ALU.mult, op1=ALU.add),
                  reads=["pp", "cw", yk], writes=[yk])
            A(P, "dve", lambda e, y=y: e.tensor_tensor(out=y[:], in0=y[:], in1=cb[:], op=ALU.mult),
              reads=[yk, "cb"], writes=[yk])
        for t0 in range(0, T, 512):
            gnorm(g, l, 0, yy[0][:, t0:t0 + 512], yy[1][:, t0:t0 + 512], ["yc0", "yc1"], t0, 512, W)
        P.flush()


def phase_sgu(g, l):
    nc, P, T = g.nc, g.P, g.T
    with contextlib.ExitStack() as st:
        sb = mk_sb(g, st)
        W = gn_alloc(g, st)
        wraw = sb("wraw", [128, 4, 128], F32)
        wbf = sb("wbf", [128, 4, 128], BF16)
        wsT = sb("wsT", [128, 4, 128], BF16)
        bs = sb("bs", [1, 4, 128], F32)
        sgn = sb("sgn", [128, 256], F32)
        uT = sb("uT", [128, 2, T], F32)
        yy = [sb("ys%d" % j, [128, T], F32) for j in range(2)]
        vt = [sb("vt%d" % i, [128, 256], F32) for i in range(2)]
        st6 = sb("st6", [128, 6], F32)
        mv = sb("mv", [128, 4], F32)
        vc = sb("vc", [128, 256], F32)
        vn = [sb("vn%d" % i, [128, 256], BF16) for i in range(2)]
        DMA(P, "sp", wraw[:], g.w["sgu_w"][l].rearrange("h p q -> p h q"), writes=["wraw"])
        DMA(P, "sp", bs[:], g.w["sgu_b"][l:l + 1], writes=["bs"])
        DMA(P, "sp", sgn[:], g.w["sgu_norm"][l].partition_broadcast(128), writes=["sgn"], slow=True)
        for j in range(2):
            DMA(P, "sp", uT[:, j, :], g.sc["usT"][j * 128:(j + 1) * 128, :], writes=["uT"])
        A(P, "dve", lambda e: e.tensor_copy(out=wbf[:], in_=wraw[:]), reads=["wraw"], writes=["wbf"])
        for h in range(4):
            A(P, "pe", lambda e, h=h: e.transpose(out=g.pst[:, h * 128:(h + 1) * 128], in_=wbf[:, h, :],
                                                  identity=g.ident_bf[:]), reads=["wbf", "ident_bf"], writes=["pst"])
        A(P, "dve", lambda e: e.tensor_copy(out=wsT[:].rearrange("p h q -> p (h q)"), in_=g.pst[:, 0:512]),
          reads=["pst"], writes=["wsT"])
        for n in range(T // 128):
            i = n % 2
            v = vt[i]; vk = "vt%d" % i
            DMA(P, "sp", v[:], g.sc["vstok"][n * 128:(n + 1) * 128, :], writes=[vk])
            A(P, "dve", lambda e, v=v: e.bn_stats(out=st6[:], in_=v[:]), reads=[vk], writes=["st6"])
            A(P, "dve", lambda e: e.bn_aggr(out=mv[:, 0:2], in_=st6[:]), reads=["st6"], writes=["mv"])
            A(P, "act", lambda e: e.activation(out=mv[:, 2:3], in_=mv[:, 1:2], func=AF.Sqrt, scale=1.0, bias=EPS),
              reads=["mv"], writes=["mv2"])
            A(P, "dve", lambda e: e.reciprocal(out=mv[:, 3:4], in_=mv[:, 2:3]), reads=["mv2"], writes=["mv3"])
            A(P, "dve", lambda e, v=v: e.tensor_scalar(out=vc[:], in0=v[:], scalar1=mv[:, 0:1], scalar2=mv[:, 3:4],
                                                      op0=ALU.subtract, op1=ALU.mult),
              reads=[vk, "mv", "mv3"], writes=["vc"])
            vnn = vn[i]; vnk = "vn%d" % i
            A(P, "dve", lambda e, vnn=vnn: e.tensor_tensor(out=vnn[:], in0=vc[:], in1=sgn[:], op=ALU.mult),
              reads=["vc", "sgn"], writes=[vnk])
            for hp in range(2):
                pi = (n * 2 + hp) % 6
                ps = g.ps[pi]; pk = "ps%d" % pi
                for e2 in range(2):
                    h = hp * 2 + e2
                    A(P, "pe", lambda e, ps=ps, vnn=vnn, hp=hp, h=h, e2=e2: e.matmul(
                        ps[:, e2 * 128:(e2 + 1) * 128], vnn[:, hp * 128:(hp + 1) * 128], wsT[:, h, :],
                        start=True, stop=False), reads=[vnk, "wsT"], writes=[pk])
                    A(P, "pe", lambda e, ps=ps, h=h, e2=e2: e.matmul(
                        ps[:, e2 * 128:(e2 + 1) * 128], g.ones_f[0:1, :], bs[0:1, h, :],
                        start=False, stop=True), reads=["ones_f", "bs"], writes=[pk])
                y = yy[hp]; yk = "ys%d" % hp
                for e2 in range(2):
                    A(P, "dve", lambda e, ps=ps, y=y, hp=hp, e2=e2, n=n: e.tensor_tensor(
                        out=y[e2 * 64:(e2 + 1) * 64, n * 128:(n + 1) * 128],
                        in0=ps[e2 * 64:(e2 + 1) * 64, e2 * 128:(e2 + 1) * 128],
                        in1=uT[e2 * 64:(e2 + 1) * 64, hp, n * 128:(n + 1) * 128], op=ALU.mult),
                      reads=[pk, "uT"], writes=[yk])
        for t0 in range(0, T, 512):
            gnorm(g, l, 3, yy[0][:, t0:t0 + 512], yy[1][:, t0:t0 + 512], ["ys0", "ys1"], t0, 512, W)
        P.flush()


def na_consts(g, st, l):
    nc, P = g.nc, g.P
    sbp = mk_sb(g, st)
    g.BT = {l: sbp("BT%d" % l, [128, 4, 14, 64], BF16)}
    g.ones_bf = sbp("ones_bf", [128, 128], BF16)
    A(P, "dve", lambda e: e.tensor_copy(out=g.ones_bf[:], in_=g.ones_f[:]), reads=["ones_f"], writes=["ones_bf"])
    with contextlib.ExitStack() as st2:
        sb = mk_sb(g, st2)
        OH = sb("OH", [31, 2, 64, 64], F32)
        madd = sb("madd", [128, 64], F32)
        rpbT = sb("rpbT", [31, 60], F32)
        A(P, "pool", lambda e: e.memset(OH[:], 1.0), writes=["OH"])
        A(P, "pool", lambda e: e.affine_select(out=OH[:], in_=OH[:], pattern=[[0, 2], [1, 64], [-1, 64]],
                                               compare_op=ALU.is_equal, fill=0.0, base=15, channel_multiplier=-1),
          reads=["OH"], writes=["OH"])
        A(P, "pool", lambda e: e.memset(madd[:], 0.0), writes=["madd"])
        for e2 in range(2):
            m = madd[e2 * 64:(e2 + 1) * 64, :]
            sel = lambda ap, pat, base, cm: A(P, "pool", lambda e: e.affine_select(
                out=ap, in_=ap, pattern=pat, compare_op=ALU.is_ge, fill=-100.0, base=base, channel_multiplier=cm),
                                              reads=["madd"], writes=["madd"])
            sel(m[:, 0:8], [[0, 8]], 15, -1)
            sel(m[:, 8:57], [[-1, 49]], 0, 1)
            sel(m[:, 8:57], [[1, 49]], 15, -1)
            sel(m[:, 57:64], [[0, 7]], -48, 1)
        for l in (l,):
            DMA(P, "sp", rpbT[:], g.w["na_rpb"][l].rearrange("h d i -> i (h d)"), writes=["rpbT"], slow=True)
            for cg in range(8):
                pi = cg % 6
                ps = g.ps[pi]; pk = "ps%d" % pi
                for ci in range(8):
                    c = cg * 8 + ci
                    A(P, "pe", lambda e, ps=ps, ci=ci, c=c: e.matmul(
                        ps[:, ci * 60:(ci + 1) * 60], OH[:, :, :, c].rearrange("i e k -> i (e k)"), rpbT[:],
                        start=True, stop=True), reads=["OH", "rpbT"], writes=[pk])
                for e2 in range(2):
                    A(P, "dve", lambda e, ps=ps, e2=e2, cg=cg, l=l: e.tensor_tensor(
                        out=g.BT[l][e2 * 64:(e2 + 1) * 64, :, :, cg * 8:(cg + 1) * 8],
                        in0=ps[e2 * 64:(e2 + 1) * 64, 0:480].rearrange("p (c h d) -> p h d c", c=8, h=4)[:, :, e2:e2 + 14, :],
                        in1=madd[e2 * 64:(e2 + 1) * 64, cg * 8:(cg + 1) * 8].unsqueeze(1).unsqueeze(1).to_broadcast([64, 4, 14, 8]),
                        op=ALU.add), reads=[pk, "madd"], writes=[("BT", l)])
        if "dbg_BT" in g.dbgnames:
            d1 = nc.dram_tensor("dbg_BT", [128, 4 * 14 * 64], BF16, kind="ExternalOutput").ap()
            DMA(P, "sp", d1, g.BT[l][:].rearrange("p h j c -> p (h j c)"), reads=[("BT", l)])
        P.flush()


def phase_na(g, l):
    nc, P, T = g.nc, g.P, g.T
    rows = T // 64
    with contextlib.ExitStack() as st:
        na_consts(g, st, l)
        sb = mk_sb(g, st)
        W = gn_alloc(g, st)
        kn = sb("kn", [128, 2, T], BF16)
        qn = sb("qn", [128, 2, T], BF16)
        Va = sb("Va", [128, T // 128, 256], BF16)
        Vb = sb("Vb", [128, T // 128 - 1, 256], BF16)
        yn = sb("yn", [128, 2, T], F32)
        pT = [sb("pT%d" % i, [128, 256], BF16) for i in range(3)]
        rec = [sb("rec%d" % i, [128, 64], F32) for i in range(2)]
        ssb = [sb("ssb%d" % i, [128, 4, 64], F32) for i in range(3)]
        for j in range(2):
            DMA(P, "sp", kn[:, j, :], g.sc["knT"][j * 128:(j + 1) * 128, :], writes=["kn"])
            DMA(P, "sp", qn[:, j, :], g.sc["qnT"][j * 128:(j + 1) * 128, :], writes=["qn"])
        DMA(P, "sp", Va[:], g.sc["vntok"].rearrange("(n p) f -> p n f", p=128), writes=["Va"])
        DMA(P, "sp", Vb[:], g.sc["vntok"][64:T - 64, :].rearrange("(n p) f -> p n f", p=128), writes=["Vb"])
        items = [(h, r) for h in range(4) for r in range(rows)]

        def bufs(it):
            return (g.ps[(it % 3) * 2], "ps%d" % ((it % 3) * 2), g.ps[(it % 3) * 2 + 1], "ps%d" % ((it % 3) * 2 + 1),
                    pT[it % 3], "pT%d" % (it % 3), rec[it % 2], "rec%d" % (it % 2), ssb[it % 3], "ssb%d" % (it % 3))

        def stageA(it):
            h, r = items[it]
            hp, hb = h // 2, (h % 2) * 64
            rs = min(max(r - 4, 0), rows - 8)
            pa, pak, pb, pbk, pt, ptk, rc, rck, sb_, sbk = bufs(it)
            j0 = rs - r + 7
            for i in range(4):
                kr0 = rs + 2 * i
                A(P, "pe", lambda e, pa=pa, i=i, kr0=kr0, r=r, hp=hp, hb=hb: e.matmul(
                    pa[:, i * 64:(i + 1) * 64], kn[hb:hb + 64, hp, kr0 * 64:kr0 * 64 + 128],
                    qn[hb:hb + 64, hp, r * 64:(r + 1) * 64], start=True, stop=True),
                  reads=["kn", "qn"], writes=[pak])
            A(P, "dve", lambda e, pa=pa, sb_=sb_, h=h, j0=j0: e.tensor_tensor(
                out=sb_[:], in0=pa[:, 0:256].rearrange("p (i q) -> p i q", i=4), in1=g.BT[l][:, h, j0:j0 + 7:2, :],
                op=ALU.add), reads=[pak, ("BT", l)], writes=[sbk])
            A(P, "act", lambda e, sb_=sb_, pt=pt: e.activation(out=pt[:], in_=sb_[:].rearrange("p i q -> p (i q)"), func=AF.Exp),
              reads=[sbk], writes=[ptk])

        def stageB(it):
            h, r = items[it]
            hp, hb = h // 2, (h % 2) * 64
            rs = min(max(r - 4, 0), rows - 8)
            pa, pak, pb, pbk, pt, ptk, rc, rck, sb_, sbk = bufs(it)
            for i in range(4):
                kr0 = rs + 2 * i
                Vt = Va[:, kr0 // 2, hp * 128:(hp + 1) * 128] if kr0 % 2 == 0 else Vb[:, (kr0 - 1) // 2, hp * 128:(hp + 1) * 128]
                A(P, "pe", lambda e, pb=pb, pt=pt, i=i, Vt=Vt: e.matmul(
                    pb[:, 0:64], Vt, pt[:, i * 64:(i + 1) * 64], start=(i == 0), stop=(i == 3)),
                  reads=[ptk, "Va", "Vb"], writes=[pbk])
            A(P, "pe", lambda e, pb=pb, pt=pt: e.matmul(pb[:, 64:320], g.ones_bf[:], pt[:, 0:256], start=True, stop=True),
              reads=[ptk, "ones_bf"], writes=[pbk])
            A(P, "dve", lambda e, pb=pb, rc=rc, hb=hb: e.tensor_reduce(
                out=rc[hb:hb + 64, :], in_=pb[hb:hb + 64, 64:320].rearrange("p (i q) -> p q i", i=4),
                axis=mybir.AxisListType.X, op=ALU.add), reads=[pbk], writes=[rck])
            A(P, "dve", lambda e, rc=rc, hb=hb: e.reciprocal(out=rc[hb:hb + 64, :], in_=rc[hb:hb + 64, :]),
              reads=[rck], writes=[rck])
            A(P, "dve", lambda e, pb=pb, rc=rc, hb=hb, hp=hp, r=r: e.tensor_tensor(
                out=yn[hb:hb + 64, hp, r * 64:(r + 1) * 64], in0=pb[hb:hb + 64, 0:64], in1=rc[hb:hb + 64, :],
                op=ALU.mult), reads=[pbk, rck], writes=["yn"])

        nit = len(items)
        for it in range(nit + 1):
            if it < nit:
                stageA(it)
            if it >= 1:
                stageB(it - 1)
        if "dbg_yn" in g.dbgnames:
            d1 = nc.dram_tensor("dbg_yn", [128, 2 * T], F32, kind="ExternalOutput").ap()
            DMA(P, "sp", d1, yn[:].rearrange("p a t -> p (a t)"), reads=["yn"])
            d2 = nc.dram_tensor("dbg_Va", [128, (T // 128) * 256], BF16, kind="ExternalOutput").ap()
            DMA(P, "sp", d2, Va[:].rearrange("p a t -> p (a t)"), reads=["Va"])
            d3 = nc.dram_tensor("dbg_Vb", [128, (T // 128 - 1) * 256], BF16, kind="ExternalOutput").ap()
            DMA(P, "sp", d3, Vb[:].rearrange("p a t -> p (a t)"), reads=["Vb"])
        for t0 in range(0, T, 512):
            gnorm(g, l, 2, yn[:, 0, t0:t0 + 512], yn[:, 1, t0:t0 + 512], ["yn", "yn"], t0, 512, W)
        P.flush()


DSC = float(np.exp(-0.5))
NB = 256


def mm(P, out, lhsT, rhs, start, stop, reads, writes):
    return P.add("pe", lambda e: e.matmul(out, lhsT, rhs, start=start, stop=stop), reads, writes)


def rwkv_consts(g, st):
    nc, P = g.nc, g.P
    sb = mk_sb(g, st)
    g.blk1 = sb("blk1", [128, 128], F32)
    g.idst = sb("idst", [128, 64], F32)
    g.msk3 = [sb("msk3_%d" % d, [64, 3, 64], F32) for d in range(2)]
    g.msk2 = [sb("msk2_%d" % d, [64, 2, 64], F32) for d in range(2)]
    A(P, "pool", lambda e: e.memset(g.blk1[:], 0.0), writes=["blk1"])
    A(P, "pool", lambda e: e.memset(g.blk1[0:64, 0:64], 1.0), writes=["blk1"])
    A(P, "pool", lambda e: e.memset(g.blk1[64:128, 64:128], 1.0), writes=["blk1"])
    A(P, "dve", lambda e: e.tensor_copy(out=g.idst[0:64, :], in_=g.tmp_f[0:64, 0:64]), reads=["tmp_f"], writes=["idst"])
    A(P, "dve", lambda e: e.tensor_copy(out=g.idst[64:128, :], in_=g.tmp_f[64:128, 64:128]), reads=["tmp_f"], writes=["idst"])

    g.hm = sb("hm", [128, 2], F32)
    g.hmn = sb("hmn", [128, 2], F32)
    g.idst_bf = sb("idst_bf", [128, 64], BF16)
    g.mask4 = [sb("mask4_%d" % d, [128, 4, 128], F32) for d in range(2)]
    g.maskT = [sb("maskT_%d" % d, [128, 128], F32) for d in range(2)]
    A(P, "pool", lambda e: e.memset(g.hm[:], 0.0), writes=["hm"])
    A(P, "pool", lambda e: e.memset(g.hm[0:64, 0:1], 1.0), writes=["hm"])
    A(P, "pool", lambda e: e.memset(g.hm[64:128, 1:2], 1.0), writes=["hm"])
    A(P, "pool", lambda e: e.tensor_scalar(out=g.hmn[:], in0=g.hm[:], scalar1=-1.0, scalar2=None, op0=ALU.mult),
      reads=["hm"], writes=["hmn"])
    A(P, "dve", lambda e: e.tensor_copy(out=g.idst_bf[:], in_=g.idst[:]), reads=["idst"], writes=["idst_bf"])

    def selbd(tile_ap, cm, step, op, key):
        A(P, "pool", lambda e: e.memset(tile_ap, 0.0), writes=[key])
        for h2 in range(2):
            ap = tile_ap[h2 * 64:(h2 + 1) * 64, h2 * 64:(h2 + 1) * 64]
            A(P, "pool", lambda e, ap=ap: e.memset(ap, 1.0), writes=[key])
            A(P, "pool", lambda e, ap=ap: e.affine_select(out=ap, in_=ap, pattern=[[step, 64]], compare_op=op, fill=0.0,
                                                         base=0, channel_multiplier=cm), reads=[key], writes=[key])
    for d in range(2):
        sg = 1 if d == 0 else -1
        selbd(g.mask4[d][:, 0, :], -sg, sg, ALU.is_gt, "mskbd")
        selbd(g.mask4[d][:, 1, :], -sg, sg, ALU.is_ge, "mskbd")
        selbd(g.mask4[d][:, 2, :], -sg, sg, ALU.is_gt, "mskbd")
        selbd(g.mask4[d][:, 3, :], -sg, sg, ALU.is_ge, "mskbd")
        selbd(g.maskT[d][:, :], sg, -sg, ALU.is_gt, "mskbd")

    def sel(ap, cm, step, op, key):
        A(P, "pool", lambda e: e.memset(ap, 1.0), writes=[key])
        A(P, "pool", lambda e: e.affine_select(out=ap, in_=ap, pattern=[[step, 64]], compare_op=op, fill=0.0,
                                               base=0, channel_multiplier=cm), reads=[key], writes=[key])
    for d in range(2):
        sg = 1 if d == 0 else -1
        sel(g.msk3[d][:, 0, :], -sg, sg, ALU.is_gt, "msk")
        sel(g.msk3[d][:, 1, :], sg, -sg, ALU.is_gt, "msk")
        sel(g.msk3[d][:, 2, :], -sg, sg, ALU.is_gt, "msk")
        sel(g.msk2[d][:, 0, :], -sg, sg, ALU.is_ge, "msk")
        sel(g.msk2[d][:, 1, :], -sg, sg, ALU.is_ge, "msk")


def phase_rwkv(g, l):
    nc, P, T = g.nc, g.P, g.T
    with contextlib.ExitStack() as st:
        sb = mk_sb(g, st)
        W = gn_alloc(g, st)
        Wl = sb("Wl", [96, 5, 256], F32)
        w0c = sb("w0c", [128, 2, 2], F32); a0c = sb("a0c", [128, 2, 2], F32)
        kkc = sb("kkc", [128, 2], F32); kac = sb("kac", [128, 2], F32); omk = sb("omk", [128, 2], F32)
        rkc = sb("rkc", [128, 2], F32); lwc = sb("lwc", [128, 2], F32); lbc = sb("lbc", [128, 2], F32)
        yacc = sb("yacc", [128, 2, T], F32)
        A(P, "pool", lambda e: e.memset(Wl[:], 0.0), writes=["Wl"])
        A(P, "pool", lambda e: e.memset(yacc[:], 0.0), writes=[("yacc", 0), ("yacc", 1)])
        for d in range(2):
            DMA(P, "sp", Wl[d * 16:(d + 1) * 16, d, :], g.w["rwkv_w_up"][l, d], writes=["Wl"])
            DMA(P, "sp", Wl[32 + d * 16:32 + (d + 1) * 16, 2 + d, :], g.w["rwkv_a_up"][l, d], writes=["Wl"])
            for hp in range(2):
                DMA(P, "sp", w0c[:, d, hp:hp + 1], g.w["rwkv_w0"][l, d, hp * 128:(hp + 1) * 128].unsqueeze(1), writes=["cst"], slow=True)
                DMA(P, "sp", a0c[:, d, hp:hp + 1], g.w["rwkv_a0"][l, d, hp * 128:(hp + 1) * 128].unsqueeze(1), writes=["cst"], slow=True)
        DMA(P, "sp", Wl[64:96, 4, :], g.w["rwkv_g_up"][l], writes=["Wl"])
        for nm, tl in (("rwkv_k_k", kkc), ("rwkv_k_a", kac), ("rwkv_lnx_w", lwc), ("rwkv_lnx_b", lbc)):
            DMA(P, "sp", tl[:], g.w[nm][l].rearrange("(c p) -> p c", p=128), writes=["cst"], slow=True)
        DMA(P, "sp", rkc[:], g.w["rwkv_r_k"][l].rearrange("(c a) k -> (a k) c", a=2), writes=["cst"], slow=True)
        A(P, "dve", lambda e: e.tensor_scalar(out=omk[:], in0=kac[:], scalar1=-1.0, scalar2=1.0, op0=ALU.mult, op1=ALU.add),
          reads=["cst"], writes=["omk"])
        C_ = dict(Wl=Wl, w0c=w0c, a0c=a0c, kkc=kkc, kac=kac, omk=omk, yacc=yacc)
        gens = []
        sid = 0
        import os
        nsw = int(os.environ.get("NSW", "4"))
        for d in range(2):
            for hp in range(2):
                if sid < nsw:
                    gens.append((rwkv_sweep_bd if os.environ.get("RWBD", "1") == "1" else rwkv_sweep)(g, l, d, hp, sid, st, C_))
                sid += 1
        alive = list(gens)
        import os
        lim = int(os.environ.get("RW_LIMIT", "1000000"))
        rounds = 0
        while alive and rounds < lim:
            nxt = []
            for gen in alive:
                try:
                    next(gen)
                    nxt.append(gen)
                except StopIteration:
                    pass
            alive = nxt
            rounds += 1
        P.flush()
        if "dbg_yacc" in g.dbgnames:
            d1 = nc.dram_tensor("dbg_yacc", [128, 2 * T], F32, kind="ExternalOutput").ap()
            DMA(P, "sp", d1, yacc[:].rearrange("p a t -> p (a t)"), reads=[("yacc", 0), ("yacc", 1)])
            P.flush()
        if lim < 1000000:
            return
        rB = sb("prB", [128, 512], F32); kB = sb("pkB", [128, 512], F32); vB = sb("pvB", [128, 512], F32)
        loB = sb("ploB", [96, 512], F32)
        u = [sb("pu%d" % i, [128, 512], F32) for i in range(4)]
        for t0 in range(0, T, 512):
            ts = slice(t0, t0 + 512)
            DMA(P, "sp", loB[:], g.sc["loraT"][:, ts], writes=["ploB"])
            for hp in range(2):
                hs = slice(hp * 128, (hp + 1) * 128)
                y = yacc[:, hp, ts]
                DMA(P, "sp", rB[:], g.sc["rT"][hs, ts], writes=["prB"])
                DMA(P, "sp", kB[:], g.sc["kT"][hs, ts], writes=["pkB"])
                DMA(P, "sp", vB[:], g.sc["vT"][hs, ts], writes=["pvB"])
                pa, pb = g.ps[4], g.ps[5]
                mm(P, pa[:], g.blk1[:], y, True, True, ["yacc", "blk1"], ["ps4"])
                A(P, "act", lambda e, y=y: e.activation(out=u[0][:], in_=y, func=AF.Square), reads=["yacc"], writes=["pu0"])
                mm(P, pb[:], g.blk1[:], u[0][:], True, True, ["pu0", "blk1"], ["ps5"])
                A(P, "dve", lambda e: e.tensor_scalar(out=u[1][:], in0=pa[:], scalar1=1.0 / 64, scalar2=None, op0=ALU.mult),
                  reads=["ps4"], writes=["pu1"])
                A(P, "dve", lambda e: e.tensor_tensor(out=u[2][:], in0=u[1][:], in1=u[1][:], op=ALU.mult),
                  reads=["pu1"], writes=["pu2"])
                A(P, "dve", lambda e: e.scalar_tensor_tensor(out=u[2][:], in0=pb[:], scalar=1.0 / 64, in1=u[2][:],
                                                             op0=ALU.mult, op1=ALU.subtract),
                  reads=["ps5", "pu2"], writes=["pu2"])
                A(P, "act", lambda e: e.activation(out=u[2][:], in_=u[2][:], func=AF.Sqrt, scale=1.0, bias=64e-5),
                  reads=["pu2"], writes=["pu2"])
                A(P, "dve", lambda e: e.reciprocal(out=u[2][:], in_=u[2][:]), reads=["pu2"], writes=["pu2"])
                A(P, "dve", lambda e, y=y: e.tensor_tensor(out=u[1][:], in0=y, in1=u[1][:], op=ALU.subtract),
                  reads=["yacc", "pu1"], writes=["pu1"])
                A(P, "dve", lambda e: e.tensor_tensor(out=u[1][:], in0=u[1][:], in1=u[2][:], op=ALU.mult),
                  reads=["pu1", "pu2"], writes=["pu1"])
                A(P, "dve", lambda e, hp=hp: e.tensor_scalar(out=u[1][:], in0=u[1][:], scalar1=lwc[:, hp:hp + 1],
                                                            scalar2=lbc[:, hp:hp + 1], op0=ALU.mult, op1=ALU.add),
                  reads=["pu1", "cst"], writes=["pu1"])
                A(P, "dve", lambda e, hp=hp: e.scalar_tensor_tensor(out=u[3][:], in0=rB[:], scalar=rkc[:, hp:hp + 1],
                                                                   in1=kB[:], op0=ALU.mult, op1=ALU.mult),
                  reads=["prB", "pkB", "cst"], writes=["pu3"])
                mm(P, pa[:], g.blk1[:], u[3][:], True, True, ["pu3", "blk1"], ["ps4"])
                A(P, "dve", lambda e: e.tensor_tensor(out=u[3][:], in0=pa[:], in1=vB[:], op=ALU.mult),
                  reads=["ps4", "pvB"], writes=["pu3"])
                A(P, "dve", lambda e: e.tensor_tensor(out=u[1][:], in0=u[1][:], in1=u[3][:], op=ALU.add),
                  reads=["pu1", "pu3"], writes=["pu1"])
                mm(P, pb[:], Wl[:, 4, hs], loB[:], True, True, ["Wl", "ploB"], ["ps5"])
                A(P, "dve", lambda e, y=y: e.tensor_tensor(out=y, in0=u[1][:], in1=pb[:], op=ALU.mult),
                  reads=["pu1", "ps5"], writes=["yacc"])
            gnorm(g, l, 1, yacc[:, 0, ts], yacc[:, 1, ts], ["yacc", "yacc"], t0, 512, W)
        P.flush()


def rwkv_sweep(g, l, d, hp, sid, st, C_):
    nc, P, T = g.nc, g.P, g.T
    sbq = mk_sb(g, st)
    K = lambda n: "%s_s%d" % (n, sid)
    f32t = {}
    for n in ("rB", "kB", "s", "a", "kkr", "kk", "t1", "ke", "bb", "cs", "cw", "u1", "E"):
        f32t[n] = sbq(K(n), [128, NB], F32)
    loB = sbq(K("loB"), [96, NB], F32)
    bft = {n: sbq(K(n), [128, NB], BF16) for n in ("rt", "at", "bt", "kt", "Bh", "Kh")}
    bft1 = {n: sbq(K(n + "1"), [64, NB], BF16) for n in ("rt", "at", "bt", "kt", "Bh", "Kh")}
    VB = sbq(K("VB"), [64, NB // 64, 256], BF16)
    Wt = sbq(K("Wt"), [128, NB // 64], F32)
    Dg = sbq(K("Dg"), [128, NB // 64, 64], BF16)
    Dg1 = sbq(K("Dg1"), [64, NB // 64, 64], BF16)

    def fm(n, h2, cs_):
        return (bft[n] if h2 == 0 else bft1[n])[0:64, cs_]

    def fk(n, h2):
        return K(n) if h2 == 0 else K(n + "1")
    L3 = sbq(K("L3"), [64, 2, 3, 64], BF16)
    M2 = sbq(K("M2"), [64, 2, 2, 64], BF16)
    QT = sbq(K("QT"), [64, 2, 64], BF16)
    PP = sbq(K("PP"), [64, 2, 2, 64], BF16)
    X = [sbq(K("X%d" % i), [64, 2, 128], BF16) for i in range(2)]
    U0d = sbq(K("U0d"), [64, 2, 2, 64], BF16)
    BK = sbq(K("BK"), [64, 2, 2, 64], BF16)
    GT = sbq(K("GT"), [64, 2, 64], BF16)
    Dsb = sbq(K("Dsb"), [64, 2, 64], BF16)
    RhT = sbq(K("RhT"), [64, 2, 64], BF16)
    H2 = [sbq(K("H2%d" % i), [64, 2, 2, 64], BF16) for i in range(2)]
    B = g.ps[sid]; BKEY = "ps%d" % sid
    PB = g.ps[4 + (sid % 2)]; PBK = "ps%d" % (4 + (sid % 2))
    Wl, w0c, a0c, kkc, kac, omk, yacc = (C_[n] for n in ("Wl", "w0c", "a0c", "kkc", "kac", "omk", "yacc"))
    hs = slice(hp * 128, (hp + 1) * 128)
    idb = g.ident_bf
    t_ = f32t
    nck = NB // 64
    A(P, "pool", lambda e: e.memset(H2[0][:], 0.0), writes=[K("H20")])
    hcur = 0
    nblk = T // NB
    blocks = list(range(nblk)) if d == 0 else list(range(nblk - 1, -1, -1))
    first_dir = False

    def v3(ap):
        return ap.rearrange("p (c t) -> p c t", t=64)

    for bi in blocks:
        bs = slice(bi * NB, (bi + 1) * NB)
        DMA(P, "sp", t_["rB"][:], g.sc["rT"][hs, bs], writes=[K("rB")])
        DMA(P, "sp", t_["kB"][:], g.sc["kT"][hs, bs], writes=[K("kB")])
        DMA(P, "sp", loB[:], g.sc["loraT"][:, bs], writes=[K("loB")])
        DMA(P, "sp", VB[:], g.sc["vtok"][bs, :].rearrange("(n s) f -> s n f", s=64), writes=[K("VB")])
        mm(P, PB[:, 0:NB], Wl[:, d, hs], loB[:], True, True, ["Wl", K("loB")], [PBK])
        mm(P, PB[:, NB:2 * NB], Wl[:, 2 + d, hs], loB[:], True, True, ["Wl", K("loB")], [PBK])
        A(P, "act", lambda e: e.activation(out=t_["s"][:], in_=PB[:, 0:NB], func=AF.Sigmoid, bias=w0c[:, d, hp:hp + 1]),
          reads=[PBK, "cst"], writes=[K("s")])
        A(P, "act", lambda e: e.activation(out=t_["a"][:], in_=PB[:, NB:2 * NB], func=AF.Sigmoid, bias=a0c[:, d, hp:hp + 1]),
          reads=[PBK, "cst"], writes=[K("a")])
        A(P, "dve", lambda e: e.tensor_scalar(out=t_["kkr"][:], in0=t_["kB"][:], scalar1=kkc[:, hp:hp + 1], scalar2=None,
                                              op0=ALU.mult), reads=[K("kB"), "cst"], writes=[K("kkr")])
        A(P, "act", lambda e: e.activation(out=t_["u1"][:], in_=t_["kkr"][:], func=AF.Square), reads=[K("kkr")], writes=[K("u1")])
        mm(P, PB[:, 0:NB], g.blk1[:], t_["u1"][:], True, True, ["blk1", K("u1")], [PBK])
        A(P, "act", lambda e: e.activation(out=t_["u1"][:], in_=PB[:, 0:NB], func=AF.Sqrt, scale=1.0, bias=1e-12),
          reads=[PBK], writes=[K("u1")])
        A(P, "dve", lambda e: e.reciprocal(out=t_["u1"][:], in_=t_["u1"][:]), reads=[K("u1")], writes=[K("u1")])
        A(P, "dve", lambda e: e.tensor_tensor(out=t_["kk"][:], in0=t_["kkr"][:], in1=t_["u1"][:], op=ALU.mult),
          reads=[K("kkr"), K("u1")], writes=[K("kk")])
        A(P, "dve", lambda e: e.tensor_scalar(out=t_["t1"][:], in0=t_["a"][:], scalar1=kac[:, hp:hp + 1],
                                              scalar2=omk[:, hp:hp + 1], op0=ALU.mult, op1=ALU.add),
          reads=[K("a"), "cst", "omk"], writes=[K("t1")])
        A(P, "dve", lambda e: e.tensor_tensor(out=t_["ke"][:], in0=t_["kB"][:], in1=t_["t1"][:], op=ALU.mult),
          reads=[K("kB"), K("t1")], writes=[K("ke")])
        A(P, "dve", lambda e: e.tensor_tensor(out=t_["bb"][:], in0=t_["kk"][:], in1=t_["a"][:], op=ALU.mult),
          reads=[K("kk"), K("a")], writes=[K("bb")])
        for c in range(nck):
            A(P, "dve", lambda e, c=c: e.tensor_tensor_scan(out=t_["cs"][:, c * 64:(c + 1) * 64], data0=g.ones_f[:, 0:64],
                                                           data1=t_["s"][:, c * 64:(c + 1) * 64], initial=0.0,
                                                           op0=ALU.mult, op1=ALU.add),
              reads=[K("s"), "ones_f"], writes=[K("cs")])
        totb = v3(t_["cs"][:])[:, :, 63:64].to_broadcast([128, nck, 64])
        if d == 0:
            cw = t_["cs"]; cwk = K("cs")
        else:
            cw = t_["cw"]; cwk = K("cw")
            A(P, "dve", lambda e: e.tensor_tensor(out=t_["u1"][:], in0=t_["s"][:], in1=t_["cs"][:], op=ALU.subtract),
              reads=[K("s"), K("cs")], writes=[K("u1")])
            A(P, "dve", lambda e: e.tensor_tensor(out=v3(cw[:]), in0=v3(t_["u1"][:]), in1=totb, op=ALU.add),
              reads=[K("u1"), K("cs")], writes=[cwk])
        A(P, "act", lambda e: e.activation(out=t_["E"][:], in_=cw[:], func=AF.Exp, scale=-DSC), reads=[cwk], writes=[K("E")])
        A(P, "dve", lambda e: e.tensor_tensor(out=bft["rt"][:], in0=t_["rB"][:], in1=t_["E"][:], op=ALU.mult),
          reads=[K("rB"), K("E")], writes=[K("rt")])
        A(P, "dve", lambda e: e.tensor_tensor(out=t_["u1"][:], in0=cw[:], in1=t_["s"][:], op=ALU.subtract),
          reads=[cwk, K("s")], writes=[K("u1")])
        A(P, "act", lambda e: e.activation(out=t_["E"][:], in_=t_["u1"][:], func=AF.Exp, scale=-DSC), reads=[K("u1")], writes=[K("E")])
        A(P, "dve", lambda e: e.scalar_tensor_tensor(out=bft["at"][:], in0=t_["kk"][:], scalar=-1.0, in1=t_["E"][:],
                                                     op0=ALU.mult, op1=ALU.mult), reads=[K("kk"), K("E")], writes=[K("at")])
        A(P, "act", lambda e: e.activation(out=t_["E"][:], in_=cw[:], func=AF.Exp, scale=DSC), reads=[cwk], writes=[K("E")])
        A(P, "dve", lambda e: e.tensor_tensor(out=bft["bt"][:], in0=t_["bb"][:], in1=t_["E"][:], op=ALU.mult),
          reads=[K("bb"), K("E")], writes=[K("bt")])
        A(P, "dve", lambda e: e.tensor_tensor(out=bft["kt"][:], in0=t_["ke"][:], in1=t_["E"][:], op=ALU.mult),
          reads=[K("ke"), K("E")], writes=[K("kt")])
        if d == 0:
            totc = v3(t_["cs"][:])[:, :, 63:64]
        else:
            totc = v3(cw[:])[:, :, 0:1]
        A(P, "dve", lambda e: e.tensor_tensor(out=v3(t_["u1"][:]), in0=totc.to_broadcast([128, nck, 64]), in1=v3(cw[:]),
                                              op=ALU.subtract), reads=[cwk, K("cs")], writes=[K("u1")])
        A(P, "act", lambda e: e.activation(out=t_["E"][:], in_=t_["u1"][:], func=AF.Exp, scale=-DSC), reads=[K("u1")], writes=[K("E")])
        A(P, "dve", lambda e: e.tensor_tensor(out=bft["Bh"][:], in0=t_["bb"][:], in1=t_["E"][:], op=ALU.mult),
          reads=[K("bb"), K("E")], writes=[K("Bh")])
        A(P, "dve", lambda e: e.tensor_tensor(out=bft["Kh"][:], in0=t_["ke"][:], in1=t_["E"][:], op=ALU.mult),
          reads=[K("ke"), K("E")], writes=[K("Kh")])
        A(P, "act", lambda e: e.activation(out=Wt[:].unsqueeze(2), in_=totc, func=AF.Exp, scale=-DSC),
          reads=[cwk, K("cs")], writes=[K("Wt")])
        A(P, "dve", lambda e: e.tensor_tensor(out=Dg[:], in0=g.idst[:].unsqueeze(1).to_broadcast([128, nck, 64]),
                                              in1=Wt[:].unsqueeze(2).to_broadcast([128, nck, 64]), op=ALU.mult),
          reads=["idst", K("Wt")], writes=[K("Dg")])
        for n in ("rt", "at", "bt", "kt", "Bh", "Kh"):
            DMA(P, "sp", bft1[n][:], bft[n][64:128, :], reads=[K(n)], writes=[K(n + "1")])
        DMA(P, "sp", Dg1[:], Dg[64:128, :, :], reads=[K("Dg")], writes=[K("Dg1")])
        yield
        chunks = list(range(nck)) if d == 0 else list(range(nck - 1, -1, -1))
        for ci in chunks:
            cs_ = slice(ci * 64, (ci + 1) * 64)
            tok = slice(bi * NB + ci * 64, bi * NB + (ci + 1) * 64)
            rt, at, bt, kt, Bh, Kh = (bft[n] for n in ("rt", "at", "bt", "kt", "Bh", "Kh"))
            for h2 in range(2):
                hb = h2 * 64; hh = slice(hb, hb + 64)
                mm(P, B[0:64, (h2 * 3 + 0) * 64:(h2 * 3 + 1) * 64], fm("bt", h2, cs_), fm("at", h2, cs_), True, True, [fk("bt", h2), fk("at", h2)], [BKEY])
                mm(P, B[0:64, (h2 * 3 + 1) * 64:(h2 * 3 + 2) * 64], fm("at", h2, cs_), fm("bt", h2, cs_), True, True, [fk("bt", h2), fk("at", h2)], [BKEY])
                mm(P, B[0:64, (h2 * 3 + 2) * 64:(h2 * 3 + 3) * 64], fm("kt", h2, cs_), fm("at", h2, cs_), True, True, [fk("kt", h2), fk("at", h2)], [BKEY])
            A(P, "dve", lambda e: e.tensor_tensor(
                out=L3[:], in0=B[0:64, 0:384].rearrange("p (h m t) -> p h m t", h=2, m=3),
                in1=g.msk3[d][:].unsqueeze(1).to_broadcast([64, 2, 3, 64]), op=ALU.mult),
              reads=[BKEY, "msk"], writes=[K("L3")])
            A(P, "dve", lambda e: e.tensor_tensor(
                out=QT[:], in0=L3[:, :, 0, :], in1=g.tmp_f[0:64, 0:64].unsqueeze(1).to_broadcast([64, 2, 64]), op=ALU.add),
              reads=[K("L3"), "tmp_f"], writes=[K("QT")])
            yield
            for h2 in range(2):
                hb = h2 * 64; hh = slice(hb, hb + 64)
                mm(P, B[0:64, (h2 * 2 + 0) * 64:(h2 * 2 + 1) * 64], fm("bt", h2, cs_), fm("rt", h2, cs_), True, True, [fk("bt", h2), fk("rt", h2)], [BKEY])
                mm(P, B[0:64, (h2 * 2 + 1) * 64:(h2 * 2 + 2) * 64], fm("kt", h2, cs_), fm("rt", h2, cs_), True, True, [fk("kt", h2), fk("rt", h2)], [BKEY])
            A(P, "dve", lambda e: e.tensor_tensor(
                out=M2[:], in0=B[0:64, 0:256].rearrange("p (h m t) -> p h m t", h=2, m=2),
                in1=g.msk2[d][:].unsqueeze(1).to_broadcast([64, 2, 2, 64]), op=ALU.mult),
              reads=[BKEY, "msk"], writes=[K("M2")])
            yield
            xi = 0
            for h2 in range(2):
                hb = h2 * 64; hh = slice(hb, hb + 64)
                mm(P, B[0:64, h2 * 128:h2 * 128 + 64], fm("at", h2, cs_), idb[0:64, 0:64], True, True, [fk("at", h2), "ident_bf"], [BKEY])
                mm(P, B[0:64, h2 * 128 + 64:h2 * 128 + 128], L3[:, h2, 2, :], VB[:, ci, hp * 128 + hb:hp * 128 + hb + 64],
                   True, True, [K("L3"), K("VB")], [BKEY])
                mm(P, B[0:64, 256 + (h2 * 2) * 64:256 + (h2 * 2 + 1) * 64], fm("Bh", h2, cs_), idb[0:64, 0:64], True, True,
                   [fk("Bh", h2), "ident_bf"], [BKEY])
                mm(P, B[0:64, 256 + (h2 * 2 + 1) * 64:256 + (h2 * 2 + 2) * 64], fm("Kh", h2, cs_), idb[0:64, 0:64], True, True,
                   [fk("Kh", h2), "ident_bf"], [BKEY])
            A(P, "act", lambda e: e.copy(out=X[0][:], in_=B[0:64, 0:256].rearrange("p (h x) -> p h x", h=2)),
              reads=[BKEY], writes=[K("X0")])
            A(P, "act", lambda e: e.copy(out=BK[:], in_=B[0:64, 256:512].rearrange("p (h m t) -> p h m t", h=2, m=2)),
              reads=[BKEY], writes=[K("BK")])
            yield
            for lev in range(6):
                Xc = X[xi]; Xn = X[1 - xi]; xck = K("X%d" % xi); xnk = K("X%d" % (1 - xi))
                for h2 in range(2):
                    mm(P, B[0:64, h2 * 128:(h2 + 1) * 128], QT[:, h2, :], Xc[:, h2, :], True, True, [K("QT"), xck], [BKEY])
                    if lev < 5:
                        if lev == 0:
                            Pm = L3[:, h2, 1, :]; PTm = L3[:, h2, 0, :]; pk = [K("L3")]
                        else:
                            Pm = PP[:, h2, 0, :]; PTm = PP[:, h2, 1, :]; pk = [K("PP")]
                        mm(P, B[0:64, 256 + (h2 * 2) * 64:256 + (h2 * 2 + 1) * 64], PTm, Pm, True, True, pk, [BKEY])
                        mm(P, B[0:64, 256 + (h2 * 2 + 1) * 64:256 + (h2 * 2 + 2) * 64], Pm, PTm, True, True, pk, [BKEY])
                A(P, "act", lambda e, Xn=Xn: e.copy(out=Xn[:], in_=B[0:64, 0:256].rearrange("p (h x) -> p h x", h=2)),
                  reads=[BKEY], writes=[xnk])
                if lev < 5:
                    A(P, "act", lambda e: e.copy(out=PP[:], in_=B[0:64, 256:512].rearrange("p (h m t) -> p h m t", h=2, m=2)),
                      reads=[BKEY], writes=[K("PP")])
                    A(P, "dve", lambda e: e.tensor_tensor(
                        out=QT[:], in0=B[0:64, 256:512].rearrange("p (h m t) -> p h m t", h=2, m=2)[:, :, 1, :],
                        in1=g.tmp_f[0:64, 0:64].unsqueeze(1).to_broadcast([64, 2, 64]), op=ALU.add),
                      reads=[BKEY, "tmp_f"], writes=[K("QT")])
                else:
                    for dup in range(2):
                        A(P, "act", lambda e, dup=dup: e.copy(
                            out=U0d[:, :, dup, :], in_=B[0:64, 0:256].rearrange("p (h x) -> p h x", h=2)[:, :, 64:128]),
                          reads=[BKEY], writes=[K("U0d")])
                xi = 1 - xi
                yield
            Xf = X[xi]; xfk = K("X%d" % xi)
            for h2 in range(2):
                hb = h2 * 64; hh = slice(hb, hb + 64)
                mm(P, B[0:64, h2 * 64:(h2 + 1) * 64], Xf[:, h2, 0:64], BK[:, h2, 0, :], True, False, [xfk, K("BK")], [BKEY])
                mm(P, B[0:64, h2 * 64:(h2 + 1) * 64], idb[0:64, 0:64], (Dg if h2 == 0 else Dg1)[0:64, ci, :], False, True, ["ident_bf", K("Dg"), K("Dg1")], [BKEY])
                mm(P, B[0:64, 128 + h2 * 64:128 + (h2 + 1) * 64], BK[:, h2, 0, :], Xf[:, h2, 64:128], True, False, [xfk, K("BK")], [BKEY])
                mm(P, B[0:64, 128 + h2 * 64:128 + (h2 + 1) * 64], BK[:, h2, 1, :], VB[:, ci, hp * 128 + hb:hp * 128 + hb + 64],
                   False, True, [K("BK"), K("VB")], [BKEY])
                mm(P, B[0:64, 256 + h2 * 64:256 + (h2 + 1) * 64], idb[0:64, 0:64], fm("rt", h2, cs_), True, False, ["ident_bf", fk("rt", h2)], [BKEY])
                mm(P, B[0:64, 256 + h2 * 64:256 + (h2 + 1) * 64], Xf[:, h2, 0:64], M2[:, h2, 0, :], False, True, [xfk, K("M2")], [BKEY])
            import os
            S6V = os.environ.get("S6V", "")
            if S6V == "mm":
                yield
                return
            A(P, "act", lambda e: e.copy(out=GT[:], in_=B[0:64, 0:128].rearrange("p (h t) -> p h t", h=2)), reads=[BKEY], writes=[K("GT")])
            if S6V == "gt":
                yield
                return
            A(P, "act", lambda e: e.copy(out=Dsb[:].rearrange("p h t -> p (h t)"), in_=B[0:64, 128:256]), reads=[BKEY], writes=[K("Dsb")])
            if S6V == "dsb":
                yield
                return
            A(P, "act", lambda e: e.copy(out=RhT[:], in_=B[0:64, 256:384].rearrange("p (h t) -> p h t", h=2)), reads=[BKEY], writes=[K("RhT")])
            yield
            Hc = H2[hcur]; Hn = H2[1 - hcur]; hck = K("H2%d" % hcur); hnk = K("H2%d" % (1 - hcur))
            for h2 in range(2):
                o = B[:, 384 + h2 * 64:384 + (h2 + 1) * 64]
                mm(P, o, Hc[:, h2, :, :].rearrange("p a t -> p (a t)"), RhT[:, h2, :], True, False, [hck, K("RhT")], [BKEY])
                mm(P, o, U0d[:, h2, :, :].rearrange("p a t -> p (a t)"), M2[:, h2, 0, :], False, False, [K("U0d"), K("M2")], [BKEY])
                mm(P, o, VB[:, ci, hp * 128:(hp + 1) * 128], M2[:, h2, 1, :], False, True, [K("VB"), K("M2")], [BKEY])
            S7V = os.environ.get("S7V", "")
            if S7V == "mmY":
                yield
                return
            for h2 in range(2):
                mm(P, B[0:64, h2 * 64:(h2 + 1) * 64], GT[:, h2, :], Hc[:, h2, 0, :], True, False, [K("GT"), hck], [BKEY])
                mm(P, B[0:64, h2 * 64:(h2 + 1) * 64], idb[0:64, 0:64], Dsb[:, h2, :], False, True, ["ident_bf", K("Dsb")], [BKEY])
            if S7V == "mmH":
                yield
                return
            for h2 in range(2):
                hh = slice(h2 * 64, (h2 + 1) * 64)
                if first_dir:
                    A(P, "act", lambda e, h2=h2, hh=hh, tok=tok: e.copy(out=yacc[hh, hp, tok], in_=B[hh, 384 + h2 * 64:384 + (h2 + 1) * 64]),
                      reads=[BKEY], writes=[("yacc", hp)])
                else:
                    A(P, "dve", lambda e, h2=h2, hh=hh, tok=tok: e.tensor_tensor(out=yacc[hh, hp, tok], in0=B[hh, 384 + h2 * 64:384 + (h2 + 1) * 64],
                                                                       in1=yacc[hh, hp, tok], op=ALU.add),
                      reads=[BKEY, ("yacc", hp)], writes=[("yacc", hp)])
            if S7V == "ev1":
                yield
                return
            if S7V == "B":
                for dup in range(2):
                    A(P, "act", lambda e, Hn=Hn, dup=dup: e.copy(
                        out=Hn[:, :, dup, :], in_=B[0:64, 128:256].rearrange("p (h t) -> p h t", h=2)),
                      reads=[BKEY], writes=[hnk])
            elif S7V == "D":
                A(P, "act", lambda e: e.copy(out=RhT[:], in_=B[0:64, 256:384].rearrange("p (h t) -> p h t", h=2)), reads=[BKEY], writes=[K("RhT")])
            elif S7V == "C":
                A(P, "act", lambda e, Hn=Hn: e.copy(
                    out=Hn[:, 0, :, :].rearrange("p a t -> p (a t)"), in_=B[0:64, 0:128]),
                  reads=[BKEY], writes=[hnk])
            else:
              for dup in range(2):
                A(P, "act", lambda e, Hn=Hn, dup=dup: e.copy(
                    out=Hn[:, :, dup, :], in_=B[0:64, 0:128].rearrange("p (h t) -> p h t", h=2)),
                  reads=[BKEY], writes=[hnk])
            hcur = 1 - hcur
            yield


def rowsum_rstd(g, P, srcs, skeys, ssq, eps, key):
    junk = g.junk
    for i, (src, sk) in enumerate(zip(srcs, skeys)):
        P.add("act", lambda e, src=src, i=i: e.activation(out=junk[:, 0:src.shape[1]], in_=src, func=AF.Square,
                                                         accum_out=ssq[:, i:i + 1]),
              reads=[sk], writes=["junk", key + "a%d" % i])
    if len(srcs) == 2:
        P.add("dve", lambda e: e.tensor_tensor(out=ssq[:, 2:3], in0=ssq[:, 0:1], in1=ssq[:, 1:2], op=ALU.add),
              reads=[key + "a0", key + "a1"], writes=[key + "s"])
        tot = ssq[:, 2:3]
    else:
        tot = ssq[:, 0:1]
    P.add("act", lambda e: e.activation(out=ssq[:, 2:3], in_=tot, func=AF.Sqrt, scale=1.0 / D, bias=eps),
          reads=[key + "s", key + "a0"], writes=[key + "q"])
    P.add("dve", lambda e: e.reciprocal(out=ssq[:, 3:4], in_=ssq[:, 2:3]), reads=[key + "q"], writes=[key + "r"])
    return ssq[:, 3:4], key + "r"


def phase3a(g, l, xin):
    nc, P, T = g.nc, g.P, g.T
    with contextlib.ExitStack() as st:
        sb = mk_sb(g, st)
        Wout = sb("Wout", [128, 8, D], BF16)
        gpost = sb("gpost", [128, D], F32)
        mT = [sb("mT%d" % i, [128, 8, 512], BF16) for i in range(2)]
        xt = [sb("x3t%d" % i, [128, D], F32) for i in range(2)]
        tmp = [sb("tmp3_%d" % i, [128, D], F32) for i in range(2)]
        g.junk = sb("junk3", [128, D], BF16)
        ssq = [sb("ssq%d" % i, [128, 4], F32) for i in range(2)]
        for kc in range(8):
            DMA(P, "sp", Wout[:, kc, :], g.sc["wout_bf"][l, kc * 128:(kc + 1) * 128, :], reads=[("bg", "bg%d" % (0 + 4 * l))],
                writes=[("Wout", kc)])
        DMA(P, "sp", gpost[:], g.w["norm_mix_post"][l].partition_broadcast(128), writes=["gpost"], slow=True)
        it = 0
        for w in range(T // 512):
            m = mT[w % 2]; mk = "mT%d" % (w % 2)
            DMA(P, "sp", m[:], g.sc["ymT"][:, w * 512:(w + 1) * 512].rearrange("(c p) t -> p c t", p=128), writes=[mk])
            for s in range(4):
                t0 = w * 512 + s * 128
                i2 = it % 2
                xx = xt[i2]; xk = "x3t%d" % i2
                tt = tmp[i2]; tk = "tmp3_%d" % i2
                sq = ssq[i2]
                pa = g.ps[(it % 3) * 2]; pak = "ps%d" % ((it % 3) * 2)
                pb = g.ps[(it % 3) * 2 + 1]; pbk = "ps%d" % ((it % 3) * 2 + 1)
                it += 1
                DMA(P, "sp", xx[:], xin[t0:t0 + 128, :], writes=[xk])
                for half, (pp, ppk) in enumerate(((pa, pak), (pb, pbk))):
                    for kc in range(8):
                        mm(P, pp[:, :], m[:, kc, s * 128:(s + 1) * 128], Wout[:, kc, half * 512:(half + 1) * 512],
                           kc == 0, kc == 7, [mk, ("Wout", kc)], [ppk])
                rstd, rk = rowsum_rstd(g, P, [pa[:, :], pb[:, :]], [pak, pbk], sq, EPS, "ssq%d" % i2)
                for half, (pp, ppk) in enumerate(((pa, pak), (pb, pbk))):
                    hs = slice(half * 512, (half + 1) * 512)
                    A(P, "dve", lambda e, pp=pp, tt=tt, hs=hs, rstd=rstd: e.scalar_tensor_tensor(
                        out=tt[:, hs], in0=pp[:, :], scalar=rstd, in1=gpost[:, hs], op0=ALU.mult, op1=ALU.mult),
                      reads=[ppk, rk, "gpost"], writes=[tk])
                A(P, "pool", lambda e, tt=tt, xx=xx: e.tensor_tensor(out=tt[:], in0=tt[:], in1=xx[:], op=ALU.add),
                  reads=[tk, xk], writes=[tk])
                DMA(P, "sp", g.sc["xb"][t0:t0 + 128, :], tt[:], reads=[tk])
        P.flush()


def ffn_windows(T):
    wins = []
    pos = 0
    while pos < T:
        s = 0 if pos == 0 else pos - 1
        if s + 512 <= T:
            N = 512
            hi = T if s + N == T else s + N - 1
        else:
            need = T - s
            N = ((need + 127) // 128) * 128
            s = T - N
            hi = T
        wins.append((s, N, pos, hi))
        pos = hi
    return wins


def phase3b(g, l, xout):
    nc, P, T = g.nc, g.P, g.T
    NM = DFF // 128
    with contextlib.ExitStack() as st:
        sb = mk_sb(g, st)
        Wup = sb("Wup", [128, 8, 2 * DFF], BF16)
        Wdn = sb("Wdn", [128, NM, D], BF16)
        gpost = sb("gpost2", [128, D], F32)
        gpre = sb("gpre2", [128, 8], F32)
        fcw = sb("fcw", [128, NM, 3], F32)
        xt = [sb("x4t%d" % i, [128, D], F32) for i in range(2)]
        tmp = sb("tmp4", [128, D], F32)
        g.junk = sb("junk4", [128, D], BF16)
        xs = sb("xs4", [128, D], BF16)
        ssq = [sb("ssq4_%d" % i, [128, 4], F32) for i in range(2)]
        hT = sb("h2T", [128, 8, 512], BF16)
        hid = sb("hidT", [128, NM, 512], BF16)
        gbuf = [sb("gbuf%d" % i, [128, 514], F32) for i in range(2)]
        cv = [sb("cv%d" % i, [128, 512], F32) for i in range(2)]
        ge = [sb("ge%d" % i, [128, 512], BF16) for i in range(2)]
        linsb = [sb("linsb%d" % i, [128, 512], BF16) for i in range(2)]
        for kc in range(8):
            DMA(P, "sp", Wup[:, kc, :], g.sc["wup_bf"][l, kc * 128:(kc + 1) * 128, :], reads=[("bg", "bg%d" % (1 + 4 * l))],
                writes=[("Wup", kc)])
        for m in range(NM):
            DMA(P, "sp", Wdn[:, m, :], g.sc["wdn_bf"][l, m * 128:(m + 1) * 128, :], reads=[("bg", "bg%d" % (2 + 4 * l))],
                writes=[("Wdn", m)])
        DMA(P, "sp", gpost[:], g.w["norm_ffn_post"][l].partition_broadcast(128), writes=["gpost2"], slow=True)
        DMA(P, "sp", gpre[:], g.w["norm_ffn_pre"][l].rearrange("(c p) -> p c", p=128), writes=["gpre2"], slow=True)
        for k in range(3):
            DMA(P, "sp", fcw[:, :, k:k + 1], g.w["ffn_conv"][l, k].rearrange("(m p) -> p m", p=128).unsqueeze(2),
                writes=["fcw"], slow=True)
        for i in range(2):
            A(P, "pool", lambda e, i=i: e.memset(gbuf[i][:], 0.0), writes=["gbuf%d" % i])
        xcnt = 0
        for (s, N, lo, hi) in ffn_windows(T):
            nsub = N // 128
            for i in range(nsub):
                t0 = s + i * 128
                xi = xcnt % 2; xcnt += 1
                xx = xt[xi]; xk = "x4t%d" % xi
                sq = ssq[xi]
                DMA(P, "sp", xx[:], g.sc["xb"][t0:t0 + 128, :], writes=[xk])
                rstd, rk = rowsum_rstd(g, P, [xx[:]], [xk], sq, EPS, "ssq4_%d" % xi)
                A(P, "dve", lambda e, xx=xx, rstd=rstd: e.tensor_scalar(out=xs[:], in0=xx[:], scalar1=rstd, scalar2=None,
                                                                       op0=ALU.mult), reads=[xk, rk], writes=["xs4"])
                for c in range(8):
                    A(P, "pe", lambda e, c=c: e.transpose(out=g.pst[:, c * 128:(c + 1) * 128], in_=xs[:, c * 128:(c + 1) * 128],
                                                          identity=g.ident_bf[:]), reads=["xs4", "ident_bf"], writes=["pst"])
                A(P, "dve", lambda e, i=i: e.tensor_tensor(
                    out=hT[:, :, i * 128:(i + 1) * 128], in0=g.pst[:].rearrange("p (c t) -> p c t", c=8),
                    in1=gpre[:].unsqueeze(2).to_broadcast([128, 8, 128]), op=ALU.mult),
                  reads=["pst", "gpre2"], writes=["h2T"])
            def tail(m):
                b2 = m % 2
                cc = cv[b2]; ck = "cv%d" % b2
                gg = ge[b2]; gek = "ge%d" % b2
                ll = linsb[b2]; lk = "linsb%d" % b2
                A(P, "act", lambda e, cc=cc, gg=gg, N=N: e.activation(out=gg[:, 0:N], in_=cc[:, 0:N], func=AF.Gelu),
                  reads=[ck], writes=[gek])
                A(P, "dve", lambda e, gg=gg, ll=ll, m=m, N=N: e.tensor_tensor(out=hid[:, m, 0:N], in0=ll[:, 0:N], in1=gg[:, 0:N],
                                                                             op=ALU.mult),
                  reads=[lk, gek], writes=[("hid", m)])
            for m in range(NM + 1):
                if m < NM:
                    b2 = m % 2
                    pg = g.ps[b2 * 2]; pgk = "ps%d" % (b2 * 2)
                    pl = g.ps[b2 * 2 + 1]; plk = "ps%d" % (b2 * 2 + 1)
                    for kc in range(8):
                        mm(P, pg[:, 0:N], Wup[:, kc, m * 128:(m + 1) * 128], hT[:, kc, 0:N], kc == 0, kc == 7,
                           [("Wup", kc), "h2T"], [pgk])
                    for kc in range(8):
                        mm(P, pl[:, 0:N], Wup[:, kc, DFF + m * 128:DFF + (m + 1) * 128], hT[:, kc, 0:N], kc == 0, kc == 7,
                           [("Wup", kc), "h2T"], [plk])
                    gb = gbuf[b2]; gk = "gbuf%d" % b2
                    cc = cv[b2]; ck = "cv%d" % b2
                    ll = linsb[b2]; lk = "linsb%d" % b2
                    A(P, "act", lambda e, gb=gb, pg=pg, N=N: e.copy(out=gb[:, 1:N + 1], in_=pg[:, 0:N]), reads=[pgk], writes=[gk])
                    A(P, "act", lambda e, ll=ll, pl=pl, N=N: e.copy(out=ll[:, 0:N], in_=pl[:, 0:N]), reads=[plk], writes=[lk])
                    if N < 512:
                        A(P, "pool", lambda e, gb=gb, N=N: e.memset(gb[:, N + 1:N + 2], 0.0), writes=[gk])
                    A(P, "dve", lambda e, gb=gb, cc=cc, m=m, N=N: e.tensor_scalar(
                        out=cc[:, 0:N], in0=gb[:, 0:N], scalar1=fcw[:, m, 0:1], scalar2=None, op0=ALU.mult),
                      reads=[gk, "fcw"], writes=[ck])
                    for k2 in (1, 2):
                        A(P, "dve", lambda e, gb=gb, cc=cc, m=m, k2=k2, N=N: e.scalar_tensor_tensor(
                            out=cc[:, 0:N], in0=gb[:, k2:N + k2], scalar=fcw[:, m, k2:k2 + 1], in1=cc[:, 0:N],
                            op0=ALU.mult, op1=ALU.add), reads=[gk, "fcw", ck], writes=[ck])
                if m >= 1:
                    tail(m - 1)
            for i in range(nsub):
                t0 = s + i * 128
                a = max(lo, t0) - t0; b = min(hi, t0 + 128) - t0
                if b <= a:
                    continue
                xi = xcnt % 2; xcnt += 1
                xx = xt[xi]; xk = "x4t%d" % xi
                sq = ssq[xi]
                DMA(P, "sp", xx[:], g.sc["xb"][t0:t0 + 128, :], writes=[xk])
                pa = g.ps[4]; pak = "ps4"; pb = g.ps[5]; pbk = "ps5"
                for half, (pp, ppk) in enumerate(((pa, pak), (pb, pbk))):
                    for m in range(NM):
                        mm(P, pp[:, :], hid[:, m, i * 128:(i + 1) * 128], Wdn[:, m, half * 512:(half + 1) * 512],
                           m == 0, m == NM - 1, [("hid", m), ("Wdn", m)], [ppk])
                rstd, rk = rowsum_rstd(g, P, [pa[:, :], pb[:, :]], [pak, pbk], sq, EPS, "ssq4_%d" % xi)
                for half, (pp, ppk) in enumerate(((pa, pak), (pb, pbk))):
                    hs = slice(half * 512, (half + 1) * 512)
                    A(P, "dve", lambda e, pp=pp, hs=hs, rstd=rstd: e.scalar_tensor_tensor(
                        out=tmp[:, hs], in0=pp[:, :], scalar=rstd, in1=gpost[:, hs], op0=ALU.mult, op1=ALU.mult),
                      reads=[ppk, rk, "gpost2"], writes=["tmp4"])
                A(P, "pool", lambda e, xx=xx: e.tensor_tensor(out=tmp[:], in0=tmp[:], in1=xx[:], op=ALU.add),
                  reads=["tmp4", xk], writes=["tmp4"])
                DMA(P, "sp", xout[t0 + a:t0 + b, :], tmp[a:b, :], reads=["tmp4"])
        P.flush()


def rwkv_sweep_bd(g, l, d, hp, sid, st, C_):
    nc, P, T = g.nc, g.P, g.T
    sbq = mk_sb(g, st)
    K = lambda n: "%s_s%d" % (n, sid)
    t_ = {}
    for n in ("rB", "kB", "s", "a", "kkr", "kk", "t1", "ke", "bb", "cs", "cw", "u1", "E"):
        t_[n] = sbq(K(n), [128, NB], F32)
    nck = NB // 64
    loB = sbq(K("loB"), [96, NB], F32)
    AR = sbq(K("AR"), [128, nck, 2, 2, 64], BF16)
    BDf = {n: sbq(K("BD" + n), [128, nck, 2, 64], BF16) for n in ("bt", "kt", "Bh", "Kh")}
    rtp = sbq(K("rtp"), [128, NB], BF16)
    Vst = sbq(K("Vst"), [128, nck, 64], BF16)
    BDV = sbq(K("BDV"), [128, nck, 2, 64], BF16)
    Wt = sbq(K("Wt"), [128, nck], F32)
    LM = sbq(K("LM"), [128, 4, 128], BF16)
    P0 = sbq(K("P0"), [128, 128], BF16)
    QT = sbq(K("QT"), [128, 128], BF16)
    Mst = sbq(K("Mst"), [128, 2, 64], BF16)
    X = [sbq(K("X%d" % i), [128, 128], BF16) for i in range(2)]
    PP = sbq(K("PP"), [128, 2, 128], BF16)
    BKt = sbq(K("BKt"), [128, 2, 128], BF16)
    AU = sbq(K("AU"), [128, 2, 2, 64], BF16)
    BDGT = sbq(K("BDGT"), [128, 128], BF16)
    BDD = sbq(K("BDD"), [128, 2, 64], BF16)
    RhT = sbq(K("RhT"), [128, 64], BF16)
    BDH = [sbq(K("BDH%d" % i), [128, 128], BF16) for i in range(2)]
    B = g.ps[sid]; BKEY = "ps%d" % sid
    PB = g.ps[4 + (sid % 2)]; PBK = "ps%d" % (4 + (sid % 2))
    Wl, w0c, a0c, kkc, kac, omk, yacc = (C_[n] for n in ("Wl", "w0c", "a0c", "kkc", "kac", "omk", "yacc"))
    hs = slice(hp * 128, (hp + 1) * 128)
    idb = g.ident_bf
    I128f = g.tmp_f
    A(P, "pool", lambda e: e.memset(BDH[0][:], 0.0), writes=[K("BDH0")])
    A(P, "pool", lambda e: e.memset(BDV[:], 0.0), writes=[K("BDV")])
    hcur = 0
    nblk = T // NB
    blocks = list(range(nblk)) if d == 0 else list(range(nblk - 1, -1, -1))

    def v3(ap):
        return ap.rearrange("p (c t) -> p c t", t=64)

    def bd_embed(out4, src, E, neg, rkeys, wkey):
        for h2 in range(2):
            sc = (g.hmn if neg else g.hm)[:, h2:h2 + 1]
            A(P, "dve", lambda e, h2=h2, sc=sc: e.scalar_tensor_tensor(
                out=out4[:, :, h2, :], in0=v3(src[:]), scalar=sc, in1=v3(E[:]), op0=ALU.mult, op1=ALU.mult),
              reads=rkeys + ["hm", "hmn"], writes=[wkey])

    for bi in blocks:
        bs = slice(bi * NB, (bi + 1) * NB)
        DMA(P, "sp", t_["rB"][:], g.sc["rT"][hs, bs], writes=[K("rB")])
        DMA(P, "sp", t_["kB"][:], g.sc["kT"][hs, bs], writes=[K("kB")])
        DMA(P, "sp", loB[:], g.sc["loraT"][:, bs], writes=[K("loB")])
        for h2 in range(2):
            src = g.sc["vtok"][bs, hp * 128 + h2 * 64:hp * 128 + (h2 + 1) * 64].rearrange("(n s) f -> s n f", s=64)
            DMA(P, "sp", Vst[h2 * 64:(h2 + 1) * 64, :, :], src, writes=[K("Vst")])
            DMA(P, "sp", BDV[h2 * 64:(h2 + 1) * 64, :, h2, :], src, writes=[K("BDV")])
        mm(P, PB[:, 0:NB], Wl[:, d, hs], loB[:], True, True, ["Wl", K("loB")], [PBK])
        mm(P, PB[:, NB:2 * NB], Wl[:, 2 + d, hs], loB[:], True, True, ["Wl", K("loB")], [PBK])
        A(P, "act", lambda e: e.activation(out=t_["s"][:], in_=PB[:, 0:NB], func=AF.Sigmoid, bias=w0c[:, d, hp:hp + 1]),
          reads=[PBK, "cst"], writes=[K("s")])
        A(P, "act", lambda e: e.activation(out=t_["a"][:], in_=PB[:, NB:2 * NB], func=AF.Sigmoid, bias=a0c[:, d, hp:hp + 1]),
          reads=[PBK, "cst"], writes=[K("a")])
        A(P, "dve", lambda e: e.tensor_scalar(out=t_["kkr"][:], in0=t_["kB"][:], scalar1=kkc[:, hp:hp + 1], scalar2=None,
                                              op0=ALU.mult), reads=[K("kB"), "cst"], writes=[K("kkr")])
        A(P, "act", lambda e: e.activation(out=t_["u1"][:], in_=t_["kkr"][:], func=AF.Square), reads=[K("kkr")], writes=[K("u1")])
        mm(P, PB[:, 0:NB], g.blk1[:], t_["u1"][:], True, True, ["blk1", K("u1")], [PBK])
        A(P, "act", lambda e: e.activation(out=t_["u1"][:], in_=PB[:, 0:NB], func=AF.Sqrt, scale=1.0, bias=1e-12),
          reads=[PBK], writes=[K("u1")])
        A(P, "dve", lambda e: e.reciprocal(out=t_["u1"][:], in_=t_["u1"][:]), reads=[K("u1")], writes=[K("u1")])
        A(P, "dve", lambda e: e.tensor_tensor(out=t_["kk"][:], in0=t_["kkr"][:], in1=t_["u1"][:], op=ALU.mult),
          reads=[K("kkr"), K("u1")], writes=[K("kk")])
        A(P, "dve", lambda e: e.tensor_scalar(out=t_["t1"][:], in0=t_["a"][:], scalar1=kac[:, hp:hp + 1],
                                              scalar2=omk[:, hp:hp + 1], op0=ALU.mult, op1=ALU.add),
          reads=[K("a"), "cst", "omk"], writes=[K("t1")])
        A(P, "pool", lambda e: e.tensor_tensor(out=t_["ke"][:], in0=t_["kB"][:], in1=t_["t1"][:], op=ALU.mult),
          reads=[K("kB"), K("t1")], writes=[K("ke")])
        A(P, "pool", lambda e: e.tensor_tensor(out=t_["bb"][:], in0=t_["kk"][:], in1=t_["a"][:], op=ALU.mult),
          reads=[K("kk"), K("a")], writes=[K("bb")])
        for c in range(nck):
            A(P, "dve", lambda e, c=c: e.tensor_tensor_scan(out=t_["cs"][:, c * 64:(c + 1) * 64], data0=g.ones_f[:, 0:64],
                                                           data1=t_["s"][:, c * 64:(c + 1) * 64], initial=0.0,
                                                           op0=ALU.mult, op1=ALU.add),
              reads=[K("s"), "ones_f"], writes=[K("cs")])
        totb = v3(t_["cs"][:])[:, :, 63:64].to_broadcast([128, nck, 64])
        if d == 0:
            cw = t_["cs"]; cwk = K("cs")
        else:
            cw = t_["cw"]; cwk = K("cw")
            A(P, "pool", lambda e: e.tensor_tensor(out=t_["u1"][:], in0=t_["s"][:], in1=t_["cs"][:], op=ALU.subtract),
              reads=[K("s"), K("cs")], writes=[K("u1")])
            A(P, "dve", lambda e: e.tensor_tensor(out=v3(cw[:]), in0=v3(t_["u1"][:]), in1=totb, op=ALU.add),
              reads=[K("u1"), K("cs")], writes=[cwk])
        A(P, "act", lambda e: e.activation(out=t_["E"][:], in_=cw[:], func=AF.Exp, scale=-DSC), reads=[cwk], writes=[K("E")])
        A(P, "pool", lambda e: e.tensor_tensor(out=rtp[:], in0=t_["rB"][:], in1=t_["E"][:], op=ALU.mult),
          reads=[K("rB"), K("E")], writes=[K("rtp")])
        bd_embed(AR[:, :, 1, :, :], t_["rB"], t_["E"], False, [K("rB"), K("E")], K("AR"))
        A(P, "pool", lambda e: e.tensor_tensor(out=t_["u1"][:], in0=cw[:], in1=t_["s"][:], op=ALU.subtract),
          reads=[cwk, K("s")], writes=[K("u1")])
        A(P, "act", lambda e: e.activation(out=t_["E"][:], in_=t_["u1"][:], func=AF.Exp, scale=-DSC), reads=[K("u1")], writes=[K("E")])
        bd_embed(AR[:, :, 0, :, :], t_["kk"], t_["E"], True, [K("kk"), K("E")], K("AR"))
        A(P, "act", lambda e: e.activation(out=t_["E"][:], in_=cw[:], func=AF.Exp, scale=DSC), reads=[cwk], writes=[K("E")])
        bd_embed(BDf["bt"][:], t_["bb"], t_["E"], False, [K("bb"), K("E")], K("BDbt"))
        bd_embed(BDf["kt"][:], t_["ke"], t_["E"], False, [K("ke"), K("E")], K("BDkt"))
        if d == 0:
            totc = v3(t_["cs"][:])[:, :, 63:64]
        else:
            totc = v3(cw[:])[:, :, 0:1]
        A(P, "dve", lambda e: e.tensor_tensor(out=v3(t_["u1"][:]), in0=totc.to_broadcast([128, nck, 64]), in1=v3(cw[:]),
                                              op=ALU.subtract), reads=[cwk, K("cs")], writes=[K("u1")])
        A(P, "act", lambda e: e.activation(out=t_["E"][:], in_=t_["u1"][:], func=AF.Exp, scale=-DSC), reads=[K("u1")], writes=[K("E")])
        bd_embed(BDf["Bh"][:], t_["bb"], t_["E"], False, [K("bb"), K("E")], K("BDBh"))
        bd_embed(BDf["Kh"][:], t_["ke"], t_["E"], False, [K("ke"), K("E")], K("BDKh"))
        A(P, "act", lambda e: e.activation(out=Wt[:].unsqueeze(2), in_=totc, func=AF.Exp, scale=-DSC),
          reads=[cwk, K("cs")], writes=[K("Wt")])
        yield
        chunks = list(range(nck)) if d == 0 else list(range(nck - 1, -1, -1))
        for ci in chunks:
            cs_ = slice(ci * 64, (ci + 1) * 64)
            tok = slice(bi * NB + ci * 64, bi * NB + (ci + 1) * 64)
            f2 = lambda ap: ap.rearrange("p a t -> p (a t)")
            bdbt = f2(BDf["bt"][:, ci]); bdkt = f2(BDf["kt"][:, ci]); bdBh = f2(BDf["Bh"][:, ci]); bdKh = f2(BDf["Kh"][:, ci])
            bdat = f2(AR[:, ci, 0]); arr = AR[:, ci].rearrange("p k a t -> p (k a t)")
            mm(P, B[:, 0:256], bdbt, arr, True, True, [K("BDbt"), K("AR")], [BKEY])
            mm(P, B[:, 256:512], bdkt, arr, True, True, [K("BDkt"), K("AR")], [BKEY])
            A(P, "dve", lambda e: e.tensor_tensor(out=LM[:], in0=B[:, 0:512].rearrange("p (m t) -> p m t", m=4),
                                                  in1=g.mask4[d][:], op=ALU.mult), reads=[BKEY, "mskbd"], writes=[K("LM")])
            A(P, "pool", lambda e: e.tensor_tensor(out=QT[:], in0=LM[:, 0, :], in1=I128f[:], op=ALU.add),
              reads=[K("LM"), "tmp_f"], writes=[K("QT")])
            A(P, "pool", lambda e: e.tensor_tensor(out=Mst[:], in0=LM[:, 1::2, 0:64], in1=LM[:, 1::2, 64:128], op=ALU.add),
              reads=[K("LM")], writes=[K("Mst")])
            yield
            mm(P, B[:, 0:128], bdat, bdbt, True, True, [K("BDbt"), K("AR")], [BKEY])
            mm(P, B[:, 128:192], bdat, g.idst_bf[:], True, True, [K("AR"), "idst_bf"], [BKEY])
            mm(P, B[:, 192:256], LM[:, 2, :], Vst[:, ci, :], True, True, [K("LM"), K("Vst")], [BKEY])
            mm(P, B[:, 256:384], bdBh, idb[:], True, True, [K("BDBh"), "ident_bf"], [BKEY])
            mm(P, B[:, 384:512], bdKh, idb[:], True, True, [K("BDKh"), "ident_bf"], [BKEY])
            A(P, "dve", lambda e: e.tensor_tensor(out=P0[:], in0=B[:, 0:128], in1=g.maskT[d][:], op=ALU.mult),
              reads=[BKEY, "mskbd"], writes=[K("P0")])
            A(P, "act", lambda e: e.copy(out=X[0][:], in_=B[:, 128:256]), reads=[BKEY], writes=[K("X0")])
            A(P, "act", lambda e: e.copy(out=BKt[:], in_=B[:, 256:512].rearrange("p (m t) -> p m t", m=2)),
              reads=[BKEY], writes=[K("BKt")])
            yield
            xi = 0
            for lev in range(6):
                Xc = X[xi]; Xn = X[1 - xi]; xck = K("X%d" % xi); xnk = K("X%d" % (1 - xi))
                mm(P, B[:, 0:128], QT[:], Xc[:], True, True, [K("QT"), xck], [BKEY])
                if lev < 5:
                    if lev == 0:
                        Pm = P0[:]; PTm = LM[:, 0, :]; pk = [K("P0"), K("LM")]
                    else:
                        Pm = PP[:, 0, :]; PTm = PP[:, 1, :]; pk = [K("PP")]
                    mm(P, B[:, 128:256], PTm, Pm, True, True, pk, [BKEY])
                    mm(P, B[:, 256:384], Pm, PTm, True, True, pk, [BKEY])
                A(P, "act", lambda e, Xn=Xn: e.copy(out=Xn[:], in_=B[:, 0:128]), reads=[BKEY], writes=[xnk])
                if lev < 5:
                    A(P, "act", lambda e: e.copy(out=PP[:], in_=B[:, 128:384].rearrange("p (m t) -> p m t", m=2)),
                      reads=[BKEY], writes=[K("PP")])
                    A(P, "dve", lambda e: e.tensor_tensor(out=QT[:], in0=B[:, 256:384], in1=I128f[:], op=ALU.add),
                      reads=[BKEY, "tmp_f"], writes=[K("QT")])
                else:
                    for kind in range(2):
                        for h2 in range(2):
                            A(P, "dve", lambda e, kind=kind, h2=h2: e.tensor_scalar(
                                out=AU[:, kind, h2, :], in0=B[:, kind * 64:(kind + 1) * 64], scalar1=g.hm[:, h2:h2 + 1],
                                scalar2=None, op0=ALU.mult), reads=[BKEY, "hm"], writes=[K("AU")])
                xi = 1 - xi
                yield
            Xf = X[xi]; xfk = K("X%d" % xi)
            bdA = f2(AU[:, 0]); bdU = f2(AU[:, 1])
            mm(P, B[:, 0:128], bdA, BKt[:, 0, :], True, True, [K("AU"), K("BKt")], [BKEY])
            mm(P, B[:, 128:192], BKt[:, 0, :], Xf[:, 64:128], True, False, [K("BKt"), xfk], [BKEY])
            mm(P, B[:, 128:192], BKt[:, 1, :], Vst[:, ci, :], False, True, [K("BKt"), K("Vst")], [BKEY])
            mm(P, B[:, 192:256], idb[:], rtp[:, cs_], True, False, ["ident_bf", K("rtp")], [BKEY])
            mm(P, B[:, 192:256], bdA, Mst[:, 0, :], False, True, [K("AU"), K("Mst")], [BKEY])
            A(P, "dve", lambda e, ci=ci: e.scalar_tensor_tensor(out=BDGT[:], in0=I128f[:], scalar=Wt[:, ci:ci + 1], in1=B[:, 0:128],
                                                               op0=ALU.mult, op1=ALU.add),
              reads=[BKEY, "tmp_f", K("Wt")], writes=[K("BDGT")])
            for h2 in range(2):
                A(P, "dve", lambda e, h2=h2: e.tensor_scalar(out=BDD[:, h2, :], in0=B[:, 128:192], scalar1=g.hm[:, h2:h2 + 1],
                                                            scalar2=None, op0=ALU.mult), reads=[BKEY, "hm"], writes=[K("BDD")])
            A(P, "dve", lambda e: e.tensor_copy(out=RhT[:], in_=B[:, 192:256]), reads=[BKEY], writes=[K("RhT")])
            yield
            Hc = BDH[hcur]; Hn = BDH[1 - hcur]; hck = K("BDH%d" % hcur); hnk = K("BDH%d" % (1 - hcur))
            mm(P, B[:, 384:448], Hc[:], RhT[:], True, False, [hck, K("RhT")], [BKEY])
            mm(P, B[:, 384:448], bdU, Mst[:, 0, :], False, False, [K("AU"), K("Mst")], [BKEY])
            mm(P, B[:, 384:448], f2(BDV[:, ci]), Mst[:, 1, :], False, True, [K("BDV"), K("Mst")], [BKEY])
            mm(P, B[:, 256:384], BDGT[:], Hc[:], True, False, [K("BDGT"), hck], [BKEY])
            mm(P, B[:, 256:384], idb[:], f2(BDD[:]), False, True, ["ident_bf", K("BDD")], [BKEY])
            A(P, "dve", lambda e, tok=tok: e.tensor_tensor(out=yacc[:, hp, tok], in0=B[:, 384:448], in1=yacc[:, hp, tok], op=ALU.add),
              reads=[BKEY, ("yacc", hp)], writes=[("yacc", hp)])
            A(P, "dve", lambda e, Hn=Hn: e.tensor_copy(out=Hn[:], in_=B[:, 256:384]), reads=[BKEY], writes=[hnk])
            hcur = 1 - hcur
            yield


_NC_CACHE = {}


def kernel(**inputs):
    from concourse.bass_utils import run_bass_kernel_spmd
    x = np.ascontiguousarray(np.asarray(inputs["x"], dtype=np.float32))
    Bn, T, _ = x.shape
    if T not in _NC_CACHE:
        _NC_CACHE[T] = build(T, nlayers=2)
    nc = _NC_CACHE[T]
    in_maps = []
    for b in range(Bn):
        m = {"x": np.ascontiguousarray(x[b])}
        for n, s in PARAMS:
            m[n] = np.ascontiguousarray(np.asarray(inputs[n], dtype=np.float32))
        in_maps.append(m)
    res = run_bass_kernel_spmd(nc, in_maps, core_ids=list(range(Bn)))
    return np.stack([np.asarray(r["y"], dtype=np.float32) for r in res.results], axis=0)
```

```python
import contextlib
import numpy as np
import concourse.bass as bass
import concourse.mybir as mybir

F32 = mybir.dt.float32
BF16 = mybir.dt.bfloat16
AF = mybir.ActivationFunctionType
ALU = mybir.AluOpType
EPOCH = 30000
NDMA_SEM = 10
ENGS = ("pe", "act", "dve", "pool", "sp")


class Op:
    __slots__ = ("eng", "fn", "reads", "writes", "is_dma", "idx", "waits", "signal", "clock",
                 "sem", "count", "dsem", "dcount")

    def __init__(self, eng, fn, reads, writes, is_dma):
        self.eng = eng; self.fn = fn; self.reads = reads; self.writes = writes
        self.is_dma = is_dma; self.waits = []; self.signal = False; self.clock = None
        self.sem = None; self.count = None; self.dsem = None; self.dcount = None


class Prog:
    def __init__(self, nc, stack):
        self.nc = nc
        self.stack = stack
        self.sems = {}
        self.ops = []
        self.last_writer = {}
        self.readers = {}
        self.known = {e: {} for e in ENGS}
        self.n_on = {e: 0 for e in ENGS}
        self.sig_cnt = {e: 0 for e in ENGS}
        self.dma_rr = {e: 0 for e in ENGS}
        self.dma_last = {}
        self.dma_cnt = {}
        self.total_ops = 0
        for e in ENGS:
            for ep in range(3):
                self._sem((e, ep))
        for q in ("sp", "pool", "act"):
            for slot in range(NDMA_SEM):
                self._sem(("d", q, slot))
        for i in range(8):
            self._sem(("d", "pool", "bg%d" % i))
        self.persist = {}
        self.bg_cnt = {}
        self.clear_all()
        nc.all_engine_barrier()

    def add_bg(self, group, fn, reads=()):
        op = Op("pool", fn, tuple(reads), (), True)
        op.idx = self.n_on["pool"]; self.n_on["pool"] += 1
        op.dsem = group
        op.dcount = self.bg_cnt.get(group, 0)
        self.bg_cnt[group] = op.dcount + 1
        op.clock = {("d", "pool", group): op.dcount}
        self.persist[("bg", group)] = op
        self.ops.append(op)
        return op

    def clear_all(self):
        for h in self.sems.values():
            self.nc.gpsimd.sem_clear(h)

    def finish(self):
        self.flush()
        nc = self.nc
        for grp, cnt in self.bg_cnt.items():
            for eng in (nc.tensor, nc.scalar, nc.vector, nc.gpsimd, nc.sync):
                eng.wait_ge(self.sems[("d", "pool", grp)], 16 * cnt)
        self.nc.all_engine_barrier()
        self.clear_all()
        self.nc.all_engine_barrier()

    def _sem(self, key):
        if key not in self.sems:
            assert not getattr(self, "_frozen", False), key
            self.sems[key] = self.stack.enter_context(
                self.nc.semaphore("s_" + "_".join(str(x) for x in key)))
        return self.sems[key]

    def _need(self, op, dep, same_ok):
        if dep is None:
            return
        kn = self.known[op.eng]
        if dep.is_dma:
            key = ("d", dep.eng, dep.dsem)
            if kn.get(key, -1) >= dep.dcount:
                return
        else:
            if dep.eng == op.eng and not same_ok:
                return
            key = dep.eng
            if kn.get(key, -1) >= dep.idx:
                return
        dep.signal = True
        op.waits.append(dep)
        for k, v in dep.clock.items():
            if kn.get(k, -1) < v:
                kn[k] = v

    def add(self, eng, fn, reads=(), writes=(), dma=False):
        op = Op(eng, fn, tuple(reads), tuple(writes), dma)
        op.idx = self.n_on[eng]; self.n_on[eng] += 1
        kn = self.known[eng]
        if dma:
            slot = self.dma_rr[eng] % NDMA_SEM; self.dma_rr[eng] += 1
            op.dsem = slot
            prev = self.dma_last.get((eng, slot))
            op.dcount = self.dma_cnt.get((eng, slot), 0)
            self.dma_cnt[(eng, slot)] = op.dcount + 1
            if prev is not None:
                self._need(op, prev, True)
            self.dma_last[(eng, slot)] = op
        for r in op.reads:
            w = self.last_writer.get(r)
            if w is None:
                w = self.persist.get(r)
            if w is not None:
                self._need(op, w, True)
            if isinstance(r, str) and r.startswith("ps"):
                for rd in self.readers.get(r, ()):
                    if rd.eng != eng:
                        self._need(op, rd, True)
        strict = (eng != "pe")
        for wkey in op.writes:
            w = self.last_writer.get(wkey)
            if w is not None:
                self._need(op, w, w.is_dma or dma or strict)
            for rd in self.readers.get(wkey, ()):
                self._need(op, rd, rd.is_dma or dma or strict)
        for r in op.reads:
            self.readers.setdefault(r, []).append(op)
        for wkey in op.writes:
            self.last_writer[wkey] = op
            self.readers[wkey] = []
        ck = dict(kn)
        if dma:
            ck[("d", eng, op.dsem)] = op.dcount
        else:
            ck[eng] = op.idx
        op.clock = ck
        self.ops.append(op)
        return op

    def flush(self):
        nc = self.nc
        if not self.ops:
            return
        self.total_ops += len(self.ops)
        per = {e: [o for o in self.ops if o.eng == e] for e in ENGS}
        lastc = {}
        for e in ENGS:
            comp = [o for o in per[e] if not o.is_dma]
            if comp:
                comp[-1].signal = True
                lastc[e] = comp[-1]
        for e in ENGS:
            c = self.sig_cnt[e]
            for o in per[e]:
                if o.is_dma:
                    continue
                if o.signal:
                    c += 1
                    o.sem = (e, (c - 1) // EPOCH); o.count = (c - 1) % EPOCH + 1
            self.sig_cnt[e] = c
        dma_final = dict(self.dma_last)
        import os
        if os.environ.get("FW_DEBUG"):
            print("FLUSH sig_cnt", self.sig_cnt, "max dma cnt", max([16 * (v.dcount + 1) for v in dma_final.values()] or [0]), "nops", len(self.ops))
        active = list(ENGS)

        def run(e, eng):
            for o in per[e]:
                for d in o.waits:
                    if d.is_dma:
                        eng.wait_ge(self._sem(("d", d.eng, d.dsem)), 16 * (d.dcount + 1))
                    else:
                        eng.wait_ge(self._sem(d.sem), d.count)
                inst = o.fn(eng)
                if o.is_dma:
                    inst.then_inc(self._sem(("d", o.eng, o.dsem)), 16)
                elif o.signal:
                    inst.then_inc(self._sem(o.sem), 1)
            for (q, slot), last in dma_final.items():
                eng.wait_ge(self._sem(("d", q, slot)), 16 * (last.dcount + 1))
            for e2, lo in lastc.items():
                if e2 != e:
                    eng.wait_ge(self._sem(lo.sem), lo.count)

        with nc.Block() as block:
            dec = {"pe": block.tensor, "act": block.scalar, "dve": block.vector,
                   "pool": block.gpsimd, "sp": block.sync}
            for e in active:
                def body(eng, e=e):
                    run(e, eng)
                dec[e](body)
        self.ops = []
        self.last_writer = {}
        self.readers = {}
        for e in ENGS:
            kn = self.known[e]
            for e2 in ENGS:
                kn[e2] = self.n_on[e2] - 1
            for (q, slot), last in dma_final.items():
                kn[("d", q, slot)] = last.dcount


import contextlib
import numpy as np

D = 1024
DIN = 2912
DFF = 2816
EPS = 1e-6

PARAMS = [("norm_mix_pre", (2, 1024)), ("norm_mix_post", (2, 1024)), ("norm_ffn_pre", (2, 1024)),
          ("norm_ffn_post", (2, 1024)), ("w_in", (2, 1024, 2912)), ("conv_a_w", (2, 3, 256)),
          ("rwkv_w0", (2, 2, 256)), ("rwkv_w_up", (2, 2, 16, 256)), ("rwkv_a0", (2, 2, 256)),
          ("rwkv_a_up", (2, 2, 16, 256)), ("rwkv_g_up", (2, 32, 256)), ("rwkv_k_k", (2, 256)),
          ("rwkv_k_a", (2, 256)), ("rwkv_r_k", (2, 4, 64)), ("rwkv_lnx_w", (2, 256)),
          ("rwkv_lnx_b", (2, 256)), ("na_rpb", (2, 4, 15, 31)), ("sgu_norm", (2, 256)),
          ("sgu_w", (2, 4, 128, 128)), ("sgu_b", (2, 4, 128)), ("merge_gain", (2, 1024)),
          ("w_out", (2, 1024, 1024)), ("ffn_w_up", (2, 1024, 5632)), ("ffn_conv", (2, 3, 2816)),
          ("ffn_w_down", (2, 2816, 1024))]


class Ctx:
    pass


def mk_sb(g, st):
    def sb(name, shape, dt):
        g.uid = getattr(g, "uid", 0) + 1
        return st.enter_context(g.nc.sbuf_tensor("%s_%d" % (name, g.uid), shape, dt))
    return sb


def build(T, nlayers=2, dbg=(), phases=('conv','sgu','na','rwkv','p3')):
    nc = bass.Bass("TRN2", target_bir_lowering=False)
    g = Ctx()
    g.nc = nc; g.T = T; g.dbgnames = dbg; g.phases = phases
    g.x = nc.dram_tensor("x", [T, D], F32, kind="ExternalInput").ap()
    g.w = {n: nc.dram_tensor(n, list(s), F32, kind="ExternalInput").ap() for n, s in PARAMS}
    g.y = nc.dram_tensor("y", [T, D], F32, kind="ExternalOutput").ap()
    def scratch(name, shape, dt):
        kind = "ExternalOutput" if name in dbg else "Internal"
        return nc.dram_tensor(name, shape, dt, kind=kind).ap()
    g.sc = dict(
        pT=scratch("pT", [256, T], F32), cbT=scratch("cbT", [256, T], F32),
        rT=scratch("rT", [256, T], F32), kT=scratch("kT", [256, T], F32), vT=scratch("vT", [256, T], F32),
        loraT=scratch("loraT", [96, T], F32), vtok=scratch("vtok", [T, 256], BF16),
        qnT=scratch("qnT", [256, T], BF16), knT=scratch("knT", [256, T], BF16),
        vntok=scratch("vntok", [T, 256], BF16),
        usT=scratch("usT", [256, T], F32), vstok=scratch("vstok", [T, 256], F32),
        ymT=scratch("ymT", [1024, T], BF16),
        xa=scratch("xa", [T, D], F32), xb=scratch("xb", [T, D], F32),
        wup_bf=scratch("wup_bf", [2, D, 2 * DFF], BF16), wdn_bf=scratch("wdn_bf", [2, DFF, D], BF16),
        wout_bf=scratch("wout_bf", [2, D, D], BF16), win_bf=scratch("win_bf", [2, D, DIN], BF16),
    )
    with contextlib.ExitStack() as st:
        P = Prog(nc, st)
        g.P = P
        g.ident_bf = st.enter_context(nc.sbuf_tensor("ident_bf", [128, 128], BF16))
        g.ones_f = st.enter_context(nc.sbuf_tensor("ones_f", [128, 128], F32))
        g.tmp_f = st.enter_context(nc.sbuf_tensor("tmp_f", [128, 128], F32))
        g.ps = [st.enter_context(nc.psum_tensor("ps%d" % i, [128, 512], F32)) for i in range(7)]
        g.pst = st.enter_context(nc.psum_tensor("pst", [128, 1024], BF16))
        P.add("pool", lambda e: e.memset(g.ones_f[:], 1.0), writes=["ones_f"])
        P.add("pool", lambda e: e.affine_select(out=g.tmp_f[:], in_=g.ones_f[:], pattern=[[1, 128]],
                                                compare_op=ALU.is_equal, fill=0.0, base=0,
                                                channel_multiplier=-1), reads=["ones_f"], writes=["tmp_f"])
        P.add("dve", lambda e: e.tensor_copy(out=g.ident_bf[:], in_=g.tmp_f[:]), reads=["tmp_f"], writes=["ident_bf"])
        P.flush()
        xin = g.x
        g.mg = st.enter_context(nc.sbuf_tensor("mg", [128, 8], F32))
        if "rwkv" in g.phases: rwkv_consts(g, st)
        for l in range(nlayers):
            load_layer_consts(g, l)
            phase1(g, l, xin)
            if l == 0:
                def bgconv(group, dst, src, rows, step):
                    for r0 in range(0, rows, step):
                        P.add_bg(group, lambda e, r0=r0, dst=dst, src=src, step=step: e.dma_start(out=dst[r0:r0 + step, :], in_=src[r0:r0 + step, :]))
                for ll in range(nlayers):
                    bgconv("bg%d" % (0 + 4 * ll), g.sc["wout_bf"][ll], g.w["w_out"][ll], D, 256)
                    bgconv("bg%d" % (1 + 4 * ll), g.sc["wup_bf"][ll], g.w["ffn_w_up"][ll], D, 128)
                    bgconv("bg%d" % (2 + 4 * ll), g.sc["wdn_bf"][ll], g.w["ffn_w_down"][ll], DFF, 256)
                    if ll > 0:
                        bgconv("bg%d" % (3 + 4 * ll), g.sc["win_bf"][ll], g.w["w_in"][ll], D, 128)
            if "conv" in g.phases: phase_conv(g, l)
            if "sgu" in g.phases: phase_sgu(g, l)
            if "na" in g.phases: phase_na(g, l)
            if "rwkv" in g.phases: phase_rwkv(g, l)
            if "p3" in g.phases:
                phase3a(g, l, xin)
                phase3b(g, l, g.y if l == nlayers - 1 else g.sc["xa"])
            P.flush()
            xin = g.sc["xa"]
        P.finish()
    return nc


FM_CHUNKS = (
    ("ch0", 0, 128, "save_ch", None, 0), ("cc0", 512, 128, "mul_ch", "pT", 0),
    ("ch1", 128, 128, "save_ch", None, 0), ("cc1", 640, 128, "mul_ch", "pT", 128),
    ("cb0", 256, 128, "copy", "cbT", 0), ("cb1", 384, 128, "copy", "cbT", 128),
    ("r0", 768, 128, "copy", "rT", 0), ("r1", 896, 128, "copy", "rT", 128),
    ("k0", 1024, 128, "copy", "kT", 0), ("k1", 1152, 128, "copy", "kT", 128),
    ("v0", 1280, 128, "copy", "vT", 0), ("v1", 1408, 128, "copy", "vT", 128),
    ("lora", 1536, 96, "lora", "loraT", 0),
    ("qn0", 1632, 128, "q", "qnT", 0), ("qn1", 1760, 128, "q", "qnT", 128),
    ("kn0", 1888, 128, "copybf", "knT", 0), ("kn1", 2016, 128, "copybf", "knT", 128),
    ("us0", 2400, 128, "gelu", "usT", 0), ("us1", 2528, 128, "gelu", "usT", 128),
)
TM_GROUPS = (("vtok", 1280, "copybf"), ("vntok", 2144, "copybf"), ("vstok", 2656, "gelu"))


def phase1(g, l, xin):
    nc, P, T = g.nc, g.P, g.T
    NW = T // 512
    with contextlib.ExitStack() as st:
        sb = mk_sb(g, st)
        Win = sb("Win", [128, 8, DIN], BF16)
        gpre = sb("gpre", [128, 8], F32)
        xt = [sb("xt%d" % i, [128, D], F32) for i in range(2)]
        junk = sb("junk", [128, D], BF16)
        xs = sb("xs", [128, D], BF16)
        ss = sb("ss", [128, 4], F32)
        hT = [sb("hT%d" % i, [128, 8, 512], BF16) for i in range(2)]
        stg = [sb("stg%d" % i, [128, 512], F32) for i in range(4)]
        stgb = [sb("stgb%d" % i, [128, 512], BF16) for i in range(2)]
        cht = sb("cht", [128, 512], F32)
        win_src = g.w["w_in"]
        for kc in range(8):
            if l == 0:
                def f(e, kc=kc):
                    return e.dma_start(out=Win[:, kc, :], in_=win_src[l, kc * 128:(kc + 1) * 128, :])
                P.add("pool", f, writes=[("Win", kc)], dma=True)
            else:
                DMA(P, "sp", Win[:, kc, :], g.sc["win_bf"][l, kc * 128:(kc + 1) * 128, :], reads=[("bg", "bg%d" % (3 + 4 * l))],
                    writes=[("Win", kc)])
        P.add("sp", lambda e: e.dma_start(out=gpre[:], in_=g.w["norm_mix_pre"][l].rearrange("(c p) -> p c", p=128),
                                          allow_slow_non_contiguous=True), writes=["gpre"], dma=True)
        cnt = dict(stg=0, stgb=0, ps=0, x=0)
        for w in range(NW):
            h = hT[w % 2]; hk = "hT%d" % (w % 2)
            for s in range(4):
                t0 = w * 512 + s * 128
                xi = cnt["x"] % 2; cnt["x"] += 1
                xtile = xt[xi]; xk = "xt%d" % xi
                P.add("sp", lambda e, xtile=xtile, t0=t0: e.dma_start(out=xtile[:], in_=xin[t0:t0 + 128, :]),
                      writes=[xk], dma=True)
                P.add("act", lambda e, xtile=xtile, s=s: e.activation(out=junk[:], in_=xtile[:], func=AF.Square,
                                                                     accum_out=ss[:, 0:1]),
                      reads=[xk], writes=["junk", "ss"])
                P.add("act", lambda e: e.activation(out=ss[:, 1:2], in_=ss[:, 0:1], func=AF.Sqrt,
                                                    scale=1.0 / D, bias=g_eps(g)),
                      reads=["ss"], writes=["ss1"])
                P.add("dve", lambda e: e.reciprocal(out=ss[:, 2:3], in_=ss[:, 1:2]), reads=["ss1"], writes=["ss2"])
                P.add("dve", lambda e, xtile=xtile: e.tensor_scalar(out=xs[:], in0=xtile[:], scalar1=ss[:, 2:3],
                                                                   scalar2=None, op0=ALU.mult),
                      reads=[xk, "ss2"], writes=["xs"])
                for c in range(8):
                    P.add("pe", lambda e, c=c: e.transpose(out=g.pst[:, c * 128:(c + 1) * 128],
                                                           in_=xs[:, c * 128:(c + 1) * 128], identity=g.ident_bf[:]),
                          reads=["xs", "ident_bf"], writes=["pst"])
                P.add("dve", lambda e, h=h, s=s: e.tensor_tensor(
                    out=h[:, :, s * 128:(s + 1) * 128], in0=g.pst[:].rearrange("p (c t) -> p c t", c=8),
                    in1=gpre[:].unsqueeze(2).to_broadcast([128, 8, 128]), op=ALU.mult),
                      reads=["pst", "gpre"], writes=[hk])
            if w == 0 and 'dbg_ss' in g.dbgnames:
                d1 = nc.dram_tensor("dbg_ss", [128, 4], F32, kind="ExternalOutput").ap()
                d2 = nc.dram_tensor("dbg_hT", [128, 8 * 512], BF16, kind="ExternalOutput").ap()
                d3 = nc.dram_tensor("dbg_id", [128, 128], BF16, kind="ExternalOutput").ap()
                d4 = nc.dram_tensor("dbg_xs", [128, 1024], BF16, kind="ExternalOutput").ap()
                P.add("sp", lambda e: e.dma_start(out=d1, in_=ss[:]), reads=["ss", "ss1", "ss2"], dma=True)
                P.add("sp", lambda e, h=h: e.dma_start(out=d2, in_=h[:].rearrange("p c t -> p (c t)")), reads=[hk], dma=True)
                P.add("sp", lambda e: e.dma_start(out=d3, in_=g.ident_bf[:]), reads=["ident_bf"], dma=True)
                P.add("sp", lambda e: e.dma_start(out=d4, in_=xs[:]), reads=["xs"], dma=True)
            tsl = slice(w * 512, (w + 1) * 512)
            for (name, c0, ncol, kind, dst, r0) in FM_CHUNKS:
                pi = cnt["ps"] % 7; cnt["ps"] += 1
                ps = g.ps[pi]; pk = "ps%d" % pi
                for kc in range(8):
                    P.add("pe", lambda e, ps=ps, kc=kc, c0=c0, ncol=ncol, h=h: e.matmul(
                        ps[0:ncol, :], Win[:, kc, c0:c0 + ncol], h[:, kc, :], start=(kc == 0), stop=(kc == 7)),
                          reads=[("Win", kc), hk], writes=[pk])
                if kind == "save_ch":
                    P.add("act", lambda e, ps=ps: e.copy(out=cht[:], in_=ps[:]), reads=[pk], writes=["cht"])
                    continue
                if kind in ("q", "copybf"):
                    si = cnt["stgb"] % 2; cnt["stgb"] += 1
                    so = stgb[si]; sk = "stgb%d" % si
                else:
                    si = cnt["stg"] % 4; cnt["stg"] += 1
                    so = stg[si]; sk = "stg%d" % si
                if kind == "mul_ch":
                    P.add("dve", lambda e, ps=ps, so=so: e.tensor_tensor(out=so[:], in0=ps[:], in1=cht[:], op=ALU.mult),
                          reads=[pk, "cht"], writes=[sk])
                elif kind == "copy":
                    P.add("dve", lambda e, ps=ps, so=so: e.tensor_copy(out=so[:], in_=ps[:]), reads=[pk], writes=[sk])
                elif kind == "copybf":
                    P.add("act", lambda e, ps=ps, so=so: e.copy(out=so[:], in_=ps[:]), reads=[pk], writes=[sk])
                elif kind == "q":
                    P.add("act", lambda e, ps=ps, so=so: e.mul(out=so[:], in_=ps[:], mul=0.125), reads=[pk], writes=[sk])
                elif kind == "gelu":
                    P.add("act", lambda e, ps=ps, so=so: e.activation(out=so[:], in_=ps[:], func=AF.Gelu),
                          reads=[pk], writes=[sk])
                elif kind == "lora":
                    P.add("act", lambda e, ps=ps, so=so: e.activation(out=so[0:32, :], in_=ps[0:32, :], func=AF.Tanh),
                          reads=[pk], writes=[sk])
                    P.add("act", lambda e, ps=ps, so=so: e.copy(out=so[32:64, :], in_=ps[32:64, :]),
                          reads=[pk], writes=[sk])
                    P.add("act", lambda e, ps=ps, so=so: e.activation(out=so[64:96, :], in_=ps[64:96, :], func=AF.Sigmoid),
                          reads=[pk], writes=[sk])
                P.add("pool", lambda e, so=so, dst=dst, r0=r0, ncol=ncol, tsl=tsl: e.dma_start(
                    out=g.sc[dst][r0:r0 + ncol, tsl], in_=so[0:ncol, :]), reads=[sk], dma=True)
            for s in range(4):
                t0 = w * 512 + s * 128
                for (dst, c0, kind) in TM_GROUPS:
                    pi = cnt["ps"] % 7; cnt["ps"] += 1
                    ps = g.ps[pi]; pk = "ps%d" % pi
                    for kc in range(8):
                        P.add("pe", lambda e, ps=ps, kc=kc, c0=c0, s=s, h=h: e.matmul(
                            ps[:, 0:256], h[:, kc, s * 128:(s + 1) * 128], Win[:, kc, c0:c0 + 256],
                            start=(kc == 0), stop=(kc == 7)), reads=[("Win", kc), hk], writes=[pk])
                    if kind == "gelu":
                        si = cnt["stg"] % 4; cnt["stg"] += 1
                        so = stg[si]; sk = "stg%d" % si
                        P.add("act", lambda e, ps=ps, so=so: e.activation(out=so[:, 0:256], in_=ps[:, 0:256], func=AF.Gelu),
                              reads=[pk], writes=[sk])
                    else:
                        si = cnt["stgb"] % 2; cnt["stgb"] += 1
                        so = stgb[si]; sk = "stgb%d" % si
                        P.add("dve", lambda e, ps=ps, so=so: e.tensor_copy(out=so[:, 0:256], in_=ps[:, 0:256]),
                              reads=[pk], writes=[sk])
                    P.add("pool", lambda e, so=so, dst=dst, t0=t0: e.dma_start(
                        out=g.sc[dst][t0:t0 + 128, :], in_=so[:, 0:256]), reads=[sk], dma=True)
        P.flush()


def g_eps(g):
    return EPS


def A(P, eng, fn, reads=(), writes=()):
    return P.add(eng, fn, reads, writes)


def DMA(P, q, out, in_, reads=(), writes=(), slow=False):
    if q == "sp" and not writes:
        q = "pool"
    if slow:
        return P.add(q, lambda e: e.dma_start(out=out, in_=in_, allow_slow_non_contiguous=True), reads, writes, dma=True)
    return P.add(q, lambda e: e.dma_start(out=out, in_=in_), reads, writes, dma=True)


def gnorm(g, l, grp, y0, y1, ykeys, t0, N, W):
    nc, P = g.nc, g.P
    ps = g.ps[6]; pk = "ps6"
    sq = W["gn_sq"]; rin = W["gn_rin"]
    for j, yj in enumerate((y0, y1)):
        P.add("act", lambda e, yj=yj, j=j: e.activation(out=sq[j][:, 0:N], in_=yj, func=AF.Square),
              reads=[ykeys[j]], writes=["gn_sq%d" % j])
    for j in range(2):
        P.add("pe", lambda e, j=j: e.matmul(ps[:, 0:N], g.ones_f[:], sq[j][:, 0:N], start=(j == 0), stop=(j == 1)),
              reads=["gn_sq%d" % j, "ones_f"], writes=[pk])
    P.add("act", lambda e: e.activation(out=rin[:, 0:N], in_=ps[:, 0:N], func=AF.Sqrt, scale=1.0 / 256, bias=EPS),
          reads=[pk], writes=["gn_rin"])
    P.add("dve", lambda e: e.reciprocal(out=rin[:, 0:N], in_=rin[:, 0:N]), reads=["gn_rin"], writes=["gn_rin"])
    for j, yj in enumerate((y0, y1)):
        ob = W["gn_ob"][j]; ok = "gn_ob%d" % j
        c = grp * 2 + j
        P.add("dve", lambda e, yj=yj, ob=ob, c=c: e.scalar_tensor_tensor(
            out=ob[:, 0:N], in0=yj, scalar=g.mg[:, c:c + 1], in1=rin[:, 0:N], op0=ALU.mult, op1=ALU.mult),
              reads=[ykeys[j], "gn_rin", "mg"], writes=[ok])
        DMA(P, "sp", g.sc["ymT"][c * 128:(c + 1) * 128, t0:t0 + N], ob[:, 0:N], reads=[ok])


def gn_alloc(g, st):
    nc = g.nc
    sb = mk_sb(g, st)
    return dict(gn_sq=[sb("gn_sq%d" % j, [128, 512], F32) for j in range(2)], gn_rin=sb("gn_rin", [128, 512], F32),
                gn_ob=[sb("gn_ob%d" % j, [128, 512], BF16) for j in range(2)])


def load_layer_consts(g, l):
    nc, P = g.nc, g.P
    DMA(P, "sp", g.mg[:], g.w["merge_gain"][l].rearrange("(c p) -> p c", p=128), writes=["mg"], slow=True)


def phase_conv(g, l):
    nc, P, T = g.nc, g.P, g.T
    with contextlib.ExitStack() as st:
        sb = mk_sb(g, st)
        W = gn_alloc(g, st)
        cw = sb("cw", [128, 2, 3], F32)
        pp = sb("pp", [128, T + 2], F32)
        cb = sb("cb", [128, T], F32)
        yy = [sb("yc%d" % j, [128, T], F32) for j in range(2)]
        for j in range(2):
            DMA(P, "sp", cw[:, j, :], g.w["conv_a_w"][l][:, j * 128:(j + 1) * 128].rearrange("k p -> p k"),
                writes=["cw"], slow=True)
        for j in range(2):
            A(P, "pool", lambda e: e.memset(pp[:, 0:1], 0.0), writes=["pp"])
            A(P, "pool", lambda e: e.memset(pp[:, T + 1:T + 2], 0.0), writes=["pp"])
            DMA(P, "sp", pp[:, 1:T + 1], g.sc["pT"][j * 128:(j + 1) * 128, :], writes=["pp"])
            DMA(P, "sp", cb[:], g.sc["cbT"][j * 128:(j + 1) * 128, :], writes=["cb"])
            y = yy[j]; yk = "yc%d" % j
            A(P, "dve", lambda e, y=y, j=j: e.tensor_scalar(out=y[:], in0=pp[:, 0:T], scalar1=cw[:, j, 0:1], scalar2=None,
                                                           op0=ALU.mult), reads=["pp", "cw"], writes=[yk])
            for k in (1, 2):
                A(P, "dve", lambda e, y=y, j=j, k=k: e.scalar_tensor_tensor(
                    out=y[:], in0=pp[:, k:T + k], scalar=cw[:, j, k:k + 1], in1=y[:], op0=ALU.mult, op1=ALU.add),
                  reads=["pp", "cw", yk], writes=[yk])
            A(P, "dve", lambda e, y=y: e.tensor_tensor(out=y[:], in0=y[:], in1=cb[:], op=ALU.mult),
              reads=[yk, "cb"], writes=[yk])
        for t0 in range(0, T, 512):
            gnorm(g, l, 0, yy[0][:, t0:t0 + 512], yy[1][:, t0:t0 + 512], ["yc0", "yc1"], t0, 512, W)
        P.flush()


def phase_sgu(g, l):
    nc, P, T = g.nc, g.P, g.T
    with contextlib.ExitStack() as st:
        sb = mk_sb(g, st)
        W = gn_alloc(g, st)
        wraw = sb("wraw", [128, 4, 128], F32)
        wbf = sb("wbf", [128, 4, 128], BF16)
        wsT = sb("wsT", [128, 4, 128], BF16)
        bs = sb("bs", [1, 4, 128], F32)
        sgn = sb("sgn", [128, 256], F32)
        uT = sb("uT", [128, 2, T], F32)
        yy = [sb("ys%d" % j, [128, T], F32) for j in range(2)]
        vt = [sb("vt%d" % i, [128, 256], F32) for i in range(2)]
        st6 = sb("st6", [128, 6], F32)
        mv = sb("mv", [128, 4], F32)
        vc = sb("vc", [128, 256], F32)
        vn = [sb("vn%d" % i, [128, 256], BF16) for i in range(2)]
        DMA(P, "sp", wraw[:], g.w["sgu_w"][l].rearrange("h p q -> p h q"), writes=["wraw"])
        DMA(P, "sp", bs[:], g.w["sgu_b"][l:l + 1], writes=["bs"])
        DMA(P, "sp", sgn[:], g.w["sgu_norm"][l].partition_broadcast(128), writes=["sgn"], slow=True)
        for j in range(2):
            DMA(P, "sp", uT[:, j, :], g.sc["usT"][j * 128:(j + 1) * 128, :], writes=["uT"])
        A(P, "dve", lambda e: e.tensor_copy(out=wbf[:], in_=wraw[:]), reads=["wraw"], writes=["wbf"])
        for h in range(4):
            A(P, "pe", lambda e, h=h: e.transpose(out=g.pst[:, h * 128:(h + 1) * 128], in_=wbf[:, h, :],
                                                  identity=g.ident_bf[:]), reads=["wbf", "ident_bf"], writes=["pst"])
        A(P, "dve", lambda e: e.tensor_copy(out=wsT[:].rearrange("p h q -> p (h q)"), in_=g.pst[:, 0:512]),
          reads=["pst"], writes=["wsT"])
        def sgA(n):
                i = n % 2
                v = vt[i]; vk = "vt%d" % i
                DMA(P, "sp", v[:], g.sc["vstok"][n * 128:(n + 1) * 128, :], writes=[vk])
                A(P, "dve", lambda e, v=v: e.bn_stats(out=st6[:], in_=v[:]), reads=[vk], writes=["st6"])
                A(P, "dve", lambda e: e.bn_aggr(out=mv[:, 0:2], in_=st6[:]), reads=["st6"], writes=["mv"])
                A(P, "act", lambda e: e.activation(out=mv[:, 2:3], in_=mv[:, 1:2], func=AF.Sqrt, scale=1.0, bias=EPS),
                  reads=["mv"], writes=["mv2"])
                A(P, "dve", lambda e: e.reciprocal(out=mv[:, 3:4], in_=mv[:, 2:3]), reads=["mv2"], writes=["mv3"])
                A(P, "dve", lambda e, v=v: e.tensor_scalar(out=vc[:], in0=v[:], scalar1=mv[:, 0:1], scalar2=mv[:, 3:4],
                                                          op0=ALU.subtract, op1=ALU.mult),
                  reads=[vk, "mv", "mv3"], writes=["vc"])
                vnn = vn[i]; vnk = "vn%d" % i
                A(P, "dve", lambda e, vnn=vnn: e.tensor_tensor(out=vnn[:], in0=vc[:], in1=sgn[:], op=ALU.mult),
                  reads=["vc", "sgn"], writes=[vnk])

        def sgB(n):
                i = n % 2
                vnn = vn[i]; vnk = "vn%d" % i
                for hp in range(2):
                    pi = (n * 2 + hp) % 6
                    ps = g.ps[pi]; pk = "ps%d" % pi
                    for e2 in range(2):
                        h = hp * 2 + e2
                        A(P, "pe", lambda e, ps=ps, vnn=vnn, hp=hp, h=h, e2=e2: e.matmul(
                            ps[:, e2 * 128:(e2 + 1) * 128], vnn[:, hp * 128:(hp + 1) * 128], wsT[:, h, :],
                            start=True, stop=False), reads=[vnk, "wsT"], writes=[pk])
                        A(P, "pe", lambda e, ps=ps, h=h, e2=e2: e.matmul(
                            ps[:, e2 * 128:(e2 + 1) * 128], g.ones_f[0:1, :], bs[0:1, h, :],
                            start=False, stop=True), reads=["ones_f", "bs"], writes=[pk])
                    y = yy[hp]; yk = "ys%d" % hp
                    for e2 in range(2):
                        A(P, "dve", lambda e, ps=ps, y=y, hp=hp, e2=e2, n=n: e.tensor_tensor(
                            out=y[e2 * 64:(e2 + 1) * 64, n * 128:(n + 1) * 128],
                            in0=ps[e2 * 64:(e2 + 1) * 64, e2 * 128:(e2 + 1) * 128],
                            in1=uT[e2 * 64:(e2 + 1) * 64, hp, n * 128:(n + 1) * 128], op=ALU.mult),
                          reads=[pk, "uT"], writes=[yk])

        nchunk = T // 128
        for n in range(nchunk + 1):
            if n < nchunk:
                sgA(n)
            if n >= 1:
                sgB(n - 1)
        for t0 in range(0, T, 512):
            gnorm(g, l, 3, yy[0][:, t0:t0 + 512], yy[1][:, t0:t0 + 512], ["ys0", "ys1"], t0, 512, W)
        P.flush()


def na_consts(g, st, l):
    nc, P = g.nc, g.P
    sbp = mk_sb(g, st)
    g.BT = {l: sbp("BT%d" % l, [128, 4, 14, 64], BF16)}
    g.ones_bf = sbp("ones_bf", [128, 128], BF16)
    A(P, "dve", lambda e: e.tensor_copy(out=g.ones_bf[:], in_=g.ones_f[:]), reads=["ones_f"], writes=["ones_bf"])
    with contextlib.ExitStack() as st2:
        sb = mk_sb(g, st2)
        OH = sb("OH", [31, 2, 64, 64], F32)
        madd = sb("madd", [128, 64], F32)
        rpbT = sb("rpbT", [31, 60], F32)
        A(P, "pool", lambda e: e.memset(OH[:], 1.0), writes=["OH"])
        A(P, "pool", lambda e: e.affine_select(out=OH[:], in_=OH[:], pattern=[[0, 2], [1, 64], [-1, 64]],
                                               compare_op=ALU.is_equal, fill=0.0, base=15, channel_multiplier=-1),
          reads=["OH"], writes=["OH"])
        A(P, "pool", lambda e: e.memset(madd[:], 0.0), writes=["madd"])
        for e2 in range(2):
            m = madd[e2 * 64:(e2 + 1) * 64, :]
            sel = lambda ap, pat, base, cm: A(P, "pool", lambda e: e.affine_select(
                out=ap, in_=ap, pattern=pat, compare_op=ALU.is_ge, fill=-100.0, base=base, channel_multiplier=cm),
                                              reads=["madd"], writes=["madd"])
            sel(m[:, 0:8], [[0, 8]], 15, -1)
            sel(m[:, 8:57], [[-1, 49]], 0, 1)
            sel(m[:, 8:57], [[1, 49]], 15, -1)
            sel(m[:, 57:64], [[0, 7]], -48, 1)
        for l in (l,):
            DMA(P, "sp", rpbT[:], g.w["na_rpb"][l].rearrange("h d i -> i (h d)"), writes=["rpbT"], slow=True)
            for cg in range(8):
                pi = cg % 6
                ps = g.ps[pi]; pk = "ps%d" % pi
                for ci in range(8):
                    c = cg * 8 + ci
                    A(P, "pe", lambda e, ps=ps, ci=ci, c=c: e.matmul(
                        ps[:, ci * 60:(ci + 1) * 60], OH[:, :, :, c].rearrange("i e k -> i (e k)"), rpbT[:],
                        start=True, stop=True), reads=["OH", "rpbT"], writes=[pk])
                for e2 in range(2):
                    A(P, "dve", lambda e, ps=ps, e2=e2, cg=cg, l=l: e.tensor_tensor(
                        out=g.BT[l][e2 * 64:(e2 + 1) * 64, :, :, cg * 8:(cg + 1) * 8],
                        in0=ps[e2 * 64:(e2 + 1) * 64, 0:480].rearrange("p (c h d) -> p h d c", c=8, h=4)[:, :, e2:e2 + 14, :],
                        in1=madd[e2 * 64:(e2 + 1) * 64, cg * 8:(cg + 1) * 8].unsqueeze(1).unsqueeze(1).to_broadcast([64, 4, 14, 8]),
                        op=ALU.add), reads=[pk, "madd"], writes=[("BT", l)])
        if "dbg_BT" in g.dbgnames:
            d1 = nc.dram_tensor("dbg_BT", [128, 4 * 14 * 64], BF16, kind="ExternalOutput").ap()
            DMA(P, "sp", d1, g.BT[l][:].rearrange("p h j c -> p (h j c)"), reads=[("BT", l)])
        P.flush()


def phase_na(g, l):
    nc, P, T = g.nc, g.P, g.T
    rows = T // 64
    with contextlib.ExitStack() as st:
        na_consts(g, st, l)
        sb = mk_sb(g, st)
        W = gn_alloc(g, st)
        kn = sb("kn", [128, 2, T], BF16)
        qn = sb("qn", [128, 2, T], BF16)
        Va = sb("Va", [128, T // 128, 256], BF16)
        Vb = sb("Vb", [128, T // 128 - 1, 256], BF16)
        yn = sb("yn", [128, 2, T], F32)
        pT = [sb("pT%d" % i, [128, 256], BF16) for i in range(3)]
        rec = [sb("rec%d" % i, [128, 64], F32) for i in range(2)]
        ssb = [sb("ssb%d" % i, [128, 4, 64], F32) for i in range(3)]
        for j in range(2):
            DMA(P, "sp", kn[:, j, :], g.sc["knT"][j * 128:(j + 1) * 128, :], writes=["kn"])
            DMA(P, "sp", qn[:, j, :], g.sc["qnT"][j * 128:(j + 1) * 128, :], writes=["qn"])
        DMA(P, "sp", Va[:], g.sc["vntok"].rearrange("(n p) f -> p n f", p=128), writes=["Va"])
        DMA(P, "sp", Vb[:], g.sc["vntok"][64:T - 64, :].rearrange("(n p) f -> p n f", p=128), writes=["Vb"])
        items = [(h, r) for h in range(4) for r in range(rows)]

        def bufs(it):
            return (g.ps[(it % 3) * 2], "ps%d" % ((it % 3) * 2), g.ps[(it % 3) * 2 + 1], "ps%d" % ((it % 3) * 2 + 1),
                    pT[it % 3], "pT%d" % (it % 3), rec[it % 2], "rec%d" % (it % 2), ssb[it % 3], "ssb%d" % (it % 3))

        def stageA(it):
            h, r = items[it]
            hp, hb = h // 2, (h % 2) * 64
            rs = min(max(r - 4, 0), rows - 8)
            pa, pak, pb, pbk, pt, ptk, rc, rck, sb_, sbk = bufs(it)
            j0 = rs - r + 7
            for i in range(4):
                kr0 = rs + 2 * i
                A(P, "pe", lambda e, pa=pa, i=i, kr0=kr0, r=r, hp=hp, hb=hb: e.matmul(
                    pa[:, i * 64:(i + 1) * 64], kn[hb:hb + 64, hp, kr0 * 64:kr0 * 64 + 128],
                    qn[hb:hb + 64, hp, r * 64:(r + 1) * 64], start=True, stop=True),
                  reads=["kn", "qn"], writes=[pak])
            A(P, "dve", lambda e, pa=pa, sb_=sb_, h=h, j0=j0: e.tensor_tensor(
                out=sb_[:], in0=pa[:, 0:256].rearrange("p (i q) -> p i q", i=4), in1=g.BT[l][:, h, j0:j0 + 7:2, :],
                op=ALU.add), reads=[pak, ("BT", l)], writes=[sbk])
            A(P, "act", lambda e, sb_=sb_, pt=pt: e.activation(out=pt[:], in_=sb_[:].rearrange("p i q -> p (i q)"), func=AF.Exp),
              reads=[sbk], writes=[ptk])

        def stageB(it):
            h, r = items[it]
            hp, hb = h // 2, (h % 2) * 64
            rs = min(max(r - 4, 0), rows - 8)
            pa, pak, pb, pbk, pt, ptk, rc, rck, sb_, sbk = bufs(it)
            for i in range(4):
                kr0 = rs + 2 * i
                Vt = Va[:, kr0 // 2, hp * 128:(hp + 1) * 128] if kr0 % 2 == 0 else Vb[:, (kr0 - 1) // 2, hp * 128:(hp + 1) * 128]
                A(P, "pe", lambda e, pb=pb, pt=pt, i=i, Vt=Vt: e.matmul(
                    pb[:, 0:64], Vt, pt[:, i * 64:(i + 1) * 64], start=(i == 0), stop=(i == 3)),
                  reads=[ptk, "Va", "Vb"], writes=[pbk])
            A(P, "pe", lambda e, pb=pb, pt=pt: e.matmul(pb[:, 64:320], g.ones_bf[:], pt[:, 0:256], start=True, stop=True),
              reads=[ptk, "ones_bf"], writes=[pbk])
            A(P, "dve", lambda e, pb=pb, rc=rc, hb=hb: e.tensor_reduce(
                out=rc[hb:hb + 64, :], in_=pb[hb:hb + 64, 64:320].rearrange("p (i q) -> p q i", i=4),
                axis=mybir.AxisListType.X, op=ALU.add), reads=[pbk], writes=[rck])
            A(P, "dve", lambda e, rc=rc, hb=hb: e.reciprocal(out=rc[hb:hb + 64, :], in_=rc[hb:hb + 64, :]),
              reads=[rck], writes=[rck])
            A(P, "dve", lambda e, pb=pb, rc=rc, hb=hb, hp=hp, r=r: e.tensor_tensor(
                out=yn[hb:hb + 64, hp, r * 64:(r + 1) * 64], in0=pb[hb:hb + 64, 0:64], in1=rc[hb:hb + 64, :],
                op=ALU.mult), reads=[pbk, rck], writes=["yn"])

        nit = len(items)
        for it in range(nit + 1):
            if it < nit:
                stageA(it)
            if it >= 1:
                stageB(it - 1)
        if "dbg_yn" in g.dbgnames:
            d1 = nc.dram_tensor("dbg_yn", [128, 2 * T], F32, kind="ExternalOutput").ap()
            DMA(P, "sp", d1, yn[:].rearrange("p a t -> p (a t)"), reads=["yn"])
            d2 = nc.dram_tensor("dbg_Va", [128, (T // 128) * 256], BF16, kind="ExternalOutput").ap()
            DMA(P, "sp", d2, Va[:].rearrange("p a t -> p (a t)"), reads=["Va"])
            d3 = nc.dram_tensor("dbg_Vb", [128, (T // 128 - 1) * 256], BF16, kind="ExternalOutput").ap()
            DMA(P, "sp", d3, Vb[:].rearrange("p a t -> p (a t)"), reads=["Vb"])
        for t0 in range(0, T, 512):
            gnorm(g, l, 2, yn[:, 0, t0:t0 + 512], yn[:, 1, t0:t0 + 512], ["yn", "yn"], t0, 512, W)
        P.flush()


DSC = float(np.exp(-0.5))
NB = 256


def mm(P, out, lhsT, rhs, start, stop, reads, writes):
    return P.add("pe", lambda e: e.matmul(out, lhsT, rhs, start=start, stop=stop), reads, writes)


def rwkv_consts(g, st):
    nc, P = g.nc, g.P
    sb = mk_sb(g, st)
    g.blk1 = sb("blk1", [128, 128], F32)
    g.idst = sb("idst", [128, 64], F32)
    g.msk3 = [sb("msk3_%d" % d, [64, 3, 64], F32) for d in range(2)]
    g.msk2 = [sb("msk2_%d" % d, [64, 2, 64], F32) for d in range(2)]
    A(P, "pool", lambda e: e.memset(g.blk1[:], 0.0), writes=["blk1"])
    A(P, "pool", lambda e: e.memset(g.blk1[0:64, 0:64], 1.0), writes=["blk1"])
    A(P, "pool", lambda e: e.memset(g.blk1[64:128, 64:128], 1.0), writes=["blk1"])
    A(P, "dve", lambda e: e.tensor_copy(out=g.idst[0:64, :], in_=g.tmp_f[0:64, 0:64]), reads=["tmp_f"], writes=["idst"])
    A(P, "dve", lambda e: e.tensor_copy(out=g.idst[64:128, :], in_=g.tmp_f[64:128, 64:128]), reads=["tmp_f"], writes=["idst"])

    g.hm = sb("hm", [128, 2], F32)
    g.hmn = sb("hmn", [128, 2], F32)
    g.idst_bf = sb("idst_bf", [128, 64], BF16)
    g.mask4 = [sb("mask4_%d" % d, [128, 4, 128], F32) for d in range(2)]
    g.maskT = [sb("maskT_%d" % d, [128, 128], F32) for d in range(2)]
    A(P, "pool", lambda e: e.memset(g.hm[:], 0.0), writes=["hm"])
    A(P, "pool", lambda e: e.memset(g.hm[0:64, 0:1], 1.0), writes=["hm"])
    A(P, "pool", lambda e: e.memset(g.hm[64:128, 1:2], 1.0), writes=["hm"])
    A(P, "pool", lambda e: e.tensor_scalar(out=g.hmn[:], in0=g.hm[:], scalar1=-1.0, scalar2=None, op0=ALU.mult),
      reads=["hm"], writes=["hmn"])
    A(P, "dve", lambda e: e.tensor_copy(out=g.idst_bf[:], in_=g.idst[:]), reads=["idst"], writes=["idst_bf"])

    def selbd(tile_ap, cm, step, op, key):
        A(P, "pool", lambda e: e.memset(tile_ap, 0.0), writes=[key])
        for h2 in range(2):
            ap = tile_ap[h2 * 64:(h2 + 1) * 64, h2 * 64:(h2 + 1) * 64]
            A(P, "pool", lambda e, ap=ap: e.memset(ap, 1.0), writes=[key])
            A(P, "pool", lambda e, ap=ap: e.affine_select(out=ap, in_=ap, pattern=[[step, 64]], compare_op=op, fill=0.0,
                                                         base=0, channel_multiplier=cm), reads=[key], writes=[key])
    for d in range(2):
        sg = 1 if d == 0 else -1
        selbd(g.mask4[d][:, 0, :], -sg, sg, ALU.is_gt, "mskbd")
        selbd(g.mask4[d][:, 1, :], -sg, sg, ALU.is_ge, "mskbd")
        selbd(g.mask4[d][:, 2, :], -sg, sg, ALU.is_gt, "mskbd")
        selbd(g.mask4[d][:, 3, :], -sg, sg, ALU.is_ge, "mskbd")
        selbd(g.maskT[d][:, :], sg, -sg, ALU.is_gt, "mskbd")

    def sel(ap, cm, step, op, key):
        A(P, "pool", lambda e: e.memset(ap, 1.0), writes=[key])
        A(P, "pool", lambda e: e.affine_select(out=ap, in_=ap, pattern=[[step, 64]], compare_op=op, fill=0.0,
                                               base=0, channel_multiplier=cm), reads=[key], writes=[key])
    for d in range(2):
        sg = 1 if d == 0 else -1
        sel(g.msk3[d][:, 0, :], -sg, sg, ALU.is_gt, "msk")
        sel(g.msk3[d][:, 1, :], sg, -sg, ALU.is_gt, "msk")
        sel(g.msk3[d][:, 2, :], -sg, sg, ALU.is_gt, "msk")
        sel(g.msk2[d][:, 0, :], -sg, sg, ALU.is_ge, "msk")
        sel(g.msk2[d][:, 1, :], -sg, sg, ALU.is_ge, "msk")


def phase_rwkv(g, l):
    nc, P, T = g.nc, g.P, g.T
    with contextlib.ExitStack() as st:
        sb = mk_sb(g, st)
        Wl = sb("Wl", [96, 5, 256], F32)
        w0c = sb("w0c", [128, 2, 2], F32); a0c = sb("a0c", [128, 2, 2], F32)
        kkc = sb("kkc", [128, 2], F32); kac = sb("kac", [128, 2], F32); omk = sb("omk", [128, 2], F32)
        rkc = sb("rkc", [128, 2], F32); lwc = sb("lwc", [128, 2], F32); lbc = sb("lbc", [128, 2], F32)
        yacc = sb("yacc", [128, 2, T], F32)
        A(P, "pool", lambda e: e.memset(Wl[:], 0.0), writes=["Wl"])
        A(P, "pool", lambda e: e.memset(yacc[:], 0.0), writes=[("yacc", 0), ("yacc", 1)])
        for d in range(2):
            DMA(P, "sp", Wl[d * 16:(d + 1) * 16, d, :], g.w["rwkv_w_up"][l, d], writes=["Wl"])
            DMA(P, "sp", Wl[32 + d * 16:32 + (d + 1) * 16, 2 + d, :], g.w["rwkv_a_up"][l, d], writes=["Wl"])
            for hp in range(2):
                DMA(P, "sp", w0c[:, d, hp:hp + 1], g.w["rwkv_w0"][l, d, hp * 128:(hp + 1) * 128].unsqueeze(1), writes=["cst"], slow=True)
                DMA(P, "sp", a0c[:, d, hp:hp + 1], g.w["rwkv_a0"][l, d, hp * 128:(hp + 1) * 128].unsqueeze(1), writes=["cst"], slow=True)
        DMA(P, "sp", Wl[64:96, 4, :], g.w["rwkv_g_up"][l], writes=["Wl"])
        for nm, tl in (("rwkv_k_k", kkc), ("rwkv_k_a", kac), ("rwkv_lnx_w", lwc), ("rwkv_lnx_b", lbc)):
            DMA(P, "sp", tl[:], g.w[nm][l].rearrange("(c p) -> p c", p=128), writes=["cst"], slow=True)
        DMA(P, "sp", rkc[:], g.w["rwkv_r_k"][l].rearrange("(c a) k -> (a k) c", a=2), writes=["cst"], slow=True)
        A(P, "dve", lambda e: e.tensor_scalar(out=omk[:], in0=kac[:], scalar1=-1.0, scalar2=1.0, op0=ALU.mult, op1=ALU.add),
          reads=["cst"], writes=["omk"])
        C_ = dict(Wl=Wl, w0c=w0c, a0c=a0c, kkc=kkc, kac=kac, omk=omk, yacc=yacc)
        gens = []
        sid = 0
        import os
        nsw = int(os.environ.get("NSW", "4"))
        st_outer = st
        st = contextlib.ExitStack()
        st.__enter__()
        for d in range(2):
            for hp in range(2):
                if sid < nsw:
                    gens.append((rwkv_sweep_bd if os.environ.get("RWBD", "1") == "1" else rwkv_sweep)(g, l, d, hp, sid, st, C_))
                sid += 1
        alive = list(gens)
        import os
        lim = int(os.environ.get("RW_LIMIT", "1000000"))
        rounds = 0
        while alive and rounds < lim:
            nxt = []
            for gen in alive:
                try:
                    next(gen)
                    nxt.append(gen)
                except StopIteration:
                    pass
            alive = nxt
            rounds += 1
        P.flush()
        st.close()
        st = st_outer
        W = gn_alloc(g, st)
        if "dbg_yacc" in g.dbgnames:
            d1 = nc.dram_tensor("dbg_yacc", [128, 2 * T], F32, kind="ExternalOutput").ap()
            DMA(P, "sp", d1, yacc[:].rearrange("p a t -> p (a t)"), reads=[("yacc", 0), ("yacc", 1)])
            P.flush()
        if lim < 1000000:
            return
        rB = sb("prB", [128, 512], F32); kB = sb("pkB", [128, 512], F32); vB = sb("pvB", [128, 512], F32)
        loB = sb("ploB", [96, 512], F32)
        u = [sb("pu%d" % i, [128, 512], F32) for i in range(4)]
        for t0 in range(0, T, 512):
            ts = slice(t0, t0 + 512)
            DMA(P, "sp", loB[:], g.sc["loraT"][:, ts], writes=["ploB"])
            for hp in range(2):
                hs = slice(hp * 128, (hp + 1) * 128)
                y = yacc[:, hp, ts]
                DMA(P, "sp", rB[:], g.sc["rT"][hs, ts], writes=["prB"])
                DMA(P, "sp", kB[:], g.sc["kT"][hs, ts], writes=["pkB"])
                DMA(P, "sp", vB[:], g.sc["vT"][hs, ts], writes=["pvB"])
                pa, pb = g.ps[4], g.ps[5]
                mm(P, pa[:], g.blk1[:], y, True, True, ["yacc", "blk1"], ["ps4"])
                A(P, "act", lambda e, y=y: e.activation(out=u[0][:], in_=y, func=AF.Square), reads=["yacc"], writes=["pu0"])
                mm(P, pb[:], g.blk1[:], u[0][:], True, True, ["pu0", "blk1"], ["ps5"])
                A(P, "dve", lambda e: e.tensor_scalar(out=u[1][:], in0=pa[:], scalar1=1.0 / 64, scalar2=None, op0=ALU.mult),
                  reads=["ps4"], writes=["pu1"])
                A(P, "dve", lambda e: e.tensor_tensor(out=u[2][:], in0=u[1][:], in1=u[1][:], op=ALU.mult),
                  reads=["pu1"], writes=["pu2"])
                A(P, "dve", lambda e: e.scalar_tensor_tensor(out=u[2][:], in0=pb[:], scalar=1.0 / 64, in1=u[2][:],
                                                             op0=ALU.mult, op1=ALU.subtract),
                  reads=["ps5", "pu2"], writes=["pu2"])
                A(P, "act", lambda e: e.activation(out=u[2][:], in_=u[2][:], func=AF.Sqrt, scale=1.0, bias=64e-5),
                  reads=["pu2"], writes=["pu2"])
                A(P, "dve", lambda e: e.reciprocal(out=u[2][:], in_=u[2][:]), reads=["pu2"], writes=["pu2"])
                A(P, "dve", lambda e, y=y: e.tensor_tensor(out=u[1][:], in0=y, in1=u[1][:], op=ALU.subtract),
                  reads=["yacc", "pu1"], writes=["pu1"])
                A(P, "dve", lambda e: e.tensor_tensor(out=u[1][:], in0=u[1][:], in1=u[2][:], op=ALU.mult),
                  reads=["pu1", "pu2"], writes=["pu1"])
                A(P, "dve", lambda e, hp=hp: e.tensor_scalar(out=u[1][:], in0=u[1][:], scalar1=lwc[:, hp:hp + 1],
                                                            scalar2=lbc[:, hp:hp + 1], op0=ALU.mult, op1=ALU.add),
                  reads=["pu1", "cst"], writes=["pu1"])
                A(P, "dve", lambda e, hp=hp: e.scalar_tensor_tensor(out=u[3][:], in0=rB[:], scalar=rkc[:, hp:hp + 1],
                                                                   in1=kB[:], op0=ALU.mult, op1=ALU.mult),
                  reads=["prB", "pkB", "cst"], writes=["pu3"])
                mm(P, pa[:], g.blk1[:], u[3][:], True, True, ["pu3", "blk1"], ["ps4"])
                A(P, "dve", lambda e: e.tensor_tensor(out=u[3][:], in0=pa[:], in1=vB[:], op=ALU.mult),
                  reads=["ps4", "pvB"], writes=["pu3"])
                A(P, "dve", lambda e: e.tensor_tensor(out=u[1][:], in0=u[1][:], in1=u[3][:], op=ALU.add),
                  reads=["pu1", "pu3"], writes=["pu1"])
                mm(P, pb[:], Wl[:, 4, hs], loB[:], True, True, ["Wl", "ploB"], ["ps5"])
                A(P, "dve", lambda e, y=y: e.tensor_tensor(out=y, in0=u[1][:], in1=pb[:], op=ALU.mult),
                  reads=["pu1", "ps5"], writes=["yacc"])
            gnorm(g, l, 1, yacc[:, 0, ts], yacc[:, 1, ts], ["yacc", "yacc"], t0, 512, W)
        P.flush()


def rwkv_sweep(g, l, d, hp, sid, st, C_):
    nc, P, T = g.nc, g.P, g.T
    sbq = mk_sb(g, st)
    K = lambda n: "%s_s%d" % (n, sid)
    f32t = {}
    for n in ("rB", "kB", "s", "a", "kkr", "kk", "t1", "ke", "bb", "cs", "cw", "u1", "E"):
        f32t[n] = sbq(K(n), [128, NB], F32)
    loB = sbq(K("loB"), [96, NB], F32)
    bft = {n: sbq(K(n), [128, NB], BF16) for n in ("rt", "at", "bt", "kt", "Bh", "Kh")}
    bft1 = {n: sbq(K(n + "1"), [64, NB], BF16) for n in ("rt", "at", "bt", "kt", "Bh", "Kh")}
    VB = sbq(K("VB"), [64, NB // 64, 256], BF16)
    Wt = sbq(K("Wt"), [128, NB // 64], F32)
    Dg = sbq(K("Dg"), [128, NB // 64, 64], BF16)
    Dg1 = sbq(K("Dg1"), [64, NB // 64, 64], BF16)

    def fm(n, h2, cs_):
        return (bft[n] if h2 == 0 else bft1[n])[0:64, cs_]

    def fk(n, h2):
        return K(n) if h2 == 0 else K(n + "1")
    L3 = sbq(K("L3"), [64, 2, 3, 64], BF16)
    M2 = sbq(K("M2"), [64, 2, 2, 64], BF16)
    QT = sbq(K("QT"), [64, 2, 64], BF16)
    PP = sbq(K("PP"), [64, 2, 2, 64], BF16)
    X = [sbq(K("X%d" % i), [64, 2, 128], BF16) for i in range(2)]
    U0d = sbq(K("U0d"), [64, 2, 2, 64], BF16)
    BK = sbq(K("BK"), [64, 2, 2, 64], BF16)
    GT = sbq(K("GT"), [64, 2, 64], BF16)
    Dsb = sbq(K("Dsb"), [64, 2, 64], BF16)
    RhT = sbq(K("RhT"), [64, 2, 64], BF16)
    H2 = [sbq(K("H2%d" % i), [64, 2, 2, 64], BF16) for i in range(2)]
    B = g.ps[sid]; BKEY = "ps%d" % sid
    PB = g.ps[4 + (sid % 2)]; PBK = "ps%d" % (4 + (sid % 2))
    Wl, w0c, a0c, kkc, kac, omk, yacc = (C_[n] for n in ("Wl", "w0c", "a0c", "kkc", "kac", "omk", "yacc"))
    hs = slice(hp * 128, (hp + 1) * 128)
    idb = g.ident_bf
    t_ = f32t
    nck = NB // 64
    A(P, "pool", lambda e: e.memset(H2[0][:], 0.0), writes=[K("H20")])
    hcur = 0
    nblk = T // NB
    blocks = list(range(nblk)) if d == 0 else list(range(nblk - 1, -1, -1))
    first_dir = False

    def v3(ap):
        return ap.rearrange("p (c t) -> p c t", t=64)

    for bi in blocks:
        bs = slice(bi * NB, (bi + 1) * NB)
        DMA(P, "sp", t_["rB"][:], g.sc["rT"][hs, bs], writes=[K("rB")])
        DMA(P, "sp", t_["kB"][:], g.sc["kT"][hs, bs], writes=[K("kB")])
        DMA(P, "sp", loB[:], g.sc["loraT"][:, bs], writes=[K("loB")])
        DMA(P, "sp", VB[:], g.sc["vtok"][bs, :].rearrange("(n s) f -> s n f", s=64), writes=[K("VB")])
        mm(P, PB[:, 0:NB], Wl[:, d, hs], loB[:], True, True, ["Wl", K("loB")], [PBK])
        mm(P, PB[:, NB:2 * NB], Wl[:, 2 + d, hs], loB[:], True, True, ["Wl", K("loB")], [PBK])
        A(P, "act", lambda e: e.activation(out=t_["s"][:], in_=PB[:, 0:NB], func=AF.Sigmoid, bias=w0c[:, d, hp:hp + 1]),
          reads=[PBK, "cst"], writes=[K("s")])
        A(P, "act", lambda e: e.activation(out=t_["a"][:], in_=PB[:, NB:2 * NB], func=AF.Sigmoid, bias=a0c[:, d, hp:hp + 1]),
          reads=[PBK, "cst"], writes=[K("a")])
        A(P, "dve", lambda e: e.tensor_scalar(out=t_["kkr"][:], in0=t_["kB"][:], scalar1=kkc[:, hp:hp + 1], scalar2=None,
                                              op0=ALU.mult), reads=[K("kB"), "cst"], writes=[K("kkr")])
        A(P, "act", lambda e: e.activation(out=t_["u1"][:], in_=t_["kkr"][:], func=AF.Square), reads=[K("kkr")], writes=[K("u1")])
        mm(P, PB[:, 0:NB], g.blk1[:], t_["u1"][:], True, True, ["blk1", K("u1")], [PBK])
        A(P, "act", lambda e: e.activation(out=t_["u1"][:], in_=PB[:, 0:NB], func=AF.Sqrt, scale=1.0, bias=1e-12),
          reads=[PBK], writes=[K("u1")])
        A(P, "dve", lambda e: e.reciprocal(out=t_["u1"][:], in_=t_["u1"][:]), reads=[K("u1")], writes=[K("u1")])
        A(P, "dve", lambda e: e.tensor_tensor(out=t_["kk"][:], in0=t_["kkr"][:], in1=t_["u1"][:], op=ALU.mult),
          reads=[K("kkr"), K("u1")], writes=[K("kk")])
        A(P, "dve", lambda e: e.tensor_scalar(out=t_["t1"][:], in0=t_["a"][:], scalar1=kac[:, hp:hp + 1],
                                              scalar2=omk[:, hp:hp + 1], op0=ALU.mult, op1=ALU.add),
          reads=[K("a"), "cst", "omk"], writes=[K("t1")])
        A(P, "dve", lambda e: e.tensor_tensor(out=t_["ke"][:], in0=t_["kB"][:], in1=t_["t1"][:], op=ALU.mult),
          reads=[K("kB"), K("t1")], writes=[K("ke")])
        A(P, "dve", lambda e: e.tensor_tensor(out=t_["bb"][:], in0=t_["kk"][:], in1=t_["a"][:], op=ALU.mult),
          reads=[K("kk"), K("a")], writes=[K("bb")])
        for c in range(nck):
            A(P, "dve", lambda e, c=c: e.tensor_tensor_scan(out=t_["cs"][:, c * 64:(c + 1) * 64], data0=g.ones_f[:, 0:64],
                                                           data1=t_["s"][:, c * 64:(c + 1) * 64], initial=0.0,
                                                           op0=ALU.mult, op1=ALU.add),
              reads=[K("s"), "ones_f"], writes=[K("cs")])
        totb = v3(t_["cs"][:])[:, :, 63:64].to_broadcast([128, nck, 64])
        if d == 0:
            cw = t_["cs"]; cwk = K("cs")
        else:
            cw = t_["cw"]; cwk = K("cw")
            A(P, "dve", lambda e: e.tensor_tensor(out=t_["u1"][:], in0=t_["s"][:], in1=t_["cs"][:], op=ALU.subtract),
              reads=[K("s"), K("cs")], writes=[K("u1")])
            A(P, "dve", lambda e: e.tensor_tensor(out=v3(cw[:]), in0=v3(t_["u1"][:]), in1=totb, op=ALU.add),
              reads=[K("u1"), K("cs")], writes=[cwk])
        A(P, "act", lambda e: e.activation(out=t_["E"][:], in_=cw[:], func=AF.Exp, scale=-DSC), reads=[cwk], writes=[K("E")])
        A(P, "dve", lambda e: e.tensor_tensor(out=bft["rt"][:], in0=t_["rB"][:], in1=t_["E"][:], op=ALU.mult),
          reads=[K("rB"), K("E")], writes=[K("rt")])
        A(P, "dve", lambda e: e.tensor_tensor(out=t_["u1"][:], in0=cw[:], in1=t_["s"][:], op=ALU.subtract),
          reads=[cwk, K("s")], writes=[K("u1")])
        A(P, "act", lambda e: e.activation(out=t_["E"][:], in_=t_["u1"][:], func=AF.Exp, scale=-DSC), reads=[K("u1")], writes=[K("E")])
        A(P, "dve", lambda e: e.scalar_tensor_tensor(out=bft["at"][:], in0=t_["kk"][:], scalar=-1.0, in1=t_["E"][:],
                                                     op0=ALU.mult, op1=ALU.mult), reads=[K("kk"), K("E")], writes=[K("at")])
        A(P, "act", lambda e: e.activation(out=t_["E"][:], in_=cw[:], func=AF.Exp, scale=DSC), reads=[cwk], writes=[K("E")])
        A(P, "dve", lambda e: e.tensor_tensor(out=bft["bt"][:], in0=t_["bb"][:], in1=t_["E"][:], op=ALU.mult),
          reads=[K("bb"), K("E")], writes=[K("bt")])
        A(P, "dve", lambda e: e.tensor_tensor(out=bft["kt"][:], in0=t_["ke"][:], in1=t_["E"][:], op=ALU.mult),
          reads=[K("ke"), K("E")], writes=[K("kt")])
        if d == 0:
            totc = v3(t_["cs"][:])[:, :, 63:64]
        else:
            totc = v3(cw[:])[:, :, 0:1]
        A(P, "dve", lambda e: e.tensor_tensor(out=v3(t_["u1"][:]), in0=totc.to_broadcast([128, nck, 64]), in1=v3(cw[:]),
                                              op=ALU.subtract), reads=[cwk, K("cs")], writes=[K("u1")])
        A(P, "act", lambda e: e.activation(out=t_["E"][:], in_=t_["u1"][:], func=AF.Exp, scale=-DSC), reads=[K("u1")], writes=[K("E")])
        A(P, "dve", lambda e: e.tensor_tensor(out=bft["Bh"][:], in0=t_["bb"][:], in1=t_["E"][:], op=ALU.mult),
          reads=[K("bb"), K("E")], writes=[K("Bh")])
        A(P, "dve", lambda e: e.tensor_tensor(out=bft["Kh"][:], in0=t_["ke"][:], in1=t_["E"][:], op=ALU.mult),
          reads=[K("ke"), K("E")], writes=[K("Kh")])
        A(P, "act", lambda e: e.activation(out=Wt[:].unsqueeze(2), in_=totc, func=AF.Exp, scale=-DSC),
          reads=[cwk, K("cs")], writes=[K("Wt")])
        A(P, "dve", lambda e: e.tensor_tensor(out=Dg[:], in0=g.idst[:].unsqueeze(1).to_broadcast([128, nck, 64]),
                                              in1=Wt[:].unsqueeze(2).to_broadcast([128, nck, 64]), op=ALU.mult),
          reads=["idst", K("Wt")], writes=[K("Dg")])
        for n in ("rt", "at", "bt", "kt", "Bh", "Kh"):
            DMA(P, "sp", bft1[n][:], bft[n][64:128, :], reads=[K(n)], writes=[K(n + "1")])
        DMA(P, "sp", Dg1[:], Dg[64:128, :, :], reads=[K("Dg")], writes=[K("Dg1")])
        yield
        chunks = list(range(nck)) if d == 0 else list(range(nck - 1, -1, -1))
        for ci in chunks:
            cs_ = slice(ci * 64, (ci + 1) * 64)
            tok = slice(bi * NB + ci * 64, bi * NB + (ci + 1) * 64)
            rt, at, bt, kt, Bh, Kh = (bft[n] for n in ("rt", "at", "bt", "kt", "Bh", "Kh"))
            for h2 in range(2):
                hb = h2 * 64; hh = slice(hb, hb + 64)
                mm(P, B[0:64, (h2 * 3 + 0) * 64:(h2 * 3 + 1) * 64], fm("bt", h2, cs_), fm("at", h2, cs_), True, True, [fk("bt", h2), fk("at", h2)], [BKEY])
                mm(P, B[0:64, (h2 * 3 + 1) * 64:(h2 * 3 + 2) * 64], fm("at", h2, cs_), fm("bt", h2, cs_), True, True, [fk("bt", h2), fk("at", h2)], [BKEY])
                mm(P, B[0:64, (h2 * 3 + 2) * 64:(h2 * 3 + 3) * 64], fm("kt", h2, cs_), fm("at", h2, cs_), True, True, [fk("kt", h2), fk("at", h2)], [BKEY])
            A(P, "dve", lambda e: e.tensor_tensor(
                out=L3[:], in0=B[0:64, 0:384].rearrange("p (h m t) -> p h m t", h=2, m=3),
                in1=g.msk3[d][:].unsqueeze(1).to_broadcast([64, 2, 3, 64]), op=ALU.mult),
              reads=[BKEY, "msk"], writes=[K("L3")])
            A(P, "dve", lambda e: e.tensor_tensor(
                out=QT[:], in0=L3[:, :, 0, :], in1=g.tmp_f[0:64, 0:64].unsqueeze(1).to_broadcast([64, 2, 64]), op=ALU.add),
              reads=[K("L3"), "tmp_f"], writes=[K("QT")])
            yield
            for h2 in range(2):
                hb = h2 * 64; hh = slice(hb, hb + 64)
                mm(P, B[0:64, (h2 * 2 + 0) * 64:(h2 * 2 + 1) * 64], fm("bt", h2, cs_), fm("rt", h2, cs_), True, True, [fk("bt", h2), fk("rt", h2)], [BKEY])
                mm(P, B[0:64, (h2 * 2 + 1) * 64:(h2 * 2 + 2) * 64], fm("kt", h2, cs_), fm("rt", h2, cs_), True, True, [fk("kt", h2), fk("rt", h2)], [BKEY])
            A(P, "dve", lambda e: e.tensor_tensor(
                out=M2[:], in0=B[0:64, 0:256].rearrange("p (h m t) -> p h m t", h=2, m=2),
                in1=g.msk2[d][:].unsqueeze(1).to_broadcast([64, 2, 2, 64]), op=ALU.mult),
              reads=[BKEY, "msk"], writes=[K("M2")])
            yield
            xi = 0
            for h2 in range(2):
                hb = h2 * 64; hh = slice(hb, hb + 64)
                mm(P, B[0:64, h2 * 128:h2 * 128 + 64], fm("at", h2, cs_), idb[0:64, 0:64], True, True, [fk("at", h2), "ident_bf"], [BKEY])
                mm(P, B[0:64, h2 * 128 + 64:h2 * 128 + 128], L3[:, h2, 2, :], VB[:, ci, hp * 128 + hb:hp * 128 + hb + 64],
                   True, True, [K("L3"), K("VB")], [BKEY])
                mm(P, B[0:64, 256 + (h2 * 2) * 64:256 + (h2 * 2 + 1) * 64], fm("Bh", h2, cs_), idb[0:64, 0:64], True, True,
                   [fk("Bh", h2), "ident_bf"], [BKEY])
                mm(P, B[0:64, 256 + (h2 * 2 + 1) * 64:256 + (h2 * 2 + 2) * 64], fm("Kh", h2, cs_), idb[0:64, 0:64], True, True,
                   [fk("Kh", h2), "ident_bf"], [BKEY])
            A(P, "act", lambda e: e.copy(out=X[0][:], in_=B[0:64, 0:256].rearrange("p (h x) -> p h x", h=2)),
              reads=[BKEY], writes=[K("X0")])
            A(P, "act", lambda e: e.copy(out=BK[:], in_=B[0:64, 256:512].rearrange("p (h m t) -> p h m t", h=2, m=2)),
              reads=[BKEY], writes=[K("BK")])
            yield
            for lev in range(6):
                Xc = X[xi]; Xn = X[1 - xi]; xck = K("X%d" % xi); xnk = K("X%d" % (1 - xi))
                for h2 in range(2):
                    mm(P, B[0:64, h2 * 128:(h2 + 1) * 128], QT[:, h2, :], Xc[:, h2, :], True, True, [K("QT"), xck], [BKEY])
                    if lev < 5:
                        if lev == 0:
                            Pm = L3[:, h2, 1, :]; PTm = L3[:, h2, 0, :]; pk = [K("L3")]
                        else:
                            Pm = PP[:, h2, 0, :]; PTm = PP[:, h2, 1, :]; pk = [K("PP")]
                        mm(P, B[0:64, 256 + (h2 * 2) * 64:256 + (h2 * 2 + 1) * 64], PTm, Pm, True, True, pk, [BKEY])
                        mm(P, B[0:64, 256 + (h2 * 2 + 1) * 64:256 + (h2 * 2 + 2) * 64], Pm, PTm, True, True, pk, [BKEY])
                A(P, "act", lambda e, Xn=Xn: e.copy(out=Xn[:], in_=B[0:64, 0:256].rearrange("p (h x) -> p h x", h=2)),
                  reads=[BKEY], writes=[xnk])
                if lev < 5:
                    A(P, "act", lambda e: e.copy(out=PP[:], in_=B[0:64, 256:512].rearrange("p (h m t) -> p h m t", h=2, m=2)),
                      reads=[BKEY], writes=[K("PP")])
                    A(P, "dve", lambda e: e.tensor_tensor(
                        out=QT[:], in0=B[0:64, 256:512].rearrange("p (h m t) -> p h m t", h=2, m=2)[:, :, 1, :],
                        in1=g.tmp_f[0:64, 0:64].unsqueeze(1).to_broadcast([64, 2, 64]), op=ALU.add),
                      reads=[BKEY, "tmp_f"], writes=[K("QT")])
                else:
                    for dup in range(2):
                        A(P, "act", lambda e, dup=dup: e.copy(
                            out=U0d[:, :, dup, :], in_=B[0:64, 0:256].rearrange("p (h x) -> p h x", h=2)[:, :, 64:128]),
                          reads=[BKEY], writes=[K("U0d")])
                xi = 1 - xi
                yield
            Xf = X[xi]; xfk = K("X%d" % xi)
            for h2 in range(2):
                hb = h2 * 64; hh = slice(hb, hb + 64)
                mm(P, B[0:64, h2 * 64:(h2 + 1) * 64], Xf[:, h2, 0:64], BK[:, h2, 0, :], True, False, [xfk, K("BK")], [BKEY])
                mm(P, B[0:64, h2 * 64:(h2 + 1) * 64], idb[0:64, 0:64], (Dg if h2 == 0 else Dg1)[0:64, ci, :], False, True, ["ident_bf", K("Dg"), K("Dg1")], [BKEY])
                mm(P, B[0:64, 128 + h2 * 64:128 + (h2 + 1) * 64], BK[:, h2, 0, :], Xf[:, h2, 64:128], True, False, [xfk, K("BK")], [BKEY])
                mm(P, B[0:64, 128 + h2 * 64:128 + (h2 + 1) * 64], BK[:, h2, 1, :], VB[:, ci, hp * 128 + hb:hp * 128 + hb + 64],
                   False, True, [K("BK"), K("VB")], [BKEY])
                mm(P, B[0:64, 256 + h2 * 64:256 + (h2 + 1) * 64], idb[0:64, 0:64], fm("rt", h2, cs_), True, False, ["ident_bf", fk("rt", h2)], [BKEY])
                mm(P, B[0:64, 256 + h2 * 64:256 + (h2 + 1) * 64], Xf[:, h2, 0:64], M2[:, h2, 0, :], False, True, [xfk, K("M2")], [BKEY])
            import os
            S6V = os.environ.get("S6V", "")
            if S6V == "mm":
                yield
                return
            A(P, "act", lambda e: e.copy(out=GT[:], in_=B[0:64, 0:128].rearrange("p (h t) -> p h t", h=2)), reads=[BKEY], writes=[K("GT")])
            if S6V == "gt":
                yield
                return
            A(P, "act", lambda e: e.copy(out=Dsb[:].rearrange("p h t -> p (h t)"), in_=B[0:64, 128:256]), reads=[BKEY], writes=[K("Dsb")])
            if S6V == "dsb":
                yield
                return
            A(P, "act", lambda e: e.copy(out=RhT[:], in_=B[0:64, 256:384].rearrange("p (h t) -> p h t", h=2)), reads=[BKEY], writes=[K("RhT")])
            yield
            Hc = H2[hcur]; Hn = H2[1 - hcur]; hck = K("H2%d" % hcur); hnk = K("H2%d" % (1 - hcur))
            for h2 in range(2):
                o = B[:, 384 + h2 * 64:384 + (h2 + 1) * 64]
                mm(P, o, Hc[:, h2, :, :].rearrange("p a t -> p (a t)"), RhT[:, h2, :], True, False, [hck, K("RhT")], [BKEY])
                mm(P, o, U0d[:, h2, :, :].rearrange("p a t -> p (a t)"), M2[:, h2, 0, :], False, False, [K("U0d"), K("M2")], [BKEY])
                mm(P, o, VB[:, ci, hp * 128:(hp + 1) * 128], M2[:, h2, 1, :], False, True, [K("VB"), K("M2")], [BKEY])
            S7V = os.environ.get("S7V", "")
            if S7V == "mmY":
                yield
                return
            for h2 in range(2):
                mm(P, B[0:64, h2 * 64:(h2 + 1) * 64], GT[:, h2, :], Hc[:, h2, 0, :], True, False, [K("GT"), hck], [BKEY])
                mm(P, B[0:64, h2 * 64:(h2 + 1) * 64], idb[0:64, 0:64], Dsb[:, h2, :], False, True, ["ident_bf", K("Dsb")], [BKEY])
            if S7V == "mmH":
                yield
                return
            for h2 in range(2):
                hh = slice(h2 * 64, (h2 + 1) * 64)
                if first_dir:
                    A(P, "act", lambda e, h2=h2, hh=hh, tok=tok: e.copy(out=yacc[hh, hp, tok], in_=B[hh, 384 + h2 * 64:384 + (h2 + 1) * 64]),
                      reads=[BKEY], writes=[("yacc", hp)])
                else:
                    A(P, "dve", lambda e, h2=h2, hh=hh, tok=tok: e.tensor_tensor(out=yacc[hh, hp, tok], in0=B[hh, 384 + h2 * 64:384 + (h2 + 1) * 64],
                                                                       in1=yacc[hh, hp, tok], op=ALU.add),
                      reads=[BKEY, ("yacc", hp)], writes=[("yacc", hp)])
            if S7V == "ev1":
                yield
                return
            if S7V == "B":
                for dup in range(2):
                    A(P, "act", lambda e, Hn=Hn, dup=dup: e.copy(
                        out=Hn[:, :, dup, :], in_=B[0:64, 128:256].rearrange("p (h t) -> p h t", h=2)),
                      reads=[BKEY], writes=[hnk])
            elif S7V == "D":
                A(P, "act", lambda e: e.copy(out=RhT[:], in_=B[0:64, 256:384].rearrange("p (h t) -> p h t", h=2)), reads=[BKEY], writes=[K("RhT")])
            elif S7V == "C":
                A(P, "act", lambda e, Hn=Hn: e.copy(
                    out=Hn[:, 0, :, :].rearrange("p a t -> p (a t)"), in_=B[0:64, 0:128]),
                  reads=[BKEY], writes=[hnk])
            else:
              for dup in range(2):
                A(P, "act", lambda e, Hn=Hn, dup=dup: e.copy(
                    out=Hn[:, :, dup, :], in_=B[0:64, 0:128].rearrange("p (h t) -> p h t", h=2)),
                  reads=[BKEY], writes=[hnk])
            hcur = 1 - hcur
            yield


def rowsum_rstd(g, P, srcs, skeys, ssq, eps, key):
    junk = g.junk
    for i, (src, sk) in enumerate(zip(srcs, skeys)):
        P.add("act", lambda e, src=src, i=i: e.activation(out=junk[:, 0:src.shape[1]], in_=src, func=AF.Square,
                                                         accum_out=ssq[:, i:i + 1]),
              reads=[sk], writes=["junk", key + "a%d" % i])
    if len(srcs) == 2:
        P.add("dve", lambda e: e.tensor_tensor(out=ssq[:, 2:3], in0=ssq[:, 0:1], in1=ssq[:, 1:2], op=ALU.add),
              reads=[key + "a0", key + "a1"], writes=[key + "s"])
        tot = ssq[:, 2:3]
    else:
        tot = ssq[:, 0:1]
    P.add("act", lambda e: e.activation(out=ssq[:, 2:3], in_=tot, func=AF.Sqrt, scale=1.0 / D, bias=eps),
          reads=[key + "s", key + "a0"], writes=[key + "q"])
    P.add("dve", lambda e: e.reciprocal(out=ssq[:, 3:4], in_=ssq[:, 2:3]), reads=[key + "q"], writes=[key + "r"])
    return ssq[:, 3:4], key + "r"


def phase3a(g, l, xin):
    nc, P, T = g.nc, g.P, g.T
    with contextlib.ExitStack() as st:
        sb = mk_sb(g, st)
        Wout = sb("Wout", [128, 8, D], BF16)
        gpost = sb("gpost", [128, D], F32)
        mT = [sb("mT%d" % i, [128, 8, 512], BF16) for i in range(2)]
        xt = [sb("x3t%d" % i, [128, D], F32) for i in range(2)]
        tmp = [sb("tmp3_%d" % i, [128, D], F32) for i in range(2)]
        g.junk = sb("junk3", [128, D], BF16)
        ssq = [sb("ssq%d" % i, [128, 4], F32) for i in range(2)]
        for kc in range(8):
            DMA(P, "sp", Wout[:, kc, :], g.sc["wout_bf"][l, kc * 128:(kc + 1) * 128, :], reads=[("bg", "bg%d" % (0 + 4 * l))],
                writes=[("Wout", kc)])
        DMA(P, "sp", gpost[:], g.w["norm_mix_post"][l].partition_broadcast(128), writes=["gpost"], slow=True)
        items = [(w, s) for w in range(T // 512) for s in range(4)]

        def bufs3(it):
            i2 = it % 2
            return (xt[i2], "x3t%d" % i2, tmp[i2], "tmp3_%d" % i2, ssq[i2], g.ps[(it % 3) * 2], "ps%d" % ((it % 3) * 2),
                    g.ps[(it % 3) * 2 + 1], "ps%d" % ((it % 3) * 2 + 1))
        rst = {}

        def pA(it):
            w, s = items[it]
            m = mT[w % 2]; mk = "mT%d" % (w % 2)
            if s == 0:
                DMA(P, "sp", m[:], g.sc["ymT"][:, w * 512:(w + 1) * 512].rearrange("(c p) t -> p c t", p=128), writes=[mk])
            t0 = w * 512 + s * 128
            xx, xk, tt, tk, sq, pa, pak, pb, pbk = bufs3(it)
            DMA(P, "sp", xx[:], xin[t0:t0 + 128, :], writes=[xk])
            for half, (pp, ppk) in enumerate(((pa, pak), (pb, pbk))):
                for kc in range(8):
                    mm(P, pp[:, :], m[:, kc, s * 128:(s + 1) * 128], Wout[:, kc, half * 512:(half + 1) * 512],
                       kc == 0, kc == 7, [mk, ("Wout", kc)], [ppk])
            rst[it] = rowsum_rstd(g, P, [pa[:, :], pb[:, :]], [pak, pbk], sq, EPS, "ssq%d" % (it % 2))

        def pB(it):
            w, s = items[it]
            t0 = w * 512 + s * 128
            xx, xk, tt, tk, sq, pa, pak, pb, pbk = bufs3(it)
            rstd, rk = rst[it]
            for half, (pp, ppk) in enumerate(((pa, pak), (pb, pbk))):
                hs = slice(half * 512, (half + 1) * 512)
                A(P, "dve", lambda e, pp=pp, tt=tt, hs=hs, rstd=rstd: e.scalar_tensor_tensor(
                    out=tt[:, hs], in0=pp[:, :], scalar=rstd, in1=gpost[:, hs], op0=ALU.mult, op1=ALU.mult),
                  reads=[ppk, rk, "gpost"], writes=[tk])
            A(P, "pool", lambda e, tt=tt, xx=xx: e.tensor_tensor(out=tt[:], in0=tt[:], in1=xx[:], op=ALU.add),
              reads=[tk, xk], writes=[tk])
            DMA(P, "sp", g.sc["xb"][t0:t0 + 128, :], tt[:], reads=[tk])

        nit = len(items)
        for it in range(nit + 1):
            if it < nit:
                pA(it)
            if it >= 1:
                pB(it - 1)
        P.flush()


def ffn_windows(T):
    wins = []
    pos = 0
    while pos < T:
        s = 0 if pos == 0 else pos - 1
        if s + 512 <= T:
            N = 512
            hi = T if s + N == T else s + N - 1
        else:
            need = T - s
            N = ((need + 127) // 128) * 128
            s = T - N
            hi = T
        wins.append((s, N, pos, hi))
        pos = hi
    return wins


def phase3b(g, l, xout):
    nc, P, T = g.nc, g.P, g.T
    NM = DFF // 128
    with contextlib.ExitStack() as st:
        sb = mk_sb(g, st)
        Wup = sb("Wup", [128, 8, 2 * DFF], BF16)
        Wdn = sb("Wdn", [128, NM, D], BF16)
        gpost = sb("gpost2", [128, D], F32)
        gpre = sb("gpre2", [128, 8], F32)
        fcw = sb("fcw", [128, NM, 3], F32)
        xt = [sb("x4t%d" % i, [128, D], F32) for i in range(2)]
        tmp = sb("tmp4", [128, D], F32)
        g.junk = sb("junk4", [128, D], BF16)
        xs = sb("xs4", [128, D], BF16)
        ssq = [sb("ssq4_%d" % i, [128, 4], F32) for i in range(2)]
        hT = sb("h2T", [128, 8, 512], BF16)
        hid = sb("hidT", [128, NM, 512], BF16)
        gbuf = [sb("gbuf%d" % i, [128, 514], F32) for i in range(2)]
        cv = [sb("cv%d" % i, [128, 512], F32) for i in range(2)]
        ge = [sb("ge%d" % i, [128, 512], BF16) for i in range(2)]
        linsb = [sb("linsb%d" % i, [128, 512], BF16) for i in range(2)]
        for kc in range(8):
            DMA(P, "sp", Wup[:, kc, :], g.sc["wup_bf"][l, kc * 128:(kc + 1) * 128, :], reads=[("bg", "bg%d" % (1 + 4 * l))],
                writes=[("Wup", kc)])
        for m in range(NM):
            DMA(P, "sp", Wdn[:, m, :], g.sc["wdn_bf"][l, m * 128:(m + 1) * 128, :], reads=[("bg", "bg%d" % (2 + 4 * l))],
                writes=[("Wdn", m)])
        DMA(P, "sp", gpost[:], g.w["norm_ffn_post"][l].partition_broadcast(128), writes=["gpost2"], slow=True)
        DMA(P, "sp", gpre[:], g.w["norm_ffn_pre"][l].rearrange("(c p) -> p c", p=128), writes=["gpre2"], slow=True)
        for k in range(3):
            DMA(P, "sp", fcw[:, :, k:k + 1], g.w["ffn_conv"][l, k].rearrange("(m p) -> p m", p=128).unsqueeze(2),
                writes=["fcw"], slow=True)
        for i in range(2):
            A(P, "pool", lambda e, i=i: e.memset(gbuf[i][:], 0.0), writes=["gbuf%d" % i])
        xcnt = 0
        for (s, N, lo, hi) in ffn_windows(T):
            nsub = N // 128
            for i in range(nsub):
                t0 = s + i * 128
                xi = xcnt % 2; xcnt += 1
                xx = xt[xi]; xk = "x4t%d" % xi
                sq = ssq[xi]
                DMA(P, "sp", xx[:], g.sc["xb"][t0:t0 + 128, :], writes=[xk])
                rstd, rk = rowsum_rstd(g, P, [xx[:]], [xk], sq, EPS, "ssq4_%d" % xi)
                A(P, "dve", lambda e, xx=xx, rstd=rstd: e.tensor_scalar(out=xs[:], in0=xx[:], scalar1=rstd, scalar2=None,
                                                                       op0=ALU.mult), reads=[xk, rk], writes=["xs4"])
                for c in range(8):
                    A(P, "pe", lambda e, c=c: e.transpose(out=g.pst[:, c * 128:(c + 1) * 128], in_=xs[:, c * 128:(c + 1) * 128],
                                                          identity=g.ident_bf[:]), reads=["xs4", "ident_bf"], writes=["pst"])
                A(P, "dve", lambda e, i=i: e.tensor_tensor(
                    out=hT[:, :, i * 128:(i + 1) * 128], in0=g.pst[:].rearrange("p (c t) -> p c t", c=8),
                    in1=gpre[:].unsqueeze(2).to_broadcast([128, 8, 128]), op=ALU.mult),
                  reads=["pst", "gpre2"], writes=["h2T"])
            def tail(m):
                b2 = m % 2
                cc = cv[b2]; ck = "cv%d" % b2
                gg = ge[b2]; gek = "ge%d" % b2
                ll = linsb[b2]; lk = "linsb%d" % b2
                A(P, "act", lambda e, cc=cc, gg=gg, N=N: e.activation(out=gg[:, 0:N], in_=cc[:, 0:N], func=AF.Gelu),
                  reads=[ck], writes=[gek])
                A(P, "dve", lambda e, gg=gg, ll=ll, m=m, N=N: e.tensor_tensor(out=hid[:, m, 0:N], in0=ll[:, 0:N], in1=gg[:, 0:N],
                                                                             op=ALU.mult),
                  reads=[lk, gek], writes=[("hid", m)])
            for m in range(NM + 1):
                if m < NM:
                    b2 = m % 2
                    pg = g.ps[b2 * 2]; pgk = "ps%d" % (b2 * 2)
                    pl = g.ps[b2 * 2 + 1]; plk = "ps%d" % (b2 * 2 + 1)
                    for kc in range(8):
                        mm(P, pg[:, 0:N], Wup[:, kc, m * 128:(m + 1) * 128], hT[:, kc, 0:N], kc == 0, kc == 7,
                           [("Wup", kc), "h2T"], [pgk])
                    for kc in range(8):
                        mm(P, pl[:, 0:N], Wup[:, kc, DFF + m * 128:DFF + (m + 1) * 128], hT[:, kc, 0:N], kc == 0, kc == 7,
                           [("Wup", kc), "h2T"], [plk])
                    gb = gbuf[b2]; gk = "gbuf%d" % b2
                    cc = cv[b2]; ck = "cv%d" % b2
                    ll = linsb[b2]; lk = "linsb%d" % b2
                    A(P, "act", lambda e, gb=gb, pg=pg, N=N: e.copy(out=gb[:, 1:N + 1], in_=pg[:, 0:N]), reads=[pgk], writes=[gk])
                    A(P, "act", lambda e, ll=ll, pl=pl, N=N: e.copy(out=ll[:, 0:N], in_=pl[:, 0:N]), reads=[plk], writes=[lk])
                    if N < 512:
                        A(P, "pool", lambda e, gb=gb, N=N: e.memset(gb[:, N + 1:N + 2], 0.0), writes=[gk])
                    A(P, "dve", lambda e, gb=gb, cc=cc, m=m, N=N: e.tensor_scalar(
                        out=cc[:, 0:N], in0=gb[:, 0:N], scalar1=fcw[:, m, 0:1], scalar2=None, op0=ALU.mult),
                      reads=[gk, "fcw"], writes=[ck])
                    for k2 in (1, 2):
                        A(P, "dve", lambda e, gb=gb, cc=cc, m=m, k2=k2, N=N: e.scalar_tensor_tensor(
                            out=cc[:, 0:N], in0=gb[:, k2:N + k2], scalar=fcw[:, m, k2:k2 + 1], in1=cc[:, 0:N],
                            op0=ALU.mult, op1=ALU.add), reads=[gk, "fcw", ck], writes=[ck])
                if m >= 1:
                    tail(m - 1)
            for i in range(nsub):
                t0 = s + i * 128
                a = max(lo, t0) - t0; b = min(hi, t0 + 128) - t0
                if b <= a:
                    continue
                xi = xcnt % 2; xcnt += 1
                xx = xt[xi]; xk = "x4t%d" % xi
                sq = ssq[xi]
                DMA(P, "sp", xx[:], g.sc["xb"][t0:t0 + 128, :], writes=[xk])
                pa = g.ps[4]; pak = "ps4"; pb = g.ps[5]; pbk = "ps5"
                for half, (pp, ppk) in enumerate(((pa, pak), (pb, pbk))):
                    for m in range(NM):
                        mm(P, pp[:, :], hid[:, m, i * 128:(i + 1) * 128], Wdn[:, m, half * 512:(half + 1) * 512],
                           m == 0, m == NM - 1, [("hid", m), ("Wdn", m)], [ppk])
                rstd, rk = rowsum_rstd(g, P, [pa[:, :], pb[:, :]], [pak, pbk], sq, EPS, "ssq4_%d" % xi)
                for half, (pp, ppk) in enumerate(((pa, pak), (pb, pbk))):
                    hs = slice(half * 512, (half + 1) * 512)
                    A(P, "dve", lambda e, pp=pp, hs=hs, rstd=rstd: e.scalar_tensor_tensor(
                        out=tmp[:, hs], in0=pp[:, :], scalar=rstd, in1=gpost[:, hs], op0=ALU.mult, op1=ALU.mult),
                      reads=[ppk, rk, "gpost2"], writes=["tmp4"])
                A(P, "pool", lambda e, xx=xx: e.tensor_tensor(out=tmp[:], in0=tmp[:], in1=xx[:], op=ALU.add),
                  reads=["tmp4", xk], writes=["tmp4"])
                DMA(P, "sp", xout[t0 + a:t0 + b, :], tmp[a:b, :], reads=["tmp4"])
        P.flush()


def rwkv_sweep_bd(g, l, d, hp, sid, st, C_):
    nc, P, T = g.nc, g.P, g.T
    sbq = mk_sb(g, st)
    K = lambda n: "%s_s%d" % (n, sid)
    t_ = {}
    for n in ("rB", "kB", "s", "a", "kkr", "kk", "t1", "ke", "bb", "cs", "cw", "u1", "E"):
        t_[n] = sbq(K(n), [128, NB], F32)
    nck = NB // 64
    loB = sbq(K("loB"), [96, NB], F32)
    OUTS = []
    for bsel in range(2):
        KB = (lambda n, bsel=bsel: "%s_s%d_b%d" % (n, sid, bsel))
        OUTS.append(dict(
            KB=KB,
            AR=sbq(KB("AR"), [128, nck, 2, 2, 64], BF16),
            BDf={n: sbq(KB("BD" + n), [128, nck, 2, 64], BF16) for n in ("bt", "kt", "Bh", "Kh")},
            rtp=sbq(KB("rtp"), [128, NB], BF16),
            Vst=sbq(KB("Vst"), [128, nck, 64], BF16),
            BDV=sbq(KB("BDV"), [128, nck, 2, 64], BF16),
            Wt=sbq(KB("Wt"), [128, nck], F32)))
    LM = sbq(K("LM"), [128, 4, 128], BF16)
    P0 = sbq(K("P0"), [128, 128], BF16)
    QT = sbq(K("QT"), [128, 128], BF16)
    Mst = sbq(K("Mst"), [128, 2, 64], BF16)
    X = [sbq(K("X%d" % i), [128, 128], BF16) for i in range(2)]
    PP = sbq(K("PP"), [128, 2, 128], BF16)
    BKt = sbq(K("BKt"), [128, 2, 128], BF16)
    AU = sbq(K("AU"), [128, 2, 2, 64], BF16)
    BDGT = sbq(K("BDGT"), [128, 128], BF16)
    BDD = sbq(K("BDD"), [128, 2, 64], BF16)
    RhT = sbq(K("RhT"), [128, 64], BF16)
    BDH = [sbq(K("BDH%d" % i), [128, 128], BF16) for i in range(2)]
    B = g.ps[sid]; BKEY = "ps%d" % sid
    PB = g.ps[4 + (sid % 2)]; PBK = "ps%d" % (4 + (sid % 2))
    Wl, w0c, a0c, kkc, kac, omk, yacc = (C_[n] for n in ("Wl", "w0c", "a0c", "kkc", "kac", "omk", "yacc"))
    hs = slice(hp * 128, (hp + 1) * 128)
    idb = g.ident_bf
    I128f = g.tmp_f
    A(P, "pool", lambda e: e.memset(BDH[0][:], 0.0), writes=[K("BDH0")])
    for o_ in OUTS:
        A(P, "pool", lambda e, o_=o_: e.memset(o_["BDV"][:], 0.0), writes=[o_["KB"]("BDV")])
    hcur = 0
    nblk = T // NB
    blocks = list(range(nblk)) if d == 0 else list(range(nblk - 1, -1, -1))

    def v3(ap):
        return ap.rearrange("p (c t) -> p c t", t=64)

    def bd_embed(out4, src, E, neg, rkeys, wkey):
        for h2 in range(2):
            sc = (g.hmn if neg else g.hm)[:, h2:h2 + 1]
            A(P, "dve", lambda e, h2=h2, sc=sc: e.scalar_tensor_tensor(
                out=out4[:, :, h2, :], in0=v3(src[:]), scalar=sc, in1=v3(E[:]), op0=ALU.mult, op1=ALU.mult),
              reads=rkeys + ["hm", "hmn"], writes=[wkey])

    def prepass(bi, o):
            KB = o["KB"]; AR = o["AR"]; BDf = o["BDf"]; rtp = o["rtp"]; Vst = o["Vst"]; BDV = o["BDV"]; Wt = o["Wt"]
            bs = slice(bi * NB, (bi + 1) * NB)
            DMA(P, "sp", t_["rB"][:], g.sc["rT"][hs, bs], writes=[K("rB")])
            DMA(P, "sp", t_["kB"][:], g.sc["kT"][hs, bs], writes=[K("kB")])
            DMA(P, "sp", loB[:], g.sc["loraT"][:, bs], writes=[K("loB")])
            for h2 in range(2):
                src = g.sc["vtok"][bs, hp * 128 + h2 * 64:hp * 128 + (h2 + 1) * 64].rearrange("(n s) f -> s n f", s=64)
                DMA(P, "sp", Vst[h2 * 64:(h2 + 1) * 64, :, :], src, writes=[KB("Vst")])
                DMA(P, "sp", BDV[h2 * 64:(h2 + 1) * 64, :, h2, :], src, writes=[KB("BDV")])
            mm(P, PB[:, 0:NB], Wl[:, d, hs], loB[:], True, True, ["Wl", K("loB")], [PBK])
            mm(P, PB[:, NB:2 * NB], Wl[:, 2 + d, hs], loB[:], True, True, ["Wl", K("loB")], [PBK])
            A(P, "act", lambda e: e.activation(out=t_["s"][:], in_=PB[:, 0:NB], func=AF.Sigmoid, bias=w0c[:, d, hp:hp + 1]),
              reads=[PBK, "cst"], writes=[K("s")])
            A(P, "act", lambda e: e.activation(out=t_["a"][:], in_=PB[:, NB:2 * NB], func=AF.Sigmoid, bias=a0c[:, d, hp:hp + 1]),
              reads=[PBK, "cst"], writes=[K("a")])
            A(P, "dve", lambda e: e.tensor_scalar(out=t_["kkr"][:], in0=t_["kB"][:], scalar1=kkc[:, hp:hp + 1], scalar2=None,
                                                  op0=ALU.mult), reads=[K("kB"), "cst"], writes=[K("kkr")])
            A(P, "act", lambda e: e.activation(out=t_["u1"][:], in_=t_["kkr"][:], func=AF.Square), reads=[K("kkr")], writes=[K("u1")])
            mm(P, PB[:, 0:NB], g.blk1[:], t_["u1"][:], True, True, ["blk1", K("u1")], [PBK])
            A(P, "act", lambda e: e.activation(out=t_["u1"][:], in_=PB[:, 0:NB], func=AF.Sqrt, scale=1.0, bias=1e-12),
              reads=[PBK], writes=[K("u1")])
            A(P, "dve", lambda e: e.reciprocal(out=t_["u1"][:], in_=t_["u1"][:]), reads=[K("u1")], writes=[K("u1")])
            A(P, "dve", lambda e: e.tensor_tensor(out=t_["kk"][:], in0=t_["kkr"][:], in1=t_["u1"][:], op=ALU.mult),
              reads=[K("kkr"), K("u1")], writes=[K("kk")])
            yield
            A(P, "dve", lambda e: e.tensor_scalar(out=t_["t1"][:], in0=t_["a"][:], scalar1=kac[:, hp:hp + 1],
                                                  scalar2=omk[:, hp:hp + 1], op0=ALU.mult, op1=ALU.add),
              reads=[K("a"), "cst", "omk"], writes=[K("t1")])
            A(P, "pool", lambda e: e.tensor_tensor(out=t_["ke"][:], in0=t_["kB"][:], in1=t_["t1"][:], op=ALU.mult),
              reads=[K("kB"), K("t1")], writes=[K("ke")])
            A(P, "pool", lambda e: e.tensor_tensor(out=t_["bb"][:], in0=t_["kk"][:], in1=t_["a"][:], op=ALU.mult),
              reads=[K("kk"), K("a")], writes=[K("bb")])
            for c in range(nck):
                A(P, "dve", lambda e, c=c: e.tensor_tensor_scan(out=t_["cs"][:, c * 64:(c + 1) * 64], data0=g.ones_f[:, 0:64],
                                                               data1=t_["s"][:, c * 64:(c + 1) * 64], initial=0.0,
                                                               op0=ALU.mult, op1=ALU.add),
                  reads=[K("s"), "ones_f"], writes=[K("cs")])
            totb = v3(t_["cs"][:])[:, :, 63:64].to_broadcast([128, nck, 64])
            if d == 0:
                cw = t_["cs"]; cwk = K("cs")
            else:
                cw = t_["cw"]; cwk = K("cw")
                A(P, "pool", lambda e: e.tensor_tensor(out=t_["u1"][:], in0=t_["s"][:], in1=t_["cs"][:], op=ALU.subtract),
                  reads=[K("s"), K("cs")], writes=[K("u1")])
                A(P, "dve", lambda e: e.tensor_tensor(out=v3(cw[:]), in0=v3(t_["u1"][:]), in1=totb, op=ALU.add),
                  reads=[K("u1"), K("cs")], writes=[cwk])
            A(P, "act", lambda e: e.activation(out=t_["E"][:], in_=cw[:], func=AF.Exp, scale=-DSC), reads=[cwk], writes=[K("E")])
            A(P, "pool", lambda e: e.tensor_tensor(out=rtp[:], in0=t_["rB"][:], in1=t_["E"][:], op=ALU.mult),
              reads=[K("rB"), K("E")], writes=[KB("rtp")])
            bd_embed(AR[:, :, 1, :, :], t_["rB"], t_["E"], False, [K("rB"), K("E")], KB("AR"))
            yield
            A(P, "pool", lambda e: e.tensor_tensor(out=t_["u1"][:], in0=cw[:], in1=t_["s"][:], op=ALU.subtract),
              reads=[cwk, K("s")], writes=[K("u1")])
            A(P, "act", lambda e: e.activation(out=t_["E"][:], in_=t_["u1"][:], func=AF.Exp, scale=-DSC), reads=[K("u1")], writes=[K("E")])
            bd_embed(AR[:, :, 0, :, :], t_["kk"], t_["E"], True, [K("kk"), K("E")], KB("AR"))
            yield
            A(P, "act", lambda e: e.activation(out=t_["E"][:], in_=cw[:], func=AF.Exp, scale=DSC), reads=[cwk], writes=[K("E")])
            bd_embed(BDf["bt"][:], t_["bb"], t_["E"], False, [K("bb"), K("E")], KB("BDbt"))
            yield
            bd_embed(BDf["kt"][:], t_["ke"], t_["E"], False, [K("ke"), K("E")], KB("BDkt"))
            yield
            if d == 0:
                totc = v3(t_["cs"][:])[:, :, 63:64]
            else:
                totc = v3(cw[:])[:, :, 0:1]
            A(P, "dve", lambda e: e.tensor_tensor(out=v3(t_["u1"][:]), in0=totc.to_broadcast([128, nck, 64]), in1=v3(cw[:]),
                                                  op=ALU.subtract), reads=[cwk, K("cs")], writes=[K("u1")])
            A(P, "act", lambda e: e.activation(out=t_["E"][:], in_=t_["u1"][:], func=AF.Exp, scale=-DSC), reads=[K("u1")], writes=[K("E")])
            bd_embed(BDf["Bh"][:], t_["bb"], t_["E"], False, [K("bb"), K("E")], KB("BDBh"))
            yield
            bd_embed(BDf["Kh"][:], t_["ke"], t_["E"], False, [K("ke"), K("E")], KB("BDKh"))
            yield
            A(P, "act", lambda e: e.activation(out=Wt[:].unsqueeze(2), in_=totc, func=AF.Exp, scale=-DSC),
              reads=[cwk, K("cs")], writes=[KB("Wt")])
            yield

    def drain(pp):
        if pp is not None:
            for _ in pp:
                pass

    def step(pp, n=1):
        if pp is not None:
            for _ in range(n):
                next(pp, None)

    drain(prepass(blocks[0], OUTS[0]))
    yield
    for bidx, bi in enumerate(blocks):
        o = OUTS[bidx % 2]
        KB = o["KB"]; AR = o["AR"]; BDf = o["BDf"]; rtp = o["rtp"]; Vst = o["Vst"]; BDV = o["BDV"]; Wt = o["Wt"]
        pp = prepass(blocks[bidx + 1], OUTS[(bidx + 1) % 2]) if bidx + 1 < len(blocks) else None
        chunks = list(range(nck)) if d == 0 else list(range(nck - 1, -1, -1))
        for ci in chunks:
            cs_ = slice(ci * 64, (ci + 1) * 64)
            tok = slice(bi * NB + ci * 64, bi * NB + (ci + 1) * 64)
            f2 = lambda ap: ap.rearrange("p a t -> p (a t)")
            bdbt = f2(BDf["bt"][:, ci]); bdkt = f2(BDf["kt"][:, ci]); bdBh = f2(BDf["Bh"][:, ci]); bdKh = f2(BDf["Kh"][:, ci])
            bdat = f2(AR[:, ci, 0]); arr = AR[:, ci].rearrange("p k a t -> p (k a t)")
            mm(P, B[:, 0:256], bdbt, arr, True, True, [KB("BDbt"), KB("AR")], [BKEY])
            mm(P, B[:, 256:512], bdkt, arr, True, True, [KB("BDkt"), KB("AR")], [BKEY])
            A(P, "dve", lambda e: e.tensor_tensor(out=LM[:], in0=B[:, 0:512].rearrange("p (m t) -> p m t", m=4),
                                                  in1=g.mask4[d][:], op=ALU.mult), reads=[BKEY, "mskbd"], writes=[K("LM")])
            A(P, "pool", lambda e: e.tensor_tensor(out=QT[:], in0=LM[:, 0, :], in1=I128f[:], op=ALU.add),
              reads=[K("LM"), "tmp_f"], writes=[K("QT")])
            A(P, "pool", lambda e: e.tensor_tensor(out=Mst[:], in0=LM[:, 1::2, 0:64], in1=LM[:, 1::2, 64:128], op=ALU.add),
              reads=[K("LM")], writes=[K("Mst")])
            step(pp)
            yield
            mm(P, B[:, 0:128], bdat, bdbt, True, True, [KB("BDbt"), KB("AR")], [BKEY])
            mm(P, B[:, 128:192], bdat, g.idst_bf[:], True, True, [KB("AR"), "idst_bf"], [BKEY])
            mm(P, B[:, 192:256], LM[:, 2, :], Vst[:, ci, :], True, True, [K("LM"), KB("Vst")], [BKEY])
            mm(P, B[:, 256:384], bdBh, idb[:], True, True, [KB("BDBh"), "ident_bf"], [BKEY])
            mm(P, B[:, 384:512], bdKh, idb[:], True, True, [KB("BDKh"), "ident_bf"], [BKEY])
            A(P, "dve", lambda e: e.tensor_tensor(out=P0[:], in0=B[:, 0:128], in1=g.maskT[d][:], op=ALU.mult),
              reads=[BKEY, "mskbd"], writes=[K("P0")])
            A(P, "act", lambda e: e.copy(out=X[0][:], in_=B[:, 128:256]), reads=[BKEY], writes=[K("X0")])
            A(P, "act", lambda e: e.copy(out=BKt[:], in_=B[:, 256:512].rearrange("p (m t) -> p m t", m=2)),
              reads=[BKEY], writes=[K("BKt")])
            step(pp)
            yield
            xi = 0
            for lev in range(6):
                Xc = X[xi]; Xn = X[1 - xi]; xck = K("X%d" % xi); xnk = K("X%d" % (1 - xi))
                mm(P, B[:, 0:128], QT[:], Xc[:], True, True, [K("QT"), xck], [BKEY])
                if lev < 5:
                    if lev == 0:
                        Pm = P0[:]; PTm = LM[:, 0, :]; pk = [K("P0"), K("LM")]
                    else:
                        Pm = PP[:, 0, :]; PTm = PP[:, 1, :]; pk = [K("PP")]
                    mm(P, B[:, 128:256], PTm, Pm, True, True, pk, [BKEY])
                    mm(P, B[:, 256:384], Pm, PTm, True, True, pk, [BKEY])
                A(P, "act", lambda e, Xn=Xn: e.copy(out=Xn[:], in_=B[:, 0:128]), reads=[BKEY], writes=[xnk])
                if lev < 5:
                    A(P, "act", lambda e: e.copy(out=PP[:], in_=B[:, 128:384].rearrange("p (m t) -> p m t", m=2)),
                      reads=[BKEY], writes=[K("PP")])
                    A(P, "dve", lambda e: e.tensor_tensor(out=QT[:], in0=B[:, 256:384], in1=I128f[:], op=ALU.add),
                      reads=[BKEY, "tmp_f"], writes=[K("QT")])
                else:
                    for kind in range(2):
                        for h2 in range(2):
                            A(P, "dve", lambda e, kind=kind, h2=h2: e.tensor_scalar(
                                out=AU[:, kind, h2, :], in0=B[:, kind * 64:(kind + 1) * 64], scalar1=g.hm[:, h2:h2 + 1],
                                scalar2=None, op0=ALU.mult), reads=[BKEY, "hm"], writes=[K("AU")])
                xi = 1 - xi
                step(pp)
                yield
            Xf = X[xi]; xfk = K("X%d" % xi)
            bdA = f2(AU[:, 0]); bdU = f2(AU[:, 1])
            mm(P, B[:, 0:128], bdA, BKt[:, 0, :], True, True, [K("AU"), K("BKt")], [BKEY])
            mm(P, B[:, 128:192], BKt[:, 0, :], Xf[:, 64:128], True, False, [K("BKt"), xfk], [BKEY])
            mm(P, B[:, 128:192], BKt[:, 1, :], Vst[:, ci, :], False, True, [K("BKt"), KB("Vst")], [BKEY])
            mm(P, B[:, 192:256], idb[:], rtp[:, cs_], True, False, ["ident_bf", KB("rtp")], [BKEY])
            mm(P, B[:, 192:256], bdA, Mst[:, 0, :], False, True, [K("AU"), K("Mst")], [BKEY])
            A(P, "dve", lambda e, ci=ci, Wt=Wt: e.scalar_tensor_tensor(out=BDGT[:], in0=I128f[:], scalar=Wt[:, ci:ci + 1], in1=B[:, 0:128],
                                                               op0=ALU.mult, op1=ALU.add),
              reads=[BKEY, "tmp_f", KB("Wt")], writes=[K("BDGT")])
            for h2 in range(2):
                A(P, "dve", lambda e, h2=h2: e.tensor_scalar(out=BDD[:, h2, :], in0=B[:, 128:192], scalar1=g.hm[:, h2:h2 + 1],
                                                            scalar2=None, op0=ALU.mult), reads=[BKEY, "hm"], writes=[K("BDD")])
            A(P, "dve", lambda e: e.tensor_copy(out=RhT[:], in_=B[:, 192:256]), reads=[BKEY], writes=[K("RhT")])
            step(pp)
            yield
            Hc = BDH[hcur]; Hn = BDH[1 - hcur]; hck = K("BDH%d" % hcur); hnk = K("BDH%d" % (1 - hcur))
            mm(P, B[:, 384:448], Hc[:], RhT[:], True, False, [hck, K("RhT")], [BKEY])
            mm(P, B[:, 384:448], bdU, Mst[:, 0, :], False, False, [K("AU"), K("Mst")], [BKEY])
            mm(P, B[:, 384:448], f2(BDV[:, ci]), Mst[:, 1, :], False, True, [KB("BDV"), K("Mst")], [BKEY])
            mm(P, B[:, 256:384], BDGT[:], Hc[:], True, False, [K("BDGT"), hck], [BKEY])
            mm(P, B[:, 256:384], idb[:], f2(BDD[:]), False, True, ["ident_bf", K("BDD")], [BKEY])
            A(P, "dve", lambda e, tok=tok: e.tensor_tensor(out=yacc[:, hp, tok], in0=B[:, 384:448], in1=yacc[:, hp, tok], op=ALU.add),
              reads=[BKEY, ("yacc", hp)], writes=[("yacc", hp)])
            A(P, "dve", lambda e, Hn=Hn: e.tensor_copy(out=Hn[:], in_=B[:, 256:384]), reads=[BKEY], writes=[hnk])
            hcur = 1 - hcur
            step(pp)
            yield
        drain(pp)


_NC_CACHE = {}


def kernel(**inputs):
    from concourse.bass_utils import run_bass_kernel_spmd
    x = np.ascontiguousarray(np.asarray(inputs["x"], dtype=np.float32))
    Bn, T, _ = x.shape
    if T not in _NC_CACHE:
        _NC_CACHE[T] = build(T, nlayers=2)
    nc = _NC_CACHE[T]
    in_maps = []
    for b in range(Bn):
        m = {"x": np.ascontiguousarray(x[b])}
        for n, s in PARAMS:
            m[n] = np.ascontiguousarray(np.asarray(inputs[n], dtype=np.float32))
        in_maps.append(m)
    res = run_bass_kernel_spmd(nc, in_maps, core_ids=list(range(Bn)))
    return np.stack([np.asarray(r["y"], dtype=np.float32) for r in res.results], axis=0)
```
